# Optimizing a Trainium2 kernel written in Bass

```python
import math
import jax
import jax.numpy as jnp
from jax import lax
import numpy as np

D_MODEL = 2048
BATCH = 8
SEQ = 4096
DEPTH = 2

GRID_W = 64
CTX_LEN = 256
N_MIXERS = 2
N_ATTN_LAYERS = (DEPTH + N_MIXERS - 1) // N_MIXERS
N_LRU_LAYERS = DEPTH // N_MIXERS
N_SUBLAYERS = 3
DA_HEAD_DIM = 64
DA_HEADS = D_MODEL // (2 * DA_HEAD_DIM)
DA_V_DIM = 2 * DA_HEAD_DIM
Q_BLOCK = 128
ROPE_BASE = 10000.0
AXIS_ROT_DIM = DA_HEAD_DIM // 2
D_RNN = 2560
LRU_BLOCKS = 10
LRU_BLOCK_W = D_RNN // LRU_BLOCKS
CONV_W = 4
CONV_PAD_LEFT = 2
CONV_PAD_RIGHT = CONV_W - 1 - CONV_PAD_LEFT
LRU_C = 8.0
A_MIN = 0.9
A_MAX = 0.999
D_FF = 5632
FFN_RES_WEIGHT = 0.5
ALPHA = (2.0 * DEPTH) ** 0.25
BETA = (8.0 * DEPTH) ** -0.25
LN_EPS = 1e-6

kernel_name = "hybrid_diffattn_rglru_macaron_dit"


def layer_norm(x, g, b):
    xf = x.astype(jnp.float32)
    mu = jnp.mean(xf, axis=-1, keepdims=True)
    var = jnp.mean(jnp.square(xf - mu), axis=-1, keepdims=True)
    y = (xf - mu) * lax.rsqrt(var + LN_EPS)
    return (y * g.astype(jnp.float32) + b.astype(jnp.float32)).astype(x.dtype)


def rms_norm(x, g):
    xf = x.astype(jnp.float32)
    y = xf * lax.rsqrt(jnp.mean(jnp.square(xf), axis=-1, keepdims=True) + LN_EPS)
    return (y * g.astype(jnp.float32)).astype(x.dtype)


def modulate(x, shift, scale):
    return x * (1 + scale) + shift


def split_mod(mod):
    return mod[..., 0, :, :], mod[..., 1, :, :], mod[..., 2, :, :]


def swiglu(u, w_gate, w_up, w_down):
    return (jax.nn.silu(u @ w_gate) * (u @ w_up)) @ w_down


def ffn_sublayer(x, mod, w_gate, w_up, w_down, g, b):
    shift, scale, gate = split_mod(mod)
    h = swiglu(modulate(x, shift, scale), w_gate, w_up, w_down)
    return layer_norm(ALPHA * x + FFN_RES_WEIGHT * gate * h, g, b)


def axial_rope_tables(n_tokens):
    rows = n_tokens // GRID_W
    row = jnp.repeat(jnp.arange(rows, dtype=jnp.float32), GRID_W)
    col = jnp.tile(jnp.arange(GRID_W, dtype=jnp.float32), rows)
    inv_freq = ROPE_BASE ** (-jnp.arange(0, AXIS_ROT_DIM, 2, dtype=jnp.float32) / AXIS_ROT_DIM)
    ang_r = row[:, None] * inv_freq[None, :]
    ang_c = col[:, None] * inv_freq[None, :]
    return jnp.cos(ang_r), jnp.sin(ang_r), jnp.cos(ang_c), jnp.sin(ang_c)


def _rope_1d(x, cos, sin):
    cos = cos[:, None, None, :].astype(x.dtype)
    sin = sin[:, None, None, :].astype(x.dtype)
    x1, x2 = jnp.split(x, 2, axis=-1)
    return jnp.concatenate([x1 * cos - x2 * sin, x1 * sin + x2 * cos], axis=-1)


def apply_axial_rope(x, tables):
    cos_r, sin_r, cos_c, sin_c = tables
    x_row, x_col = jnp.split(x, 2, axis=-1)
    return jnp.concatenate([_rope_1d(x_row, cos_r, sin_r), _rope_1d(x_col, cos_c, sin_c)], axis=-1)


def diff_attn_core(q, k, v, lam):
    s = jnp.einsum('bqhed,bkhed->bheqk', q * (DA_HEAD_DIM ** -0.5), k,
                   preferred_element_type=jnp.float32)
    p = jax.nn.softmax(s, axis=-1)
    a = p[:, :, 0] - lam * p[:, :, 1]
    return jnp.einsum('bhqk,bkhv->bqhv', a.astype(v.dtype), v)


def diff_attention(u, uc, w_qkv, w_o, lq1, lk1, lq2, lk2, subln_g, lambda_init, need_ctx):
    B, S, _ = u.shape
    f32 = jnp.float32
    lam = (jnp.exp(jnp.sum(lq1.astype(f32) * lk1.astype(f32)))
           - jnp.exp(jnp.sum(lq2.astype(f32) * lk2.astype(f32))) + lambda_init)

    def proj(t):
        n = t.shape[1]
        q, k, v = jnp.split(t @ w_qkv, 3, axis=-1)
        return (q.reshape(B, n, DA_HEADS, 2, DA_HEAD_DIM),
                k.reshape(B, n, DA_HEADS, 2, DA_HEAD_DIM),
                v.reshape(B, n, DA_HEADS, DA_V_DIM))

    def finish(o):
        o = rms_norm(o, subln_g) * (1.0 - lambda_init)
        return o.reshape(o.shape[0], o.shape[1], D_MODEL) @ w_o

    tables = axial_rope_tables(S)
    q, k, v = proj(u)
    q = apply_axial_rope(q, tables)
    k = apply_axial_rope(k, tables)
    qc, kc, vc = proj(uc)
    k_all = jnp.concatenate([kc, k], axis=1)
    v_all = jnp.concatenate([vc, v], axis=1)
    n_blk = S // Q_BLOCK
    q_blocks = jnp.moveaxis(q.reshape(B, n_blk, Q_BLOCK, DA_HEADS, 2, DA_HEAD_DIM), 1, 0)
    o = lax.map(lambda qb: diff_attn_core(qb, k_all, v_all, lam), q_blocks)
    o = jnp.moveaxis(o, 0, 1).reshape(B, S, DA_HEADS, DA_V_DIM)
    out = finish(o)
    out_c = finish(diff_attn_core(qc, kc, vc, lam)) if need_ctx else None
    return out, out_c


def depthwise_conv_centred(x, w, b):
    n = x.shape[1]
    xp = jnp.pad(x, ((0, 0), (CONV_PAD_LEFT, CONV_PAD_RIGHT), (0, 0)))
    y = xp[:, 0:n] * w[0]
    for j in range(1, CONV_W):
        y = y + xp[:, j:j + n] * w[j]
    return y + b


def block_diag_linear(x, w, b):
    xb = x.reshape(x.shape[0], x.shape[1], LRU_BLOCKS, LRU_BLOCK_W)
    y = jnp.einsum('bnhi,hij->bnhj', xb, w)
    return y.reshape(x.shape) + b


def rglru_coeffs(xr, w_a, b_a, w_i, b_i, a_param):
    f32 = jnp.float32
    r = jax.nn.sigmoid(block_diag_linear(xr, w_a, b_a)).astype(f32)
    i = jax.nn.sigmoid(block_diag_linear(xr, w_i, b_i)).astype(f32)
    log_a = -LRU_C * r * jax.nn.softplus(-a_param.astype(f32))
    a = jnp.exp(log_a)
    mult = jnp.sqrt(-jnp.expm1(2.0 * log_a))
    return a, mult * i * xr.astype(f32)


def _combine(left, right):
    a1, b1 = left
    a2, b2 = right
    return a1 * a2, a2 * b1 + b2


def linear_scan(a, b, h0, reverse):
    a_cum, b_cum = lax.associative_scan(_combine, (a, b), reverse=reverse, axis=1)
    if h0 is None:
        return b_cum
    return b_cum + a_cum * h0[:, None, :]


def rglru_mixer(u, uc, w_in, conv_w, conv_b, w_a, b_a, w_i, b_i, a_param, w_out, need_ctx):
    def branches(t):
        g, xr = jnp.split(t @ w_in, 2, axis=-1)
        return jax.nn.gelu(g), depthwise_conv_centred(xr, conv_w, conv_b)

    g, xr = branches(u)
    gc, xrc = branches(uc)
    h_lat = []
    h_ctx = []
    for d, rev in enumerate((False, True)):
        ac, bc = rglru_coeffs(xrc, w_a[d], b_a[d], w_i[d], b_i[d], a_param[d])
        hc = linear_scan(ac, bc, None, rev)
        h0 = hc[:, 0] if rev else hc[:, -1]
        a, b = rglru_coeffs(xr, w_a[d], b_a[d], w_i[d], b_i[d], a_param[d])
        h_lat.append(linear_scan(a, b, h0, rev))
        h_ctx.append(hc)
    out = ((h_lat[0] + h_lat[1]).astype(u.dtype) * g) @ w_out
    out_c = (((h_ctx[0] + h_ctx[1]).astype(uc.dtype) * gc) @ w_out) if need_ctx else None
    return out, out_c


def _normal(key, shape, scale):
    return jax.random.normal(key, shape, jnp.float32) * scale


def setup_inputs(seed: int = 0) -> dict:
    key = jax.random.key(seed)
    ks = jax.random.split(key, 28)
    D = D_MODEL
    a0 = jax.random.uniform(ks[25], (N_LRU_LAYERS, 2, D_RNN), jnp.float32, A_MIN, A_MAX)
    return {
        "x": _normal(ks[0], (BATCH, SEQ, D), 1.0),
        "c": _normal(ks[1], (BATCH, D), 1.0),
        "ctx": _normal(ks[2], (BATCH, CTX_LEN, D), 1.0),
        "c_ctx": _normal(ks[3], (D,), 1.0),
        "w_ada": _normal(ks[4], (DEPTH, D, N_SUBLAYERS * 3 * D), 0.5 * D ** -0.5),
        "b_ada": _normal(ks[5], (DEPTH, N_SUBLAYERS * 3 * D), 0.02),
        "ln_g": 1.0 + _normal(ks[6], (DEPTH, N_SUBLAYERS, D), 0.02),
        "ln_b": _normal(ks[7], (DEPTH, N_SUBLAYERS, D), 0.02),
        "ffn_w_gate": _normal(ks[8], (DEPTH, 2, D, D_FF), D ** -0.5),
        "ffn_w_up": _normal(ks[9], (DEPTH, 2, D, D_FF), D ** -0.5),
        "ffn_w_down": _normal(ks[10], (DEPTH, 2, D_FF, D), BETA * D_FF ** -0.5),
        "attn_w_qkv": _normal(ks[11], (N_ATTN_LAYERS, D, 3 * D), D ** -0.5),
        "attn_w_o": _normal(ks[12], (N_ATTN_LAYERS, D, D), BETA * D ** -0.5),
        "attn_lambda_q1": _normal(ks[13], (N_ATTN_LAYERS, DA_HEAD_DIM), 0.1),
        "attn_lambda_k1": _normal(ks[14], (N_ATTN_LAYERS, DA_HEAD_DIM), 0.1),
        "attn_lambda_q2": _normal(ks[15], (N_ATTN_LAYERS, DA_HEAD_DIM), 0.1),
        "attn_lambda_k2": _normal(ks[16], (N_ATTN_LAYERS, DA_HEAD_DIM), 0.1),
        "attn_subln_g": 1.0 + _normal(ks[17], (N_ATTN_LAYERS, DA_V_DIM), 0.02),
        "lru_w_in": _normal(ks[18], (N_LRU_LAYERS, D, 2 * D_RNN), D ** -0.5),
        "lru_conv_w": _normal(ks[19], (N_LRU_LAYERS, CONV_W, D_RNN), CONV_W ** -0.5),
        "lru_conv_b": _normal(ks[20], (N_LRU_LAYERS, D_RNN), 0.02),
        "lru_w_a": _normal(ks[21], (N_LRU_LAYERS, 2, LRU_BLOCKS, LRU_BLOCK_W, LRU_BLOCK_W), LRU_BLOCK_W ** -0.5),
        "lru_b_a": _normal(ks[22], (N_LRU_LAYERS, 2, D_RNN), 0.02),
        "lru_w_i": _normal(ks[23], (N_LRU_LAYERS, 2, LRU_BLOCKS, LRU_BLOCK_W, LRU_BLOCK_W), LRU_BLOCK_W ** -0.5),
        "lru_b_i": _normal(ks[24], (N_LRU_LAYERS, 2, D_RNN), 0.02),
        "lru_a_param": jnp.log(a0) - jnp.log1p(-a0),
        "lru_w_out": _normal(ks[26], (N_LRU_LAYERS, D_RNN, D), BETA * D_RNN ** -0.5),
    }


def reference(x, c, ctx, c_ctx, w_ada, b_ada, ln_g, ln_b, ffn_w_gate, ffn_w_up, ffn_w_down,
              attn_w_qkv, attn_w_o, attn_lambda_q1, attn_lambda_k1, attn_lambda_q2, attn_lambda_k2,
              attn_subln_g, lru_w_in, lru_conv_w, lru_conv_b, lru_w_a, lru_b_a, lru_w_i, lru_b_i,
              lru_a_param, lru_w_out):
    B, S, D = x.shape
    xc = ctx
    for i in range(DEPTH):
        need_ctx = i < DEPTH - 1
        m = (jax.nn.silu(c) @ w_ada[i] + b_ada[i]).reshape(B, N_SUBLAYERS, 3, 1, D)
        mc = (jax.nn.silu(c_ctx) @ w_ada[i] + b_ada[i]).reshape(N_SUBLAYERS, 3, 1, D)

        x = ffn_sublayer(x, m[:, 0], ffn_w_gate[i, 0], ffn_w_up[i, 0], ffn_w_down[i, 0], ln_g[i, 0], ln_b[i, 0])
        xc = ffn_sublayer(xc, mc[0], ffn_w_gate[i, 0], ffn_w_up[i, 0], ffn_w_down[i, 0], ln_g[i, 0], ln_b[i, 0])

        sh, sc, gt = split_mod(m[:, 1])
        shc, scc, gtc = split_mod(mc[1])
        u = modulate(x, sh, sc)
        uc = modulate(xc, shc, scc)
        j = i // N_MIXERS
        if i % N_MIXERS == 0:
            lambda_init = 0.8 - 0.6 * math.exp(-0.3 * i)
            o, oc = diff_attention(u, uc, attn_w_qkv[j], attn_w_o[j], attn_lambda_q1[j], attn_lambda_k1[j],
                                   attn_lambda_q2[j], attn_lambda_k2[j], attn_subln_g[j], lambda_init, need_ctx)
        else:
            o, oc = rglru_mixer(u, uc, lru_w_in[j], lru_conv_w[j], lru_conv_b[j], lru_w_a[j], lru_b_a[j],
                                lru_w_i[j], lru_b_i[j], lru_a_param[j], lru_w_out[j], need_ctx)
        x = layer_norm(ALPHA * x + gt * o, ln_g[i, 1], ln_b[i, 1])
        if need_ctx:
            xc = layer_norm(ALPHA * xc + gtc * oc, ln_g[i, 1], ln_b[i, 1])

        x = ffn_sublayer(x, m[:, 2], ffn_w_gate[i, 1], ffn_w_up[i, 1], ffn_w_down[i, 1], ln_g[i, 2], ln_b[i, 2])
        if need_ctx:
            xc = ffn_sublayer(xc, mc[2], ffn_w_gate[i, 1], ffn_w_up[i, 1], ffn_w_down[i, 1], ln_g[i, 2], ln_b[i, 2])
    return x
```

```python
import math
from contextlib import ExitStack
import numpy as np
import concourse.bass as bass
import concourse.mybir as mybir
from concourse.bass_utils import run_bass_kernel_spmd

F32 = mybir.dt.float32
BF16 = mybir.dt.bfloat16
AF = mybir.ActivationFunctionType
ALU = mybir.AluOpType
AX = mybir.AxisListType

D = 2048
DC = 16
DFF = 5632
FC = 44
CTX = 256
DRNN = 2560
RC = 20
NH = 16
ALPHA = 2.0 ** 0.5
LN_EPS = 1e-6
LAMBDA_INIT = 0.8 - 0.6 * math.exp(-0.3 * 0)
WSLOT = 5632
NWBUF = 4
PIPE_QK = False
ENGS = ("tensor", "vector", "scalar", "gpsimd", "sync")


def _C(name, *a, **k):
    return (name, a, k)


class DSem:
    def __init__(self, h):
        self.h = h
        self.n = 0

    def tok(self):
        return (self.h, self.n)


class Op:
    __slots__ = ("fn", "ws", "inc", "dma")

    def __init__(self, fn, ws, inc, dma):
        self.fn, self.ws, self.inc, self.dma = fn, ws, inc, dma


class Prog:
    def __init__(self, nc):
        self.nc = nc
        self.ops = {e: [] for e in ENGS}
        self.esem = {e: nc.alloc_semaphore("es_" + e) for e in ENGS if e != "sync"}
        self.ecnt = {e: 0 for e in ENGS}
        self.waited = {e: {} for e in ENGS}
        self.dsems = []
        self.ninst = 0

    def new_sem(self, name=None):
        s = DSem(self.nc.alloc_semaphore(name or f"ds{len(self.dsems)}"))
        self.dsems.append(s)
        return s

    def _flat(self, toks, out):
        for t in toks:
            if t is None:
                continue
            if isinstance(t, list) or (isinstance(t, tuple) and (len(t) != 2 or isinstance(t[0], (tuple, list)) or t[0] is None)):
                self._flat(t, out)
            else:
                out.append(t)

    def _waits(self, eng, toks):
        fl = []
        self._flat(toks, fl)
        ws = []
        for sem, v in fl:
            if v <= 0:
                continue
            key = id(sem)
            if self.waited[eng].get(key, 0) >= v:
                continue
            self.waited[eng][key] = v
            ws.append((sem, v))
        return ws

    def op(self, eng, fn, waits=(), sig=False):
        ws = self._waits(eng, waits)
        tok = None
        inc = None
        if sig:
            self.ecnt[eng] += 1
            inc = (self.esem[eng], 1)
            tok = (self.esem[eng], self.ecnt[eng])
        self.ops[eng].append(Op(fn, ws, inc, False))
        self.ninst += 1
        return tok

    def pe(self, fn, waits=(), sig=False):
        return self.op("tensor", fn, waits, sig)

    def dve(self, fn, waits=(), sig=False):
        return self.op("vector", fn, waits, sig)

    def act(self, fn, waits=(), sig=False):
        return self.op("scalar", fn, waits, sig)

    def pool(self, fn, waits=(), sig=False):
        return self.op("gpsimd", fn, waits, sig)

    def dma(self, eng, out, in_, dsem, waits=()):
        ws = self._waits(eng, waits)
        dsem.n += 16
        self.ops[eng].append(Op(lambda e, o=out, i=in_: e.dma_start(out=o, in_=i), ws, (dsem.h, 16), True))
        self.ninst += 1
        return (dsem.h, dsem.n)

    def wait_only(self, eng, waits):
        ws = self._waits(eng, waits)
        if ws:
            self.ops[eng].append(Op(None, ws, None, False))

    def barrier(self):
        toks = []
        for e in ("tensor", "vector", "scalar", "gpsimd"):
            last = None
            for o in reversed(self.ops[e]):
                if o.fn is not None and not o.dma:
                    last = o
                    break
            if last is not None and last.inc is None:
                self.ecnt[e] += 1
                last.inc = (self.esem[e], 1)
            if self.ecnt[e] > 0:
                toks.append((self.esem[e], self.ecnt[e]))
        for s in self.dsems:
            if s.n > 0:
                toks.append((s.h, s.n))
        for e in ENGS:
            self.wait_only(e, toks)

    def emit(self):
        nc = self.nc
        ops = self.ops
        with nc.Block() as block:
            def body(ename):
                def f(e):
                    for o in ops[ename]:
                        for sem, v in o.ws:
                            e.wait_ge(sem, v)
                        if o.fn is None:
                            continue
                        ins = o.fn(e) if callable(o.fn) else getattr(e, o.fn[0])(*o.fn[1], **o.fn[2])
                        if o.inc is not None:
                            ins.then_inc(o.inc[0], o.inc[1])
                return f
            block.sync(body("sync"))
            block.tensor(body("tensor"))
            block.vector(body("vector"))
            block.scalar(body("scalar"))
            block.gpsimd(body("gpsimd"))
        self.ops = {e: [] for e in ENGS}


class Bank:
    def __init__(self, t):
        self.t = t
        self.free = None


class WStream:
    def __init__(self, P, nc):
        self.P = P
        self.slots = [nc.alloc_sbuf_tensor(f"wslot{i}", [128, WSLOT], BF16) for i in range(NWBUF)]
        self.sems = [P.new_sem(f"wsem{i}") for i in range(NWBUF)]
        self.free = [None] * NWBUF
        self.queue = []
        self.inflight = []
        self.gi = 0
        self.nrel = 0
        self.ready = None

    def push(self, srcs):
        self.queue.extend(srcs)

    def _issue(self):
        while self.queue and self.gi - self.nrel < NWBUF:
            src = self.queue.pop(0)
            s = self.gi % NWBUF
            self.gi += 1
            E = src.shape[1]
            tok = self.P.dma("sync", self.slots[s][:, 0:E], src, self.sems[s], waits=[self.free[s], self.ready])
            self.inflight.append((s, tok))

    def next(self):
        self._issue()
        s, tok = self.inflight.pop(0)
        return s, self.slots[s], tok

    def release(self, s, tok):
        self.free[s] = tok
        self.nrel += 1
        self._issue()


class Ctx:
    pass


_UID = [0]


def mk_sb(nc, es):
    _UID[0] += 1
    u = _UID[0]
    return lambda n, s, d: es.enter_context(nc.sbuf_tensor(f"{n}_u{u}", s, d))


def build(S_LAT=4096, dbg=False, phases=None):
    NT = CTX + S_LAT
    NKC = NT // 128
    nc = bass.Bass("TRN2", target_bir_lowering=False)
    P = Prog(nc)
    G = Ctx()
    G.nc, G.P, G.SLAT, G.NT, G.NKC = nc, P, S_LAT, NT, NKC
    tiles = [(0, CTX, 1)] + [(CTX + 512 * i, 512, 0) for i in range(S_LAT // 512)]
    lat_tiles = tiles[1:]
    G.tiles, G.lat_tiles = tiles, lat_tiles

    def din(name, shape, dt=F32):
        return nc.dram_tensor(name, list(shape), dt, kind="ExternalInput").ap()

    skind = "ExternalOutput" if dbg else "Internal"

    def dscr(name, shape, dt):
        return nc.dram_tensor(name, list(shape), dt, kind=skind).ap()

    I = Ctx()
    I.x0 = din("x0", [NT, D])
    I.cfm = din("cfm", [128, 2, 16])
    I.wada = din("wada", [2, 36, 128, 16 * 512])
    I.bada = din("bada", [128, 2, 144])
    I.lng = din("lng", [6, D])
    I.lnb = din("lnb", [6, D])
    I.wgu = din("wgu", [4, FC, 128, 4096])
    I.wdn = din("wdn", [4, DC, 128, FC * 128])
    I.wqk = din("wqk", [32, 128, 4096])
    I.wv = din("wv", [8, 128, 4096])
    I.wo = din("wo", [16, 128, 2048])
    I.lam = din("lam", [4, 64])
    I.subg = din("subg", [128, 1])
    I.cos = din("cos", [128, NT])
    I.sin = din("sin", [128, NT])
    I.win = din("win", [40, 128, 2048])
    I.convw = din("convw", [128, RC, 4])
    I.lvec = din("lvec", [128, 6, RC])
    I.apar = din("apar", [128, 2, RC])
    I.wgate = din("wgate", [10, 128, 2048])
    I.wout = din("wout", [16, 128, RC * 128])
    I.eye = din("eye", [128, 128])
    I.permm = din("permm", [128, 128])
    out = nc.dram_tensor("out", [S_LAT, D], F32, kind="ExternalOutput").ap()

    S = Ctx()
    S.XA = dscr("XA", [NT, D], F32)
    S.XB = dscr("XB", [NT, D], F32)
    S.wgu = dscr("s_wgu", [4, FC, 128, 4096], BF16)
    S.wdn = dscr("s_wdn", [4, DC, 128, FC * 128], BF16)
    S.wqk = dscr("s_wqk", [32, 128, 4096], BF16)
    S.wv = dscr("s_wv", [8, 128, 4096], BF16)
    S.wo = dscr("s_wo", [16, 128, 2048], BF16)
    S.win = dscr("s_win", [40, 128, 2048], BF16)
    S.wgate = dscr("s_wgate", [10, 128, 2048], BF16)
    S.wout = dscr("s_wout", [16, 128, RC * 128], BF16)
    S.QT = dscr("QT", [NH, 128, NT], BF16)
    S.KT = dscr("KT", [NH, 128, NT], BF16)
    S.V = dscr("V", [NT, D], BF16)
    S.AOT = dscr("AOT", [NH, 128, NT], BF16)
    S.GT = dscr("GT", [RC, 128, NT], BF16)
    S.XRT = dscr("XRT", [RC, 128, NT], F32)
    S.HGT = dscr("HGT", [RC, 128, NT], BF16)
    G.I, G.S, G.out = I, S, out

    G.ws = WStream(P, nc)
    G.identF = nc.alloc_sbuf_tensor("identF", [128, 128], F32)
    G.identB = nc.alloc_sbuf_tensor("identB", [128, 128], BF16)
    G.MOD = nc.alloc_sbuf_tensor("MOD", [128, 288, 2], F32)
    G.PS = [Bank(nc.alloc_psum_tensor(f"bank{i}", [128, 512], F32)) for i in range(8)]
    G.sems = [P.new_sem(f"gs{i}") for i in range(24)]

    phases = phases or ["ada", "conv", "l0f0", "attn", "l0f2", "l1f0", "lru", "l1f2"]

    t = P.dma("sync", G.identF[:], I.eye, G.sems[0])
    P.dve(_C("tensor_copy", out=G.identB[:], in_=G.identF[:]), waits=[t])
    P.barrier()
    P.emit()

    if "ada" in phases:
        phase_ada(G, conv=("conv" in phases))
    X0, XA, XB = I.x0, S.XA, S.XB
    if "l0f0" in phases:
        phase_ffn(G, 0, 0, X0, XA, tiles)
    if "attn" in phases:
        phase_attn(G, XA, XB)
    if "l0f2" in phases:
        phase_ffn(G, 0, 2, XB, XA, tiles)
    if "l1f0" in phases:
        phase_ffn(G, 1, 0, XA, XB, tiles)
    if "lru" in phases:
        phase_lru(G, XB, XA)
    if "l1f2" in phases:
        phase_ffn(G, 1, 2, XA, out, lat_tiles, out_off=CTX)
    P.barrier()
    P.emit()
    return nc


def modcol(i, sl, comp, c):
    return i * 144 + (sl * 3 + comp) * 16 + c


def phase_ada(G, conv=True):
    nc, P, I = G.nc, G.P, G.I
    with ExitStack() as es:
        sb = mk_sb(nc, es)
        cv = Converter(G, es) if conv else None
        craw = sb("craw", [128, 2, 16], F32)
        cs = sb("cs", [128, 16, 2], F32)
        bada = sb("badas", [128, 2, 144], F32)
        slabs = [sb(f"adaslab{i}", [128, 16 * 512], F32) for i in range(2)]
        ssem = [G.sems[0], G.sems[1]]
        t0 = P.dma("sync", craw[:], I.cfm, G.sems[2])
        t1 = P.dma("sync", bada[:], I.bada, G.sems[3])
        ta = None
        for w in range(2):
            ta = P.act(_C("activation", out=cs[:, :, w], in_=craw[:, w, :], func=AF.Silu), waits=[t0], sig=True)
        sfree = [None, None]
        banks = [G.PS[0], G.PS[1]]
        n = 0
        for i in range(2):
            for cb in range(36):
                s = n % 2
                lt = P.dma("sync", slabs[s][:], I.wada[i, cb], ssem[s], waits=[sfree[s]])
                for q in range(4):
                    bank = banks[(n * 4 + q) % 2]
                    tm = None
                    for k in range(16):
                        tm = P.pe(_C("matmul",
                            bank.t[:, 0:2], lhsT=slabs[s][:, k * 512 + q * 128: k * 512 + (q + 1) * 128],
                            rhs=cs[:, k, :], start=(k == 0), stop=(k == 15)),
                            waits=[lt, ta, bank.free] if k == 0 else (), sig=(k == 15))
                    col = i * 144 + cb * 4 + q
                    tv = P.dve(_C("tensor_scalar",
                        out=G.MOD[:, col, :], in0=bank.t[:, 0:2], scalar1=bada[:, i, cb * 4 + q:cb * 4 + q + 1], scalar2=None,
                        op0=ALU.add), waits=[tm, t1], sig=True)
                    bank.free = tv
                sfree[s] = tm
                n += 1
                if cv is not None:
                    cv.step(6)
        if cv is not None:
            cv.step(10 ** 6)
        for i in range(2):
            for sl in range(3):
                c0 = modcol(i, sl, 1, 0)
                P.dve(_C("tensor_scalar", out=G.MOD[:, c0:c0 + 16, :], in0=G.MOD[:, c0:c0 + 16, :],
                                                       scalar1=1.0, scalar2=None, op0=ALU.add), waits=[tv])
                if sl != 1:
                    g0 = modcol(i, sl, 2, 0)
                    P.dve(_C("tensor_scalar", out=G.MOD[:, g0:g0 + 16, :], in0=G.MOD[:, g0:g0 + 16, :],
                                                           scalar1=0.5, scalar2=None, op0=ALU.mult))
        P.barrier()
        P.emit()


class Converter:
    def __init__(self, G, es):
        self.G = G
        nc, P, I, S = G.nc, G.P, G.I, G.S
        jobs = []

        def add(src, dst):
            sh = src.shape
            lead = sh[:-2]
            E = sh[-1]
            idxs = [()]
            for n in lead:
                idxs = [ix + (j,) for ix in idxs for j in range(n)]
            for ix in idxs:
                s_ = src
                d_ = dst
                for j in ix:
                    s_ = s_[j]
                    d_ = d_[j]
                for e0 in range(0, E, 4096):
                    n_ = min(4096, E - e0)
                    jobs.append((s_[:, e0:e0 + n_], d_[:, e0:e0 + n_], n_))

        for a, b in ((I.wgu, S.wgu), (I.wdn, S.wdn), (I.wqk, S.wqk), (I.wv, S.wv), (I.wo, S.wo),
                     (I.win, S.win), (I.wgate, S.wgate), (I.wout, S.wout)):
            add(a, b)
        self.jobs = jobs
        self.j = 0
        NB = self.NB = 3
        sb = mk_sb(nc, es)
        self.fin = [sb(f"cvin{i}", [128, 4096], F32) for i in range(NB)]
        self.fout = [sb(f"cvout{i}", [128, 4096], BF16) for i in range(NB)]
        self.lsem = [G.sems[8 + i] for i in range(NB)]
        self.ssem = [G.sems[8 + NB + i] for i in range(NB)]
        self.infree = [None] * NB
        self.outfree = [None] * NB

    def step(self, n):
        P = self.G.P
        for _ in range(n):
            if self.j >= len(self.jobs):
                return
            j = self.j
            self.j += 1
            src, dst, n_ = self.jobs[j]
            s = j % self.NB
            fin, fout = self.fin[s], self.fout[s]
            lt = P.dma("sync", fin[:, 0:n_], src, self.lsem[s], waits=[self.infree[s]])
            if j % 2 == 0:
                ct = P.dve(_C("tensor_copy", out=fout[:, 0:n_], in_=fin[:, 0:n_]),
                           waits=[lt, self.outfree[s]], sig=True)
            else:
                ct = P.act(_C("activation", out=fout[:, 0:n_], in_=fin[:, 0:n_], func=AF.Copy),
                           waits=[lt, self.outfree[s]], sig=True)
            self.infree[s] = ct
            self.outfree[s] = P.dma("gpsimd", dst, fout[:, 0:n_], self.ssem[s], waits=[ct])


def prologue(G, B, X_in, t0, T, i, sl, w):
    P = G.P
    S4 = T // 128
    tokx = P.dma("gpsimd", B.xin[:, 0:S4, :], X_in[t0:t0 + T, :].rearrange("(s p) d -> p s d", p=128),
                 B.xin_sem, waits=[B.xin_free])
    ev = None
    tp = None
    for c in range(DC):
        bank = B.tpb[c % 2]
        for s in range(S4):
            tp = P.pe(_C("transpose",
                out=bank.t[:, s * 128:(s + 1) * 128], in_=B.xin[:, s, c * 128:(c + 1) * 128], identity=G.identF[:]),
                waits=[tokx, bank.free] if s == 0 else (), sig=(s == S4 - 1))
        sc = modcol(i, sl, 1, c)
        sh = modcol(i, sl, 0, c)
        ev = P.dve(_C("tensor_scalar",
            out=B.uT[:, c, 0:T], in0=bank.t[:, 0:T], scalar1=G.MOD[:, sc, w:w + 1], scalar2=G.MOD[:, sh, w:w + 1],
            op0=ALU.mult, op1=ALU.add), waits=[tp, B.uT_free] if c == 0 else [tp], sig=True)
        bank.free = ev
    B.xin_free = tp
    B.uT_ready = ev


def layer_norm_rows(G, B, xs, s, lnw, after):
    P = G.P
    tk = None
    for q in range(4):
        tk = P.dve(_C("bn_stats", out=B.stats[:, s, q, :], in_=xs[:, q * 512:(q + 1) * 512]),
                   waits=[after] if q == 0 else (), sig=(q == 3))
    t1 = P.dve(_C("bn_aggr", out=B.mv[:, s, :], in_=B.stats[:, s, :, :].rearrange("p a b -> p (a b)")),
               waits=[tk], sig=True)
    t2 = P.dve(_C("tensor_scalar", out=B.sm[:, s, 0:1], in0=B.mv[:, s, 1:2], scalar1=LN_EPS, scalar2=None,
                                         op0=ALU.add), waits=[t1], sig=True)
    t3 = P.act(_C("activation", out=B.sm[:, s, 1:2], in_=B.sm[:, s, 0:1], func=AF.Sqrt), waits=[t2], sig=True)
    t4 = P.dve(_C("reciprocal", out=B.sm[:, s, 2:3], in_=B.sm[:, s, 1:2]), waits=[t3], sig=True)
    t5 = P.dve(_C("scalar_tensor_tensor", out=B.sm[:, s, 3:4], in0=B.mv[:, s, 0:1], scalar=-1.0,
                                                in1=B.sm[:, s, 2:3], op0=ALU.mult, op1=ALU.mult), waits=[t4], sig=True)
    t6 = P.act(_C("activation", out=xs, in_=xs, func=AF.Identity, scale=B.sm[:, s, 2:3], bias=B.sm[:, s, 3:4]),
               waits=[t5, after], sig=True)
    P.pool(_C("tensor_tensor", out=xs, in0=xs, in1=B.gbc[:], op=ALU.mult), waits=[t6, lnw])
    t8 = P.pool(_C("tensor_tensor", out=xs, in0=xs, in1=B.bbc[:], op=ALU.add), sig=True)
    return t8


class OutProj:
    def __init__(self, G, B, KC, X_in, X_out, t0, T, gcol0, w, out_row0):
        self.G, self.B, self.KC, self.X_in, self.X_out = G, B, KC, X_in, X_out
        self.t0, self.T, self.gcol0, self.w, self.out_row0 = t0, T, gcol0, w, out_row0
        self.pending = None
        self.last_stt = None

    def begin(self):
        G, B, P = self.G, self.B, self.G.P
        S4 = self.T // 128
        self.xl = P.dma("gpsimd", B.x_all[:, 0:S4, :],
                        self.X_in[self.t0:self.t0 + self.T, :].rearrange("(s p) d -> p s d", p=128),
                        B.xall_sem, waits=[B.xall_free])

    def slab(self, m, buf, ltok, actT, act_ready):
        G, B, P, T, KC = self.G, self.B, self.G.P, self.T, self.KC
        bank = B.dnb[m % 2]
        tm = None
        for k in range(KC):
            tm = P.pe(_C("matmul", bank.t[:, 0:T], lhsT=buf[:, k * 128:(k + 1) * 128],
                                                         rhs=actT[:, k, 0:T], start=(k == 0), stop=(k == KC - 1)),
                      waits=[ltok, act_ready, bank.free] if k == 0 else (), sig=(k == KC - 1))
        gyb = B.gy[m % 3]
        gc = self.gcol0 + m
        ta = P.act(_C("activation",
            out=gyb[:, 0:T], in_=bank.t[:, 0:T], func=AF.Copy, scale=G.MOD[:, gc, self.w:self.w + 1]),
            waits=[tm, B.gy_free[m % 3]], sig=True)
        bank.free = ta
        if self.pending is not None:
            self.transposes(*self.pending)
        self.pending = (m, gyb, ta)
        return tm

    def transposes(self, m, gyb, ta):
        G, B, P, T = self.G, self.B, self.G.P, self.T
        S4 = T // 128
        bank = B.tpb[m % 2]
        tp = None
        for s in range(S4):
            tp = P.pe(_C("transpose", out=bank.t[:, s * 128:(s + 1) * 128],
                                                            in_=gyb[:, s * 128:(s + 1) * 128], identity=G.identF[:]),
                      waits=[ta, bank.free] if s == 0 else (), sig=(s == S4 - 1))
        B.gy_free[m % 3] = tp
        xv = B.x_all[:, 0:S4, m * 128:(m + 1) * 128]
        tv = P.dve(_C("scalar_tensor_tensor",
            out=xv, in0=xv, scalar=ALPHA, in1=bank.t[:, 0:T].rearrange("p (s d) -> p s d", d=128),
            op0=ALU.mult, op1=ALU.add), waits=[tp, self.xl], sig=True)
        bank.free = tv
        self.last_stt = tv

    def end(self):
        G, B, P, T = self.G, self.B, self.G.P, self.T
        S4 = T // 128
        self.transposes(*self.pending)
        self.pending = None
        sts = []
        for s in range(S4):
            xs = B.x_all[:, s, :]
            t8 = layer_norm_rows(G, B, xs, s, B.ln_ready, self.last_stt)
            r0 = self.out_row0 + s * 128
            sts.append(P.dma("gpsimd", self.X_out[r0:r0 + 128, :], xs, B.st_sem, waits=[t8]))
        B.xall_free = sts


def alloc_common(G, es, with_h=True):
    nc = G.nc
    sb = mk_sb(nc, es)
    B = Ctx()
    B.xin = sb("xin", [128, 4, D], F32)
    B.uT = sb("uT", [128, DC, 512], BF16)
    B.x_all = sb("x_all", [128, 4, D], F32)
    B.gbc = sb("gbc", [128, D], F32)
    B.bbc = sb("bbc", [128, D], F32)
    B.gy = [sb(f"gy{i}", [128, 512], F32) for i in range(3)]
    B.stats = sb("stats", [128, 4, 4, 6], F32)
    B.mv = sb("mv", [128, 4, 2], F32)
    B.sm = sb("sm", [128, 4, 4], F32)
    B.gy_free = [None] * 3
    B.xin_sem, B.xall_sem, B.st_sem, B.ln_sem = G.sems[0], G.sems[1], G.sems[2], G.sems[3]
    B.xin_free = None
    B.xall_free = None
    B.uT_free = None
    B.uT_ready = None
    B.tpb = [G.PS[6], G.PS[7]]
    B.dnb = [G.PS[4], G.PS[5]]
    for b in G.PS:
        b.free = None
    return B


def load_ln(G, B, idx):
    P, I = G.P, G.I
    P.dma("gpsimd", B.gbc[:], I.lng[idx].partition_broadcast(128), B.ln_sem)
    B.ln_ready = P.dma("gpsimd", B.bbc[:], I.lnb[idx].partition_broadcast(128), B.ln_sem)


def phase_ffn(G, i, sl, X_in, X_out, tiles, out_off=0):
    nc, P, S = G.nc, G.P, G.S
    fi = i * 2 + (0 if sl == 0 else 1)
    ws = G.ws
    with ExitStack() as es:
        sb = mk_sb(nc, es)
        B = alloc_common(G, es)
        hT = sb("hT", [128, FC, 512], BF16)
        sg = [sb(f"sg{k}", [128, 512], F32) for k in range(2)]
        sg_free = [None, None]
        gub = [(G.PS[0], G.PS[1]), (G.PS[2], G.PS[3])]
        load_ln(G, B, i * 3 + sl)
        for (t0, T, w) in tiles:
            ws.push([S.wgu[fi, f] for f in range(FC)])
            ws.push([S.wdn[fi, m] for m in range(DC)])
        prev = None
        for ti, (t0, T, w) in enumerate(tiles):
            prologue(G, B, X_in, t0, T, i, sl, w)
            if prev is not None:
                prev.end()
                prev = None
            tlast = None
            for f in range(FC):
                s, buf, lt = ws.next()
                pp = f % 2
                toks = [None, None]
                for gu in range(2):
                    bank = gub[pp][gu]
                    for c in range(DC):
                        toks[gu] = P.pe(_C("matmul",
                            bank.t[:, 0:T], lhsT=buf[:, (gu * 16 + c) * 128:(gu * 16 + c + 1) * 128],
                            rhs=B.uT[:, c, 0:T], start=(c == 0), stop=(c == DC - 1)),
                            waits=[lt, B.uT_ready, bank.free] if c == 0 else (), sig=(c == DC - 1))
                ws.release(s, toks[1])
                gb, ub = gub[pp]
                ta = P.act(_C("activation", out=sg[pp][:, 0:T], in_=gb.t[:, 0:T], func=AF.Silu),
                           waits=[toks[0], sg_free[pp]], sig=True)
                gb.free = ta
                tv = P.dve(_C("tensor_tensor", out=hT[:, f, 0:T], in0=sg[pp][:, 0:T],
                                                                        in1=ub.t[:, 0:T], op=ALU.mult),
                           waits=[ta, toks[1]], sig=True)
                ub.free = tv
                sg_free[pp] = tv
                tlast = toks[1]
            B.uT_free = tlast
            h_ready = tv
            st = OutProj(G, B, FC, X_in, X_out, t0, T, modcol(i, sl, 2, 0), w, t0 - out_off)
            st.begin()
            for m in range(DC):
                s, buf, lt = ws.next()
                tm = st.slab(m, buf, lt, hT, h_ready)
                ws.release(s, tm)
            prev = st
        prev.end()
        P.barrier()
        P.emit()


def phase_attn(G, X_in, X_out):
    nc, P, I, S = G.nc, G.P, G.I, G.S
    NT, NKC, tiles = G.NT, G.NKC, G.tiles
    ws = G.ws
    i, sl = 0, 1
    with ExitStack() as es:
        sb = mk_sb(nc, es)
        B = Ctx()
        B.xin = sb("xin", [128, 4, D], F32)
        B.uT = sb("uT", [128, DC, 512], BF16)
        B.xin_sem = G.sems[0]
        B.xin_free = None
        B.uT_free = None
        B.tpb = [G.PS[6], G.PS[7]]
        for b in G.PS:
            b.free = None
        wv = sb("wv", [128, DC, D], BF16)
        cosb = [sb(f"cos{k}", [128, 512], F32) for k in range(2)]
        sinb = [sb(f"sin{k}", [128, 512], F32) for k in range(2)]
        cs_free = [None, None]
        t1 = [sb(f"rt1_{k}", [128, 512], F32) for k in range(2)]
        t2 = [sb(f"rt2_{k}", [128, 512], F32) for k in range(2)]
        qst = [sb(f"qst{k}", [128, 512], BF16) for k in range(4)]
        qsb = [sb(f"qsb{k}", [128, 512], BF16) for k in range(2)]
        qsb_free = [None, None]
        permF = sb("permF", [128, 128], F32)
        permT = sb("permT", [128, 128], BF16)
        tpl = P.dma("gpsimd", permF[:], I.permm, G.sems[9])
        perm_ready = P.dve(_C("tensor_copy", out=permT[:], in_=permF[:]), waits=[tpl], sig=True)
        vst = [sb(f"vst{k}", [128, D], BF16) for k in range(2)]
        twv = None
        for k in range(8):
            twv = P.dma("gpsimd", wv[:, 2 * k:2 * k + 2, :].rearrange("p a b -> p (a b)"), S.wv[k], G.sems[2])
        qsem = [G.sems[3 + k] for k in range(4)]
        vsem = [G.sems[7 + k] for k in range(2)]
        q_free = [None] * 4
        v_free = [None] * 2
        t_free = [None, None]
        qkb = [(G.PS[0], G.PS[1]), (G.PS[2], G.PS[3])]
        vb = [G.PS[4], G.PS[5]]
        for (t0, T, w) in tiles:
            ws.push([S.wqk[m] for m in range(32)])
        nq = 0
        nv = 0
        for ti, (t0, T, w) in enumerate(tiles):
            S4 = T // 128
            cos, sin = cosb[ti % 2], sinb[ti % 2]
            P.dma("gpsimd", cos[:, 0:T], I.cos[:, t0:t0 + T], G.sems[1], waits=[cs_free[ti % 2]])
            tc_ = ts_ = P.dma("gpsimd", sin[:, 0:T], I.sin[:, t0:t0 + T], G.sems[1])
            prologue(G, B, X_in, t0, T, i, sl, w)
            tlast = None
            for m in range(32):
                s, buf, lt = ws.next()
                pp = m % 2
                toks = [None, None]
                for v_ in range(2):
                    bank = qkb[pp][v_]
                    for c in range(DC):
                        toks[v_] = P.pe(_C("matmul",
                            bank.t[:, 0:T], lhsT=buf[:, (v_ * 16 + c) * 128:(v_ * 16 + c + 1) * 128],
                            rhs=B.uT[:, c, 0:T], start=(c == 0), stop=(c == DC - 1)),
                            waits=[lt, B.uT_ready, bank.free] if c == 0 else (), sig=(c == DC - 1))
                ws.release(s, toks[1])
                b0, b1 = qkb[pp]
                P.dve(_C("tensor_tensor", out=t1[pp][:, 0:T], in0=b0.t[:, 0:T],
                                                              in1=cos[:, 0:T], op=ALU.mult),
                      waits=[toks[0], tc_, t_free[pp]])
                td = P.dve(_C("tensor_tensor", out=t2[pp][:, 0:T], in0=b1.t[:, 0:T],
                                                                   in1=sin[:, 0:T], op=ALU.mult),
                           waits=[toks[1], ts_], sig=True)
                b0.free = td
                b1.free = td
                qs_ = nq % 4
                tp_ = P.pool(_C("tensor_tensor", out=qst[qs_][:, 0:T], in0=t1[pp][:, 0:T],
                                                                       in1=t2[pp][:, 0:T], op=ALU.add),
                             waits=[td, q_free[qs_]], sig=True)
                t_free[pp] = tp_
                dst = (S.QT if m % 2 == 0 else S.KT)[m // 2][:, t0:t0 + T]
                q_free[qs_] = P.dma("gpsimd", dst, qst[qs_][:, 0:T], qsem[qs_], waits=[tp_])
                nq += 1
                tlast = toks[1]
                cs_free[ti % 2] = td
                cs_free[ti % 2] = td
            for s in range(S4):
                vs_ = nv % 2
                tev = None
                for n_ in range(4):
                    bank = vb[n_ % 2]
                    tm = None
                    for c in range(DC):
                        tm = P.pe(_C("matmul",
                            bank.t[:, :], lhsT=B.uT[:, c, s * 128:(s + 1) * 128],
                            rhs=wv[:, c, n_ * 512:(n_ + 1) * 512], start=(c == 0), stop=(c == DC - 1)),
                            waits=[twv, B.uT_ready, bank.free] if c == 0 else (), sig=(c == DC - 1))
                    tlast = tm
                    tev = P.act(_C("activation",
                        out=vst[vs_][:, n_ * 512:(n_ + 1) * 512], in_=bank.t[:, :], func=AF.Copy),
                        waits=[tm, v_free[vs_]] if n_ == 0 else [tm], sig=True)
                    bank.free = tev
                r0 = t0 + s * 128
                v_free[vs_] = P.dma("gpsimd", S.V[r0:r0 + 128, :], vst[vs_][:], vsem[vs_], waits=[tev])
                nv += 1
            B.uT_free = tlast
        P.barrier()
        P.emit()

    import os
    if os.environ.get("STOP_AFTER_QKV"):
        return
    with ExitStack() as es:
        sb = mk_sb(nc, es)
        for b in G.PS:
            b.free = None
        KTs = [sb(f"KTs{k}", [128, NT], BF16) for k in range(2)]
        QTs = [sb(f"QTs{k}", [128, NT], BF16) for k in range(2)]
        Vs = [sb(f"Vs{k}", [128, NKC, 132], BF16) for k in range(2)]
        NPB = 6
        Pb = [sb(f"Pb{k}", [128, 2, 512], BF16) for k in range(NPB)]
        aost = [sb(f"aost{k}", [128, 512], BF16) for k in range(2)]
        lamp = sb("lamp", [128, 4, 64], F32)
        lsm = sb("lsm", [128, 8], F32)
        gsub = sb("gsub", [128, 1], F32)
        mhalf = sb("mhalf", [128, 1], F32)
        ep = sb("ep", [128, 4, 8], F32)
        accs = [sb(f"accs{k}", [128, 3 * 396], F32) for k in range(2)]
        accs_free = [None, None]
        nacc = 0
        tmpo = [sb(f"tmpo{k}", [128, 128], F32) for k in range(2)]
        obuf = [sb(f"obuf{k}", [128, 128], F32) for k in range(2)]
        sqb = [sb(f"sqb{k}", [128, 128], F32) for k in range(2)]
        onb = [sb(f"onb{k}", [128, 128], BF16) for k in range(2)]
        hsem = [G.sems[0], G.sems[1]]
        aosem = [G.sems[2], G.sems[3]]
        tl = P.dma("gpsimd", lamp[:].rearrange("p a b -> p (a b)"),
                   I.lam.rearrange("a b -> (a b)").partition_broadcast(128), G.sems[4])
        tg = P.dma("gpsimd", gsub[:], I.subg, G.sems[5])
        P.dve(_C("memset", mhalf[:], -0.5))
        tk = None
        for j in range(2):
            tk = P.dve(_C("tensor_tensor", out=lamp[:, 2 * j, :], in0=lamp[:, 2 * j, :],
                                                      in1=lamp[:, 2 * j + 1, :], op=ALU.mult), waits=[tl], sig=True)
            tk = P.dve(_C("tensor_reduce", out=lsm[:, j:j + 1], in_=lamp[:, 2 * j, :], axis=AX.X, op=ALU.add),
                       waits=[tk], sig=True)
        tk = P.act(_C("activation", out=lsm[:, 2:4], in_=lsm[:, 0:2], func=AF.Exp), waits=[tk], sig=True)
        tk = P.dve(_C("tensor_tensor", out=lsm[:, 4:5], in0=lsm[:, 3:4], in1=lsm[:, 2:3], op=ALU.subtract),
                   waits=[tk], sig=True)
        tk = P.dve(_C("tensor_scalar", out=lsm[:, 5:6], in0=lsm[:, 4:5], scalar1=-LAMBDA_INIT, scalar2=None,
                                             op0=ALU.add), waits=[tk], sig=True)
        nlam_ready = tk
        tgs = P.dve(_C("tensor_scalar", out=gsub[:], in0=gsub[:], scalar1=1.0 - LAMBDA_INIT, scalar2=None,
                                              op0=ALU.mult), waits=[tg], sig=True)
        for k in range(2):
            P.dve(_C("memset", Vs[k][:, :, 128:132], 1.0))
        P.barrier()
        sbank = [(G.PS[0], G.PS[1]), (G.PS[2], G.PS[3])]
        accb = [G.PS[4], G.PS[5], G.PS[6]]
        tpbank = G.PS[7]
        acc_loc = [(accb[a // 3], (a % 3) * 132) for a in range(8)]
        qblocks = [(0, CTX, 2)] + [(CTX + 512 * j, 512, NKC) for j in range(G.SLAT // 512)]
        h_free = [None, None]
        pb_free = [None] * NPB
        npb = 0
        ao_free = [None, None]
        nao = 0
        acc_free = None
        eo_free = [None, None]
        nep = 0

        def load_head(h):
            k = h % 2
            P.dma("gpsimd", KTs[k][:], S.KT[h], hsem[k], waits=[h_free[k]])
            P.dma("gpsimd", QTs[k][:], S.QT[h], hsem[k])
            return P.dma("gpsimd", Vs[k][:, :, 0:128],
                         S.V[:, h * 128:(h + 1) * 128].rearrange("(c p) v -> p c v", p=128), hsem[k])

        hl = load_head(0)
        for h in range(NH):
            k = h % 2
            hl_next = load_head(h + 1) if h + 1 < NH else None
            KT, QT, V = KTs[k], QTs[k], Vs[k]
            last_pe = None
            for (q0, QN, nkc) in qblocks:
                QS = QN // 128
                pend = None

                def pv(kc, pi, te):
                    nonlocal last_pe
                    tm = None
                    started = set()
                    for e_ in range(2):
                        for qs in range(QS):
                            bank, col = acc_loc[e_ * 4 + qs]
                            first = (kc == 0 and id(bank) not in started)
                            started.add(id(bank))
                            tm = P.pe(_C("matmul",
                                bank.t[:, col:col + 129], lhsT=Pb[pi][:, e_, qs * 128:(qs + 1) * 128],
                                rhs=V[:, kc, 0:129], start=first, stop=(kc == nkc - 1), skip_group_check=True),
                                waits=[te, acc_free] if (kc == 0 and e_ == 0 and qs == 0) else [te],
                                sig=(e_ == 1 and qs == QS - 1))
                    pb_free[pi] = tm
                    last_pe = tm

                for kc in range(nkc):
                    pp = kc % 2
                    tsc = None
                    for e_ in range(2):
                        bank = sbank[pp][e_]
                        tsc = P.pe(_C("matmul",
                            bank.t[:, 0:QN], lhsT=KT[e_ * 64:(e_ + 1) * 64, kc * 128:(kc + 1) * 128],
                            rhs=QT[e_ * 64:(e_ + 1) * 64, q0:q0 + QN], start=True, stop=True),
                            waits=[hl, bank.free], sig=(e_ == 1))
                    pi = npb % NPB
                    npb += 1
                    te = None
                    for e_ in range(2):
                        bank = sbank[pp][e_]
                        te = P.act(_C("activation",
                            out=Pb[pi][:, e_, 0:QN], in_=bank.t[:, 0:QN], func=AF.Exp, scale=0.125),
                            waits=[tsc, pb_free[pi]] if e_ == 0 else (), sig=True)
                        bank.free = te
                    if pend is not None:
                        pv(*pend)
                    pend = (kc, pi, te)
                pv(*pend)
                ao = aost[nao % 2]
                reads = []
                tcp = None
                asb = accs[nacc % 2]
                tev = None
                for bi_ in range(3):
                    tev = P.dve(_C("tensor_copy", out=asb[:, bi_ * 396:(bi_ + 1) * 396], in_=accb[bi_].t[:, 0:396]),
                                waits=[last_pe, accs_free[nacc % 2]] if bi_ == 0 else (), sig=(bi_ == 2))
                reads.append(tev)
                last_acc = tev
                for qs in range(QS):
                    b0, c0 = asb, qs * 132
                    b1, c1 = asb, (4 + qs) * 132
                    j = nep % 2
                    nep += 1
                    sc = ep[:, qs, :]
                    t_ = P.dve(_C("reciprocal", out=sc[:, 0:1], in_=b0[:, c0 + 128:c0 + 129]),
                               waits=[last_acc, eo_free[j]], sig=True)
                    t_ = P.dve(_C("reciprocal", out=sc[:, 1:2], in_=b1[:, c1 + 128:c1 + 129]), waits=[t_], sig=True)
                    t_ = P.dve(_C("tensor_tensor", out=sc[:, 2:3], in0=sc[:, 1:2], in1=lsm[:, 5:6], op=ALU.mult),
                               waits=[t_, nlam_ready], sig=True)
                    t_ = P.dve(_C("tensor_scalar", out=tmpo[j][:], in0=b1[:, c1:c1 + 128], scalar1=sc[:, 2:3],
                                  scalar2=None, op0=ALU.mult), waits=[t_], sig=True)
                    to = P.dve(_C("scalar_tensor_tensor", out=obuf[j][:], in0=b0[:, c0:c0 + 128], scalar=sc[:, 0:1],
                                  in1=tmpo[j][:], op0=ALU.mult, op1=ALU.add), waits=[t_], sig=True)
                    last_to = to
                    t_ = P.dve(_C("tensor_tensor", out=sqb[j][:], in0=obuf[j][:], in1=obuf[j][:], op=ALU.mult),
                               waits=[to], sig=True)
                    t_ = P.dve(_C("tensor_reduce", out=sc[:, 3:4], in_=sqb[j][:], axis=AX.X, op=ALU.add),
                               waits=[t_], sig=True)
                    t_ = P.dve(_C("tensor_scalar", out=sc[:, 4:5], in0=sc[:, 3:4], scalar1=1.0 / 128.0,
                                  scalar2=LN_EPS, op0=ALU.mult, op1=ALU.add), waits=[t_], sig=True)
                    t_ = P.pool(_C("tensor_tensor", out=sc[:, 5:6], in0=sc[:, 4:5], in1=mhalf[:], op=ALU.pow),
                                waits=[t_], sig=True)
                    t_ = P.dve(_C("tensor_scalar", out=onb[j][:], in0=obuf[j][:], scalar1=sc[:, 5:6],
                                  scalar2=None, op0=ALU.mult), waits=[t_], sig=True)
                    tpv = tpbank.t[:, qs * 64:(qs + 1) * 64].bitcast(BF16)
                    tt = P.pe(_C("transpose", out=tpv, in_=onb[j][:], identity=G.identB[:]),
                              waits=[t_, tpbank.free] if qs == 0 else [t_], sig=True)
                    eo_free[j] = tt
                    tcp = P.dve(_C("tensor_scalar",
                        out=ao[:, qs * 128:(qs + 1) * 128], in0=tpv, scalar1=gsub[:, 0:1], scalar2=None, op0=ALU.mult),
                        waits=[tt, tgs, ao_free[nao % 2]] if qs == 0 else [tt], sig=True)
                tpbank.free = tcp
                acc_free = reads
                accs_free[nacc % 2] = last_to
                nacc += 1
                ao_free[nao % 2] = P.dma("gpsimd", S.AOT[h][:, q0:q0 + QN], ao[:, 0:QN], aosem[nao % 2], waits=[tcp])
                nao += 1
            h_free[k] = last_pe
            hl = hl_next
        P.barrier()
        P.emit()

    with ExitStack() as es:
        sb = mk_sb(nc, es)
        B = alloc_common(G, es)
        aT = [sb(f"aT{k}", [128, NH, 512], BF16) for k in range(2)]
        asem = [G.sems[4], G.sems[5]]
        a_free = [None, None]
        load_ln(G, B, i * 3 + sl)
        for (t0, T, w) in tiles:
            ws.push([S.wo[m] for m in range(16)])
        prev = None
        for ti, (t0, T, w) in enumerate(tiles):
            k = ti % 2
            ta = P.dma("gpsimd", aT[k][:, :, 0:T], S.AOT[:, :, t0:t0 + T].rearrange("h p t -> p h t"), asem[k],
                       waits=[a_free[k]])
            st = OutProj(G, B, NH, X_in, X_out, t0, T, modcol(i, sl, 2, 0), w, t0)
            if prev is not None:
                prev.end()
            st.begin()
            tm = None
            for m in range(16):
                s, buf, lt = ws.next()
                tm = st.slab(m, buf, lt, aT[k], ta)
                ws.release(s, tm)
            a_free[k] = tm
            prev = st
        prev.end()
        P.barrier()
        P.emit()


def phase_lru(G, X_in, X_out):
    nc, P, I, S = G.nc, G.P, G.I, G.S
    NT, tiles, lat_tiles = G.NT, G.tiles, G.lat_tiles
    ws = G.ws
    i, sl = 1, 1
    SL = G.SLAT
    with ExitStack() as es:
        sb = mk_sb(nc, es)
        B = Ctx()
        B.xin = sb("xin", [128, 4, D], F32)
        B.uT = sb("uT", [128, DC, 512], BF16)
        B.xin_sem = G.sems[0]
        B.xin_free = None
        B.uT_free = None
        B.tpb = [G.PS[6], G.PS[7]]
        for b in G.PS:
            b.free = None
        NST = 4
        ga = [sb(f"ga{k}", [128, 512], F32) for k in range(2)]
        gb_ = [sb(f"gb{k}", [128, 512], F32) for k in range(2)]
        gst = [sb(f"gst{k}", [128, 512], BF16) for k in range(NST)]
        xst = [sb(f"xst{k}", [128, 512], F32) for k in range(NST)]
        gsem = [G.sems[1 + k] for k in range(NST)]
        xsem = [G.sems[5 + k] for k in range(NST)]
        g_free = [None] * NST
        x_free = [None] * NST
        gab_free = [None, None]
        pb = [G.PS[0], G.PS[1], G.PS[2], G.PS[3]]
        for (t0, T, w) in tiles:
            ws.push([S.win[m] for m in range(40)])
        ng = 0
        nx = 0
        for (t0, T, w) in tiles:
            prologue(G, B, X_in, t0, T, i, sl, w)
            tm = None
            for m in range(40):
                s, buf, lt = ws.next()
                bank = pb[m % 4]
                for c in range(DC):
                    tm = P.pe(_C("matmul",
                        bank.t[:, 0:T], lhsT=buf[:, c * 128:(c + 1) * 128], rhs=B.uT[:, c, 0:T],
                        start=(c == 0), stop=(c == DC - 1)),
                        waits=[lt, B.uT_ready, bank.free] if c == 0 else (), sig=(c == DC - 1))
                ws.release(s, tm)
                if m < RC:
                    j = ng % 2
                    k_ = ng % NST
                    ng += 1
                    t_ = P.act(_C("activation", out=ga[j][:, 0:T], in_=bank.t[:, 0:T], func=AF.Square),
                               waits=[tm, gab_free[j]], sig=True)
                    P.dve(_C("tensor_scalar", out=ga[j][:, 0:T], in0=ga[j][:, 0:T], scalar1=0.044715,
                                                         scalar2=1.0, op0=ALU.mult, op1=ALU.add), waits=[t_])
                    t_ = P.dve(_C("tensor_tensor", out=ga[j][:, 0:T], in0=ga[j][:, 0:T],
                                                                         in1=bank.t[:, 0:T], op=ALU.mult), sig=True)
                    t_ = P.act(_C("activation", out=gb_[j][:, 0:T], in_=ga[j][:, 0:T], func=AF.Sigmoid,
                                                           scale=1.5957691216057308), waits=[t_], sig=True)
                    t_ = P.dve(_C("tensor_tensor",
                        out=gst[k_][:, 0:T], in0=gb_[j][:, 0:T], in1=bank.t[:, 0:T], op=ALU.mult),
                        waits=[t_, g_free[k_]], sig=True)
                    bank.free = t_
                    gab_free[j] = t_
                    g_free[k_] = P.dma("gpsimd", S.GT[m][:, t0:t0 + T], gst[k_][:, 0:T], gsem[k_], waits=[t_])
                else:
                    k_ = nx % NST
                    nx += 1
                    t_ = P.act(_C("activation", out=xst[k_][:, 0:T], in_=bank.t[:, 0:T], func=AF.Copy),
                               waits=[tm, x_free[k_]], sig=True)
                    bank.free = t_
                    x_free[k_] = P.dma("gpsimd", S.XRT[m - RC][:, t0:t0 + T], xst[k_][:, 0:T], xsem[k_], waits=[t_])
            B.uT_free = tm
        P.barrier()
        P.emit()

    with ExitStack() as es:
        sb = mk_sb(nc, es)
        for b in G.PS:
            b.free = None
        HX = sb("HX", [128, NT], F32)
        xr = sb("xr", [128, 2, NT], F32)
        xrb = sb("xrb", [128, 2, NT], BF16)
        R = sb("R", [128, NT], F32)
        IG = sb("IG", [128, NT], F32)
        MM = sb("MM", [128, NT], F32)
        hsum = sb("hsum", [128, SL], F32)
        Gs = sb("Gs", [128, SL], BF16)
        hgo = sb("hgo", [128, SL], BF16)
        convw = sb("convw", [128, RC, 4], F32)
        lvec = sb("lvec", [128, 6, RC], F32)
        apar = sb("apar", [128, 2, RC], F32)
        cvec = sb("cvec", [128, 2, 2, RC], F32)
        tl1 = P.dma("gpsimd", convw[:], I.convw, G.sems[0])
        tl2 = P.dma("gpsimd", lvec[:], I.lvec, G.sems[0])
        tl3 = P.dma("gpsimd", apar[:], I.apar, G.sems[0])
        t_ = P.act(_C("activation", out=cvec[:, :, 0, :], in_=apar[:], func=AF.Exp, scale=-1.0), waits=[tl3], sig=True)
        t_ = P.act(_C("activation", out=cvec[:, :, 0, :], in_=cvec[:, :, 0, :], func=AF.Ln, bias=1.0),
                   waits=[t_], sig=True)
        t_ = P.dve(_C("tensor_scalar", out=cvec[:, :, 0, :], in0=cvec[:, :, 0, :], scalar1=-8.0, scalar2=None,
                                             op0=ALU.mult), waits=[t_], sig=True)
        c_ready = P.dve(_C("tensor_scalar", out=cvec[:, :, 1, :], in0=cvec[:, :, 0, :], scalar1=2.0, scalar2=None,
                                                  op0=ALU.mult), waits=[t_], sig=True)
        xsem, gsem_, osem = G.sems[1], G.sems[2], G.sems[3]
        ws.push([S.wgate[hb] for hb in range(10)])
        tblocks = [(t, min(512, NT - t)) for t in range(0, NT, 512)]
        segs = [(0, CTX), (CTX, NT)]
        gbanks = [G.PS[0], G.PS[1], G.PS[2], G.PS[3]]
        ngb = 0
        hx_free = None
        xr_free = None
        xrb_free = None
        out_free = None
        g_free = None
        for hb in range(10):
            s, buf, lt = ws.next()
            tcv = None
            tcb = None
            for ic in range(2):
                ch = hb * 2 + ic
                tx = P.dma("gpsimd", HX[:], S.XRT[ch], xsem, waits=[hx_free])
                for (a_, b_) in segs:
                    P.dve(_C("tensor_scalar",
                        out=xr[:, ic, a_:b_], in0=HX[:, a_:b_], scalar1=convw[:, ch, 2:3], scalar2=lvec[:, 0, ch:ch + 1],
                        op0=ALU.mult, op1=ALU.add), waits=[tx, tl1, tl2, xr_free])
                    for (j_, d_) in ((1, 1), (0, 2)):
                        P.dve(_C("scalar_tensor_tensor",
                            out=xr[:, ic, a_ + d_:b_], in0=HX[:, a_:b_ - d_], scalar=convw[:, ch, j_:j_ + 1],
                            in1=xr[:, ic, a_ + d_:b_], op0=ALU.mult, op1=ALU.add))
                    tcv = P.dve(_C("scalar_tensor_tensor",
                        out=xr[:, ic, a_:b_ - 1], in0=HX[:, a_ + 1:b_], scalar=convw[:, ch, 3:4],
                        in1=xr[:, ic, a_:b_ - 1], op0=ALU.mult, op1=ALU.add), sig=True)
                hx_free = tcv
                tcb = P.act(_C("activation", out=xrb[:, ic, :], in_=xr[:, ic, :], func=AF.Copy),
                            waits=[tcv, xrb_free], sig=True)
            last_use = None
            for oc in range(2):
                ch = hb * 2 + oc
                tg_ = P.dma("gpsimd", Gs[:], S.GT[ch][:, CTX:NT], gsem_, waits=[g_free])
                for d in range(2):
                    tsr = None
                    tsi = None
                    for (tb0, tbn) in tblocks:
                        for gt in range(2):
                            bank = gbanks[ngb % 4]
                            ngb += 1
                            tm = None
                            for ic in range(2):
                                col = (((d * 2 + gt) * 2 + ic) * 2 + oc) * 128
                                tm = P.pe(_C("matmul",
                                    bank.t[:, 0:tbn], lhsT=buf[:, col:col + 128], rhs=xrb[:, ic, tb0:tb0 + tbn],
                                    start=(ic == 0), stop=(ic == 1)),
                                    waits=[lt, tcb, bank.free] if ic == 0 else (), sig=(ic == 1))
                            dst = R if gt == 0 else IG
                            bcol = (1 + d) if gt == 0 else (3 + d)
                            ta = P.act(_C("activation",
                                out=dst[:, tb0:tb0 + tbn], in_=bank.t[:, 0:tbn], func=AF.Sigmoid,
                                bias=lvec[:, bcol, ch:ch + 1]), waits=[tm, tl2, last_use], sig=True)
                            bank.free = ta
                            if gt == 0:
                                tsr = ta
                            else:
                                tsi = ta
                    last_gate_pe = tm
                    P.act(_C("activation", out=MM[:], in_=R[:], func=AF.Exp,
                                                            scale=cvec[:, d, 1, ch:ch + 1]), waits=[c_ready, last_use])
                    P.act(_C("activation", out=MM[:], in_=MM[:], func=AF.Sqrt, scale=-1.0, bias=1.0))
                    ta_ = P.act(_C("activation", out=R[:], in_=R[:], func=AF.Exp,
                                                                  scale=cvec[:, d, 0, ch:ch + 1]), sig=True)
                    P.dve(_C("tensor_tensor", out=IG[:], in0=IG[:], in1=MM[:], op=ALU.mult), waits=[ta_, tsi])
                    tb_ = P.dve(_C("tensor_tensor", out=IG[:], in0=IG[:], in1=xr[:, oc, :], op=ALU.mult), sig=True)
                    if d == 0:
                        P.dve(_C("tensor_tensor_scan", out=HX[:], data0=R[:], data1=IG[:], initial=0.0,
                                                             op0=ALU.mult, op1=ALU.add), waits=[tb_, hx_free])
                        tsc = P.dve(_C("tensor_copy", out=hsum[:], in_=HX[:, CTX:NT]), waits=[out_free], sig=True)
                    else:
                        t1_ = P.dve(_C("tensor_tensor_scan",
                            out=HX[:, 0:CTX][:, ::-1], data0=R[:, 0:CTX][:, ::-1], data1=IG[:, 0:CTX][:, ::-1],
                            initial=0.0, op0=ALU.mult, op1=ALU.add), waits=[tb_], sig=True)
                        t2_ = P.dve(_C("scalar_tensor_tensor",
                            out=IG[:, NT - 1:NT], in0=R[:, NT - 1:NT], scalar=HX[:, 0:1], in1=IG[:, NT - 1:NT],
                            op0=ALU.mult, op1=ALU.add), waits=[t1_], sig=True)
                        t3_ = P.dve(_C("tensor_tensor_scan",
                            out=HX[:, CTX:NT][:, ::-1], data0=R[:, CTX:NT][:, ::-1], data1=IG[:, CTX:NT][:, ::-1],
                            initial=0.0, op0=ALU.mult, op1=ALU.add), waits=[t2_], sig=True)
                        tsc = P.dve(_C("tensor_tensor", out=hsum[:], in0=hsum[:], in1=HX[:, CTX:NT], op=ALU.add),
                                    waits=[t3_], sig=True)
                    last_use = tsc
                    hx_free = tsc
                to_ = P.pool(_C("tensor_tensor", out=hgo[:], in0=hsum[:], in1=Gs[:], op=ALU.mult),
                             waits=[tsc, tg_, out_free], sig=True)
                g_free = to_
                out_free = P.dma("gpsimd", S.HGT[ch][:, CTX:NT], hgo[:], osem, waits=[to_])
            ws.release(s, last_gate_pe)
            xr_free = last_use
            xrb_free = last_gate_pe
        P.barrier()
        P.emit()

    with ExitStack() as es:
        sb = mk_sb(nc, es)
        B = alloc_common(G, es)
        aT = [sb(f"hT{k}", [128, RC, 512], BF16) for k in range(2)]
        asem = [G.sems[4], G.sems[5]]
        a_free = [None, None]
        load_ln(G, B, i * 3 + sl)
        for (t0, T, w) in lat_tiles:
            ws.push([S.wout[m] for m in range(16)])
        prev = None
        for ti, (t0, T, w) in enumerate(lat_tiles):
            k = ti % 2
            ta = P.dma("gpsimd", aT[k][:, :, 0:T], S.HGT[:, :, t0:t0 + T].rearrange("h p t -> p h t"), asem[k],
                       waits=[a_free[k]])
            st = OutProj(G, B, RC, X_in, X_out, t0, T, modcol(i, sl, 2, 0), w, t0)
            if prev is not None:
                prev.end()
            st.begin()
            tm = None
            for m in range(16):
                s, buf, lt = ws.next()
                tm = st.slab(m, buf, lt, aT[k], ta)
                ws.release(s, tm)
            a_free[k] = tm
            prev = st
        prev.end()
        P.barrier()
        P.emit()


def _tile_w(W, KC, MC):
    return np.ascontiguousarray(W.reshape(KC, 128, MC, 128).transpose(2, 1, 0, 3)).reshape(MC, 128, KC * 128)


def _fm(v, n):
    return np.ascontiguousarray(np.asarray(v).reshape(n, 128).T)


def rope_tables(S_LAT):
    GRID_W = 64
    t = np.arange(S_LAT)
    row = (t // GRID_W).astype(np.float32)
    col = (t % GRID_W).astype(np.float32)
    inv = (np.float32(10000.0) ** (-np.arange(0, 32, 2, dtype=np.float32) / np.float32(32))).astype(np.float32)
    cos = np.ones((128, CTX + S_LAT), np.float32)
    sin = np.zeros((128, CTX + S_LAT), np.float32)
    for e in range(2):
        for d in range(64):
            pos = row if d < 32 else col
            dd = d % 32
            j = dd % 16
            ang = (pos * inv[j]).astype(np.float32)
            cos[e * 64 + d, CTX:] = np.cos(ang)
            sgn = -1.0 if dd < 16 else 1.0
            sin[e * 64 + d, CTX:] = sgn * np.sin(ang)
    return cos, sin


def _perm_cols():
    p = np.zeros(128, np.int64)
    for e in range(2):
        for d in range(64):
            dd = d % 32
            partner = d + 16 if dd < 16 else d - 16
            p[e * 64 + d] = e * 64 + partner
    return p


def prep_shared(inp, S_LAT):
    f = lambda a: np.asarray(a, dtype=np.float32)
    sh = {}
    wada = f(inp["w_ada"])
    sh["wada"] = np.ascontiguousarray(wada.reshape(2, 16, 128, 36, 512).transpose(0, 3, 2, 1, 4)).reshape(2, 36, 128, 16 * 512)
    bada = f(inp["b_ada"])
    sh["bada"] = np.ascontiguousarray(bada.reshape(2, 144, 128).transpose(2, 0, 1))
    sh["lng"] = f(inp["ln_g"]).reshape(6, D)
    sh["lnb"] = f(inp["ln_b"]).reshape(6, D)
    wg, wu, wd = f(inp["ffn_w_gate"]), f(inp["ffn_w_up"]), f(inp["ffn_w_down"])
    wgu = np.empty((4, FC, 128, 4096), np.float32)
    wdn = np.empty((4, DC, 128, FC * 128), np.float32)
    for i in range(2):
        for j in range(2):
            g = _tile_w(wg[i, j], DC, FC)
            u = _tile_w(wu[i, j], DC, FC)
            wgu[i * 2 + j, :, :, 0:2048] = g
            wgu[i * 2 + j, :, :, 2048:4096] = u
            wdn[i * 2 + j] = _tile_w(wd[i, j], FC, DC)
    sh["wgu"], sh["wdn"] = wgu, wdn
    wqkv = f(inp["attn_w_qkv"])[0]
    perm = _perm_cols()
    wqk = np.empty((32, 128, 4096), np.float32)
    for h in range(NH):
        for t_, off in ((0, 0), (1, D)):
            Wh = wqkv[:, off + h * 128: off + (h + 1) * 128]
            wqk[h * 2 + t_, :, 0:2048] = _tile_w(Wh, DC, 1)[0]
            wqk[h * 2 + t_, :, 2048:4096] = _tile_w(Wh[:, perm], DC, 1)[0]
    sh["wqk"] = wqk
    Wv = wqkv[:, 2 * D:3 * D]
    sh["wv"] = np.ascontiguousarray(Wv.reshape(DC, 128, D).transpose(1, 0, 2)).reshape(128, 8, 4096).transpose(1, 0, 2).copy()
    sh["wo"] = _tile_w(f(inp["attn_w_o"])[0], DC, DC)
    sh["lam"] = np.stack([f(inp["attn_lambda_q1"])[0], f(inp["attn_lambda_k1"])[0],
                          f(inp["attn_lambda_q2"])[0], f(inp["attn_lambda_k2"])[0]])
    sh["subg"] = f(inp["attn_subln_g"])[0].reshape(128, 1).copy()
    cos, sin = rope_tables(S_LAT)
    sh["cos"], sh["sin"] = cos, sin
    sh["win"] = _tile_w(f(inp["lru_w_in"])[0], DC, 40)
    cw = f(inp["lru_conv_w"])[0]
    sh["convw"] = np.ascontiguousarray(cw.reshape(4, RC, 128).transpose(2, 1, 0))
    lv = np.empty((128, 6, RC), np.float32)
    lv[:, 0] = _fm(f(inp["lru_conv_b"])[0], RC)
    ba, bi = f(inp["lru_b_a"])[0], f(inp["lru_b_i"])[0]
    lv[:, 1] = _fm(ba[0], RC)
    lv[:, 2] = _fm(ba[1], RC)
    lv[:, 3] = _fm(bi[0], RC)
    lv[:, 4] = _fm(bi[1], RC)
    lv[:, 5] = 0.0
    sh["lvec"] = lv
    ap_ = f(inp["lru_a_param"])[0]
    sh["apar"] = np.stack([_fm(ap_[0], RC), _fm(ap_[1], RC)], axis=1).copy()
    wa, wi = f(inp["lru_w_a"])[0], f(inp["lru_w_i"])[0]
    wgate = np.empty((10, 128, 2048), np.float32)
    for hb in range(10):
        for d in range(2):
            for gt, Wt in ((0, wa), (1, wi)):
                for ic in range(2):
                    for oc in range(2):
                        col = (((d * 2 + gt) * 2 + ic) * 2 + oc) * 128
                        wgate[hb, :, col:col + 128] = Wt[d, hb, ic * 128:(ic + 1) * 128, oc * 128:(oc + 1) * 128]
    sh["wgate"] = wgate
    sh["wout"] = _tile_w(f(inp["lru_w_out"])[0], RC, DC)
    sh["eye"] = np.eye(128, dtype=np.float32)
    pm = np.zeros((128, 128), np.float32)
    pm[perm, np.arange(128)] = 1.0
    sh["permm"] = pm
    return sh


def prep_core(inp, b):
    f = lambda a: np.asarray(a, dtype=np.float32)
    d = {}
    d["x0"] = np.concatenate([f(inp["ctx"])[b], f(inp["x"])[b]], axis=0)
    cf = np.empty((128, 2, 16), np.float32)
    cf[:, 0] = _fm(f(inp["c"])[b], 16)
    cf[:, 1] = _fm(f(inp["c_ctx"]), 16)
    d["cfm"] = cf
    return d


_CACHE = {}


def kernel(**inputs):
    x = np.asarray(inputs["x"])
    Bn, S_LAT, _ = x.shape
    key = S_LAT
    if key not in _CACHE:
        _CACHE[key] = build(S_LAT)
    nc = _CACHE[key]
    sh = prep_shared(inputs, S_LAT)
    in_maps = []
    for b in range(Bn):
        m = dict(sh)
        m.update(prep_core(inputs, b))
        in_maps.append(m)
    res = run_bass_kernel_spmd(nc, in_maps, core_ids=list(range(Bn)))
    return np.stack([np.asarray(r["out"]) for r in res.results], axis=0).astype(np.float32)
```

```python
import math
from contextlib import ExitStack
import numpy as np
import concourse.bass as bass
import concourse.mybir as mybir
from concourse.bass_utils import run_bass_kernel_spmd

F32 = mybir.dt.float32
BF16 = mybir.dt.bfloat16
AF = mybir.ActivationFunctionType
ALU = mybir.AluOpType
AX = mybir.AxisListType

D = 2048
DC = 16
DFF = 5632
FC = 44
CTX = 256
DRNN = 2560
RC = 20
NH = 16
ALPHA = 2.0 ** 0.5
LN_EPS = 1e-6
LAMBDA_INIT = 0.8 - 0.6 * math.exp(-0.3 * 0)
WSLOT = 5632
NWBUF = 4
PIPE_QK = True
ENGS = ("tensor", "vector", "scalar", "gpsimd", "sync")


def _C(name, *a, **k):
    return (name, a, k)


class DSem:
    def __init__(self, h):
        self.h = h
        self.n = 0

    def tok(self):
        return (self.h, self.n)


class Op:
    __slots__ = ("fn", "ws", "inc", "dma")

    def __init__(self, fn, ws, inc, dma):
        self.fn, self.ws, self.inc, self.dma = fn, ws, inc, dma


class Prog:
    def __init__(self, nc):
        self.nc = nc
        self.ops = {e: [] for e in ENGS}
        self.esem = {e: nc.alloc_semaphore("es_" + e) for e in ENGS if e != "sync"}
        self.ecnt = {e: 0 for e in ENGS}
        self.waited = {e: {} for e in ENGS}
        self.dsems = []
        self.ninst = 0

    def new_sem(self, name=None):
        s = DSem(self.nc.alloc_semaphore(name or f"ds{len(self.dsems)}"))
        self.dsems.append(s)
        return s

    def _flat(self, toks, out):
        for t in toks:
            if t is None:
                continue
            if isinstance(t, list) or (isinstance(t, tuple) and (len(t) != 2 or isinstance(t[0], (tuple, list)) or t[0] is None)):
                self._flat(t, out)
            else:
                out.append(t)

    def _waits(self, eng, toks):
        fl = []
        self._flat(toks, fl)
        ws = []
        for sem, v in fl:
            if v <= 0:
                continue
            key = id(sem)
            if self.waited[eng].get(key, 0) >= v:
                continue
            self.waited[eng][key] = v
            ws.append((sem, v))
        return ws

    def op(self, eng, fn, waits=(), sig=False):
        ws = self._waits(eng, waits)
        tok = None
        inc = None
        if sig:
            self.ecnt[eng] += 1
            inc = (self.esem[eng], 1)
            tok = (self.esem[eng], self.ecnt[eng])
        self.ops[eng].append(Op(fn, ws, inc, False))
        self.ninst += 1
        return tok

    def pe(self, fn, waits=(), sig=False):
        return self.op("tensor", fn, waits, sig)

    def dve(self, fn, waits=(), sig=False):
        return self.op("vector", fn, waits, sig)

    def act(self, fn, waits=(), sig=False):
        return self.op("scalar", fn, waits, sig)

    def pool(self, fn, waits=(), sig=False):
        return self.op("gpsimd", fn, waits, sig)

    def dma(self, eng, out, in_, dsem, waits=()):
        ws = self._waits(eng, waits)
        dsem.n += 16
        self.ops[eng].append(Op(lambda e, o=out, i=in_: e.dma_start(out=o, in_=i), ws, (dsem.h, 16), True))
        self.ninst += 1
        return (dsem.h, dsem.n)

    def wait_only(self, eng, waits):
        ws = self._waits(eng, waits)
        if ws:
            self.ops[eng].append(Op(None, ws, None, False))

    def barrier(self):
        toks = []
        for e in ("tensor", "vector", "scalar", "gpsimd"):
            last = None
            for o in reversed(self.ops[e]):
                if o.fn is not None and not o.dma:
                    last = o
                    break
            if last is not None and last.inc is None:
                self.ecnt[e] += 1
                last.inc = (self.esem[e], 1)
            if self.ecnt[e] > 0:
                toks.append((self.esem[e], self.ecnt[e]))
        for s in self.dsems:
            if s.n > 0:
                toks.append((s.h, s.n))
        for e in ENGS:
            self.wait_only(e, toks)

    def emit(self):
        nc = self.nc
        ops = self.ops
        with nc.Block() as block:
            def body(ename):
                def f(e):
                    for o in ops[ename]:
                        for sem, v in o.ws:
                            e.wait_ge(sem, v)
                        if o.fn is None:
                            continue
                        ins = o.fn(e) if callable(o.fn) else getattr(e, o.fn[0])(*o.fn[1], **o.fn[2])
                        if o.inc is not None:
                            ins.then_inc(o.inc[0], o.inc[1])
                return f
            block.sync(body("sync"))
            block.tensor(body("tensor"))
            block.vector(body("vector"))
            block.scalar(body("scalar"))
            block.gpsimd(body("gpsimd"))
        self.ops = {e: [] for e in ENGS}


class Bank:
    def __init__(self, t):
        self.t = t
        self.free = None


class WStream:
    def __init__(self, P, nc):
        self.P = P
        self.slots = [nc.alloc_sbuf_tensor(f"wslot{i}", [128, WSLOT], BF16) for i in range(NWBUF)]
        self.sems = [P.new_sem(f"wsem{i}") for i in range(NWBUF)]
        self.free = [None] * NWBUF
        self.queue = []
        self.inflight = []
        self.gi = 0
        self.nrel = 0
        self.ready = None

    def push(self, srcs):
        self.queue.extend(srcs)

    def _issue(self):
        while self.queue and self.gi - self.nrel < NWBUF:
            src = self.queue.pop(0)
            s = self.gi % NWBUF
            self.gi += 1
            E = src.shape[1]
            tok = self.P.dma("sync", self.slots[s][:, 0:E], src, self.sems[s], waits=[self.free[s], self.ready])
            self.inflight.append((s, tok))

    def next(self):
        self._issue()
        s, tok = self.inflight.pop(0)
        return s, self.slots[s], tok

    def release(self, s, tok):
        self.free[s] = tok
        self.nrel += 1
        self._issue()


class Ctx:
    pass


_UID = [0]


def mk_sb(nc, es):
    _UID[0] += 1
    u = _UID[0]
    return lambda n, s, d: es.enter_context(nc.sbuf_tensor(f"{n}_u{u}", s, d))


def build(S_LAT=4096, dbg=False, phases=None):
    NT = CTX + S_LAT
    NKC = NT // 128
    nc = bass.Bass("TRN2", target_bir_lowering=False)
    P = Prog(nc)
    G = Ctx()
    G.nc, G.P, G.SLAT, G.NT, G.NKC = nc, P, S_LAT, NT, NKC
    tiles = [(0, CTX, 1)] + [(CTX + 512 * i, 512, 0) for i in range(S_LAT // 512)]
    lat_tiles = tiles[1:]
    G.tiles, G.lat_tiles = tiles, lat_tiles

    def din(name, shape, dt=F32):
        return nc.dram_tensor(name, list(shape), dt, kind="ExternalInput").ap()

    skind = "ExternalOutput" if dbg else "Internal"

    def dscr(name, shape, dt):
        return nc.dram_tensor(name, list(shape), dt, kind=skind).ap()

    I = Ctx()
    I.x0 = din("x0", [NT, D])
    I.cfm = din("cfm", [128, 2, 16])
    I.wada = din("wada", [2, 36, 128, 16 * 512])
    I.bada = din("bada", [128, 2, 144])
    I.lng = din("lng", [6, D])
    I.lnb = din("lnb", [6, D])
    I.wgu = din("wgu", [4, FC, 128, 4096])
    I.wdn = din("wdn", [4, DC, 128, FC * 128])
    I.wqk = din("wqk", [32, 128, 4096])
    I.wv = din("wv", [8, 128, 4096])
    I.wo = din("wo", [16, 128, 2048])
    I.lam = din("lam", [4, 64])
    I.subg = din("subg", [128, 1])
    I.cos = din("cos", [128, NT])
    I.sin = din("sin", [128, NT])
    I.win = din("win", [40, 128, 2048])
    I.convw = din("convw", [128, RC, 4])
    I.lvec = din("lvec", [128, 6, RC])
    I.apar = din("apar", [128, 2, RC])
    I.wgate = din("wgate", [10, 128, 2048])
    I.wout = din("wout", [16, 128, RC * 128])
    I.eye = din("eye", [128, 128])
    I.permm = din("permm", [128, 128])
    out = nc.dram_tensor("out", [S_LAT, D], F32, kind="ExternalOutput").ap()

    S = Ctx()
    S.XA = dscr("XA", [NT, D], F32)
    S.XB = dscr("XB", [NT, D], F32)
    S.wgu = dscr("s_wgu", [4, FC, 128, 4096], BF16)
    S.wdn = dscr("s_wdn", [4, DC, 128, FC * 128], BF16)
    S.wqk = dscr("s_wqk", [32, 128, 4096], BF16)
    S.wv = dscr("s_wv", [8, 128, 4096], BF16)
    S.wo = dscr("s_wo", [16, 128, 2048], BF16)
    S.win = dscr("s_win", [40, 128, 2048], BF16)
    S.wgate = dscr("s_wgate", [10, 128, 2048], BF16)
    S.wout = dscr("s_wout", [16, 128, RC * 128], BF16)
    S.QT = dscr("QT", [NH, 128, NT], BF16)
    S.KT = dscr("KT", [NH, 128, NT], BF16)
    S.V = dscr("V", [NT, D], BF16)
    S.AOT = dscr("AOT", [NH, 128, NT], BF16)
    S.GT = dscr("GT", [RC, 128, NT], BF16)
    S.XRT = dscr("XRT", [RC, 128, NT], F32)
    S.HGT = dscr("HGT", [RC, 128, NT], BF16)
    G.I, G.S, G.out = I, S, out

    G.ws = WStream(P, nc)
    G.identF = nc.alloc_sbuf_tensor("identF", [128, 128], F32)
    G.identB = nc.alloc_sbuf_tensor("identB", [128, 128], BF16)
    G.MOD = nc.alloc_sbuf_tensor("MOD", [128, 288, 2], F32)
    G.PS = [Bank(nc.alloc_psum_tensor(f"bank{i}", [128, 512], F32)) for i in range(8)]
    G.sems = [P.new_sem(f"gs{i}") for i in range(24)]

    phases = phases or ["ada", "conv", "l0f0", "attn", "l0f2", "l1f0", "lru", "l1f2"]

    t = P.dma("sync", G.identF[:], I.eye, G.sems[0])
    P.dve(_C("tensor_copy", out=G.identB[:], in_=G.identF[:]), waits=[t])
    P.barrier()
    P.emit()

    if "ada" in phases:
        phase_ada(G, conv=("conv" in phases))
    X0, XA, XB = I.x0, S.XA, S.XB
    if "l0f0" in phases:
        phase_ffn(G, 0, 0, X0, XA, tiles)
    if "attn" in phases:
        phase_attn(G, XA, XB)
    if "l0f2" in phases:
        phase_ffn(G, 0, 2, XB, XA, tiles)
    if "l1f0" in phases:
        phase_ffn(G, 1, 0, XA, XB, tiles)
    if "lru" in phases:
        phase_lru(G, XB, XA)
    if "l1f2" in phases:
        phase_ffn(G, 1, 2, XA, out, lat_tiles, out_off=CTX)
    P.barrier()
    P.emit()
    return nc


def modcol(i, sl, comp, c):
    return i * 144 + (sl * 3 + comp) * 16 + c


def phase_ada(G, conv=True):
    nc, P, I = G.nc, G.P, G.I
    with ExitStack() as es:
        sb = mk_sb(nc, es)
        cv = Converter(G, es) if conv else None
        craw = sb("craw", [128, 2, 16], F32)
        cs = sb("cs", [128, 16, 2], F32)
        bada = sb("badas", [128, 2, 144], F32)
        slabs = [sb(f"adaslab{i}", [128, 16 * 512], F32) for i in range(2)]
        ssem = [G.sems[0], G.sems[1]]
        t0 = P.dma("sync", craw[:], I.cfm, G.sems[2])
        t1 = P.dma("sync", bada[:], I.bada, G.sems[3])
        ta = None
        for w in range(2):
            ta = P.act(_C("activation", out=cs[:, :, w], in_=craw[:, w, :], func=AF.Silu), waits=[t0], sig=True)
        sfree = [None, None]
        banks = [G.PS[0], G.PS[1]]
        n = 0
        for i in range(2):
            for cb in range(36):
                s = n % 2
                lt = P.dma("sync", slabs[s][:], I.wada[i, cb], ssem[s], waits=[sfree[s]])
                for q in range(4):
                    bank = banks[(n * 4 + q) % 2]
                    tm = None
                    for k in range(16):
                        tm = P.pe(_C("matmul",
                            bank.t[:, 0:2], lhsT=slabs[s][:, k * 512 + q * 128: k * 512 + (q + 1) * 128],
                            rhs=cs[:, k, :], start=(k == 0), stop=(k == 15)),
                            waits=[lt, ta, bank.free] if k == 0 else (), sig=(k == 15))
                    col = i * 144 + cb * 4 + q
                    tv = P.dve(_C("tensor_scalar",
                        out=G.MOD[:, col, :], in0=bank.t[:, 0:2], scalar1=bada[:, i, cb * 4 + q:cb * 4 + q + 1], scalar2=None,
                        op0=ALU.add), waits=[tm, t1], sig=True)
                    bank.free = tv
                sfree[s] = tm
                n += 1
                if cv is not None:
                    cv.step(6)
        if cv is not None:
            cv.step(10 ** 6)
        for i in range(2):
            for sl in range(3):
                c0 = modcol(i, sl, 1, 0)
                P.dve(_C("tensor_scalar", out=G.MOD[:, c0:c0 + 16, :], in0=G.MOD[:, c0:c0 + 16, :],
                                                       scalar1=1.0, scalar2=None, op0=ALU.add), waits=[tv])
                if sl != 1:
                    g0 = modcol(i, sl, 2, 0)
                    P.dve(_C("tensor_scalar", out=G.MOD[:, g0:g0 + 16, :], in0=G.MOD[:, g0:g0 + 16, :],
                                                           scalar1=0.5, scalar2=None, op0=ALU.mult))
        P.barrier()
        P.emit()


class Converter:
    def __init__(self, G, es):
        self.G = G
        nc, P, I, S = G.nc, G.P, G.I, G.S
        jobs = []

        def add(src, dst):
            sh = src.shape
            lead = sh[:-2]
            E = sh[-1]
            idxs = [()]
            for n in lead:
                idxs = [ix + (j,) for ix in idxs for j in range(n)]
            for ix in idxs:
                s_ = src
                d_ = dst
                for j in ix:
                    s_ = s_[j]
                    d_ = d_[j]
                for e0 in range(0, E, 4096):
                    n_ = min(4096, E - e0)
                    jobs.append((s_[:, e0:e0 + n_], d_[:, e0:e0 + n_], n_))

        for a, b in ((I.wgu, S.wgu), (I.wdn, S.wdn), (I.wqk, S.wqk), (I.wv, S.wv), (I.wo, S.wo),
                     (I.win, S.win), (I.wgate, S.wgate), (I.wout, S.wout)):
            add(a, b)
        self.jobs = jobs
        self.j = 0
        NB = self.NB = 3
        sb = mk_sb(nc, es)
        self.fin = [sb(f"cvin{i}", [128, 4096], F32) for i in range(NB)]
        self.fout = [sb(f"cvout{i}", [128, 4096], BF16) for i in range(NB)]
        self.lsem = [G.sems[8 + i] for i in range(NB)]
        self.ssem = [G.sems[8 + NB + i] for i in range(NB)]
        self.infree = [None] * NB
        self.outfree = [None] * NB

    def step(self, n):
        P = self.G.P
        for _ in range(n):
            if self.j >= len(self.jobs):
                return
            j = self.j
            self.j += 1
            src, dst, n_ = self.jobs[j]
            s = j % self.NB
            fin, fout = self.fin[s], self.fout[s]
            lt = P.dma("sync", fin[:, 0:n_], src, self.lsem[s], waits=[self.infree[s]])
            if j % 2 == 0:
                ct = P.dve(_C("tensor_copy", out=fout[:, 0:n_], in_=fin[:, 0:n_]),
                           waits=[lt, self.outfree[s]], sig=True)
            else:
                ct = P.act(_C("activation", out=fout[:, 0:n_], in_=fin[:, 0:n_], func=AF.Copy),
                           waits=[lt, self.outfree[s]], sig=True)
            self.infree[s] = ct
            self.outfree[s] = P.dma("gpsimd", dst, fout[:, 0:n_], self.ssem[s], waits=[ct])


def prologue(G, B, X_in, t0, T, i, sl, w):
    P = G.P
    S4 = T // 128
    tokx = P.dma("gpsimd", B.xin[:, 0:S4, :], X_in[t0:t0 + T, :].rearrange("(s p) d -> p s d", p=128),
                 B.xin_sem, waits=[B.xin_free])
    ev = None
    tp = None
    for c in range(DC):
        bank = B.tpb[c % 2]
        for s in range(S4):
            tp = P.pe(_C("transpose",
                out=bank.t[:, s * 128:(s + 1) * 128], in_=B.xin[:, s, c * 128:(c + 1) * 128], identity=G.identF[:]),
                waits=[tokx, bank.free] if s == 0 else (), sig=(s == S4 - 1))
        sc = modcol(i, sl, 1, c)
        sh = modcol(i, sl, 0, c)
        ev = P.dve(_C("tensor_scalar",
            out=B.uT[:, c, 0:T], in0=bank.t[:, 0:T], scalar1=G.MOD[:, sc, w:w + 1], scalar2=G.MOD[:, sh, w:w + 1],
            op0=ALU.mult, op1=ALU.add), waits=[tp, B.uT_free] if c == 0 else [tp], sig=True)
        bank.free = ev
    B.xin_free = tp
    B.uT_ready = ev


def layer_norm_rows(G, B, xs, s, lnw, after):
    P = G.P
    tk = None
    for q in range(4):
        tk = P.dve(_C("bn_stats", out=B.stats[:, s, q, :], in_=xs[:, q * 512:(q + 1) * 512]),
                   waits=[after] if q == 0 else (), sig=(q == 3))
    t1 = P.dve(_C("bn_aggr", out=B.mv[:, s, :], in_=B.stats[:, s, :, :].rearrange("p a b -> p (a b)")),
               waits=[tk], sig=True)
    t2 = P.dve(_C("tensor_scalar", out=B.sm[:, s, 0:1], in0=B.mv[:, s, 1:2], scalar1=LN_EPS, scalar2=None,
                                         op0=ALU.add), waits=[t1], sig=True)
    t3 = P.act(_C("activation", out=B.sm[:, s, 1:2], in_=B.sm[:, s, 0:1], func=AF.Sqrt), waits=[t2], sig=True)
    t4 = P.dve(_C("reciprocal", out=B.sm[:, s, 2:3], in_=B.sm[:, s, 1:2]), waits=[t3], sig=True)
    t5 = P.dve(_C("scalar_tensor_tensor", out=B.sm[:, s, 3:4], in0=B.mv[:, s, 0:1], scalar=-1.0,
                                                in1=B.sm[:, s, 2:3], op0=ALU.mult, op1=ALU.mult), waits=[t4], sig=True)
    t6 = P.act(_C("activation", out=xs, in_=xs, func=AF.Identity, scale=B.sm[:, s, 2:3], bias=B.sm[:, s, 3:4]),
               waits=[t5, after], sig=True)
    P.pool(_C("tensor_tensor", out=xs, in0=xs, in1=B.gbc[:], op=ALU.mult), waits=[t6, lnw])
    t8 = P.pool(_C("tensor_tensor", out=xs, in0=xs, in1=B.bbc[:], op=ALU.add), sig=True)
    return t8


class OutProj:
    def __init__(self, G, B, KC, X_in, X_out, t0, T, gcol0, w, out_row0):
        self.G, self.B, self.KC, self.X_in, self.X_out = G, B, KC, X_in, X_out
        self.t0, self.T, self.gcol0, self.w, self.out_row0 = t0, T, gcol0, w, out_row0
        self.pending = None
        self.last_stt = None

    def begin(self):
        G, B, P = self.G, self.B, self.G.P
        S4 = self.T // 128
        self.xl = P.dma("gpsimd", B.x_all[:, 0:S4, :],
                        self.X_in[self.t0:self.t0 + self.T, :].rearrange("(s p) d -> p s d", p=128),
                        B.xall_sem, waits=[B.xall_free])

    def slab(self, m, buf, ltok, actT, act_ready):
        G, B, P, T, KC = self.G, self.B, self.G.P, self.T, self.KC
        bank = B.dnb[m % 2]
        tm = None
        for k in range(KC):
            tm = P.pe(_C("matmul", bank.t[:, 0:T], lhsT=buf[:, k * 128:(k + 1) * 128],
                                                         rhs=actT[:, k, 0:T], start=(k == 0), stop=(k == KC - 1)),
                      waits=[ltok, act_ready, bank.free] if k == 0 else (), sig=(k == KC - 1))
        gyb = B.gy[m % 3]
        gc = self.gcol0 + m
        ta = P.act(_C("activation",
            out=gyb[:, 0:T], in_=bank.t[:, 0:T], func=AF.Copy, scale=G.MOD[:, gc, self.w:self.w + 1]),
            waits=[tm, B.gy_free[m % 3]], sig=True)
        bank.free = ta
        if self.pending is not None:
            self.transposes(*self.pending)
        self.pending = (m, gyb, ta)
        return tm

    def transposes(self, m, gyb, ta):
        G, B, P, T = self.G, self.B, self.G.P, self.T
        S4 = T // 128
        bank = B.tpb[m % 2]
        tp = None
        for s in range(S4):
            tp = P.pe(_C("transpose", out=bank.t[:, s * 128:(s + 1) * 128],
                                                            in_=gyb[:, s * 128:(s + 1) * 128], identity=G.identF[:]),
                      waits=[ta, bank.free] if s == 0 else (), sig=(s == S4 - 1))
        B.gy_free[m % 3] = tp
        xv = B.x_all[:, 0:S4, m * 128:(m + 1) * 128]
        tv = P.dve(_C("scalar_tensor_tensor",
            out=xv, in0=xv, scalar=ALPHA, in1=bank.t[:, 0:T].rearrange("p (s d) -> p s d", d=128),
            op0=ALU.mult, op1=ALU.add), waits=[tp, self.xl], sig=True)
        bank.free = tv
        self.last_stt = tv

    def end(self):
        G, B, P, T = self.G, self.B, self.G.P, self.T
        S4 = T // 128
        self.transposes(*self.pending)
        self.pending = None
        sts = []
        for s in range(S4):
            xs = B.x_all[:, s, :]
            t8 = layer_norm_rows(G, B, xs, s, B.ln_ready, self.last_stt)
            r0 = self.out_row0 + s * 128
            sts.append(P.dma("gpsimd", self.X_out[r0:r0 + 128, :], xs, B.st_sem, waits=[t8]))
        B.xall_free = sts


def alloc_common(G, es, with_h=True):
    nc = G.nc
    sb = mk_sb(nc, es)
    B = Ctx()
    B.xin = sb("xin", [128, 4, D], F32)
    B.uT = sb("uT", [128, DC, 512], BF16)
    B.x_all = sb("x_all", [128, 4, D], F32)
    B.gbc = sb("gbc", [128, D], F32)
    B.bbc = sb("bbc", [128, D], F32)
    B.gy = [sb(f"gy{i}", [128, 512], F32) for i in range(3)]
    B.stats = sb("stats", [128, 4, 4, 6], F32)
    B.mv = sb("mv", [128, 4, 2], F32)
    B.sm = sb("sm", [128, 4, 4], F32)
    B.gy_free = [None] * 3
    B.xin_sem, B.xall_sem, B.st_sem, B.ln_sem = G.sems[0], G.sems[1], G.sems[2], G.sems[3]
    B.xin_free = None
    B.xall_free = None
    B.uT_free = None
    B.uT_ready = None
    B.tpb = [G.PS[6], G.PS[7]]
    B.dnb = [G.PS[4], G.PS[5]]
    for b in G.PS:
        b.free = None
    return B


def load_ln(G, B, idx):
    P, I = G.P, G.I
    P.dma("gpsimd", B.gbc[:], I.lng[idx].partition_broadcast(128), B.ln_sem)
    B.ln_ready = P.dma("gpsimd", B.bbc[:], I.lnb[idx].partition_broadcast(128), B.ln_sem)


def phase_ffn(G, i, sl, X_in, X_out, tiles, out_off=0):
    nc, P, S = G.nc, G.P, G.S
    fi = i * 2 + (0 if sl == 0 else 1)
    ws = G.ws
    with ExitStack() as es:
        sb = mk_sb(nc, es)
        B = alloc_common(G, es)
        hT = sb("hT", [128, FC, 512], BF16)
        sg = [sb(f"sg{k}", [128, 512], F32) for k in range(2)]
        sg_free = [None, None]
        gub = [(G.PS[0], G.PS[1]), (G.PS[2], G.PS[3])]
        load_ln(G, B, i * 3 + sl)
        for (t0, T, w) in tiles:
            ws.push([S.wgu[fi, f] for f in range(FC)])
            ws.push([S.wdn[fi, m] for m in range(DC)])
        prev = None
        for ti, (t0, T, w) in enumerate(tiles):
            prologue(G, B, X_in, t0, T, i, sl, w)
            if prev is not None:
                prev.end()
                prev = None
            tlast = None
            for f in range(FC):
                s, buf, lt = ws.next()
                pp = f % 2
                toks = [None, None]
                for gu in range(2):
                    bank = gub[pp][gu]
                    for c in range(DC):
                        toks[gu] = P.pe(_C("matmul",
                            bank.t[:, 0:T], lhsT=buf[:, (gu * 16 + c) * 128:(gu * 16 + c + 1) * 128],
                            rhs=B.uT[:, c, 0:T], start=(c == 0), stop=(c == DC - 1)),
                            waits=[lt, B.uT_ready, bank.free] if c == 0 else (), sig=(c == DC - 1))
                ws.release(s, toks[1])
                gb, ub = gub[pp]
                ta = P.act(_C("activation", out=sg[pp][:, 0:T], in_=gb.t[:, 0:T], func=AF.Silu),
                           waits=[toks[0], sg_free[pp]], sig=True)
                gb.free = ta
                tv = P.dve(_C("tensor_tensor", out=hT[:, f, 0:T], in0=sg[pp][:, 0:T],
                                                                        in1=ub.t[:, 0:T], op=ALU.mult),
                           waits=[ta, toks[1]], sig=True)
                ub.free = tv
                sg_free[pp] = tv
                tlast = toks[1]
            B.uT_free = tlast
            h_ready = tv
            st = OutProj(G, B, FC, X_in, X_out, t0, T, modcol(i, sl, 2, 0), w, t0 - out_off)
            st.begin()
            for m in range(DC):
                s, buf, lt = ws.next()
                tm = st.slab(m, buf, lt, hT, h_ready)
                ws.release(s, tm)
            prev = st
        prev.end()
        P.barrier()
        P.emit()


def phase_attn(G, X_in, X_out):
    nc, P, I, S = G.nc, G.P, G.I, G.S
    NT, NKC, tiles = G.NT, G.NKC, G.tiles
    ws = G.ws
    i, sl = 0, 1
    with ExitStack() as es:
        sb = mk_sb(nc, es)
        B = Ctx()
        B.xin = sb("xin", [128, 4, D], F32)
        B.uT = sb("uT", [128, DC, 512], BF16)
        B.xin_sem = G.sems[0]
        B.xin_free = None
        B.uT_free = None
        B.tpb = [G.PS[6], G.PS[7]]
        for b in G.PS:
            b.free = None
        wv = sb("wv", [128, DC, D], BF16)
        cosb = [sb(f"cos{k}", [128, 512], F32) for k in range(2)]
        sinb = [sb(f"sin{k}", [128, 512], F32) for k in range(2)]
        cs_free = [None, None]
        t1 = [sb(f"rt1_{k}", [128, 512], F32) for k in range(2)]
        t2 = [sb(f"rt2_{k}", [128, 512], F32) for k in range(2)]
        qst = [sb(f"qst{k}", [128, 512], BF16) for k in range(4)]
        qsb = [sb(f"qsb{k}", [128, 512], BF16) for k in range(2)]
        qsb_free = [None, None]
        permF = sb("permF", [128, 128], F32)
        permT = sb("permT", [128, 128], BF16)
        tpl = P.dma("gpsimd", permF[:], I.permm, G.sems[9])
        perm_ready = P.dve(_C("tensor_copy", out=permT[:], in_=permF[:]), waits=[tpl], sig=True)
        vst = [sb(f"vst{k}", [128, D], BF16) for k in range(2)]
        twv = None
        for k in range(8):
            twv = P.dma("gpsimd", wv[:, 2 * k:2 * k + 2, :].rearrange("p a b -> p (a b)"), S.wv[k], G.sems[2])
        qsem = [G.sems[3 + k] for k in range(4)]
        vsem = [G.sems[7 + k] for k in range(2)]
        q_free = [None] * 4
        v_free = [None] * 2
        t_free = [None, None]
        qkb = [(G.PS[0], G.PS[1]), (G.PS[2], G.PS[3])]
        vb = [G.PS[4], G.PS[5]]
        for (t0, T, w) in tiles:
            ws.push([S.wqk[m][:, 0:2048] for m in range(32)])
        nq = 0
        nv = 0
        for ti, (t0, T, w) in enumerate(tiles):
            S4 = T // 128
            cos, sin = cosb[ti % 2], sinb[ti % 2]
            P.dma("gpsimd", cos[:, 0:T], I.cos[:, t0:t0 + T], G.sems[1], waits=[cs_free[ti % 2]])
            tc_ = ts_ = P.dma("gpsimd", sin[:, 0:T], I.sin[:, t0:t0 + T], G.sems[1])
            prologue(G, B, X_in, t0, T, i, sl, w)
            tlast = None
            pend = None

            def finish(m, pp, tq, tcopy):
                nonlocal nq, tlast
                b0, b1 = qkb[pp]
                tpm = P.pe(_C("matmul", b1.t[:, 0:T], lhsT=permT[:], rhs=qsb[pp][:, 0:T], start=True, stop=True),
                           waits=[tcopy, b1.free, perm_ready], sig=True)
                qsb_free[pp] = tpm
                tlast = tpm
                P.dve(_C("tensor_tensor", out=t1[pp][:, 0:T], in0=b0.t[:, 0:T], in1=cos[:, 0:T], op=ALU.mult),
                      waits=[tq, tcopy, tc_, t_free[pp]])
                td = P.dve(_C("tensor_tensor", out=t2[pp][:, 0:T], in0=b1.t[:, 0:T], in1=sin[:, 0:T], op=ALU.mult),
                           waits=[tpm, ts_], sig=True)
                b0.free = td
                b1.free = td
                qs_ = nq % 4
                tp_ = P.pool(_C("tensor_tensor", out=qst[qs_][:, 0:T], in0=t1[pp][:, 0:T], in1=t2[pp][:, 0:T],
                                op=ALU.add), waits=[td, q_free[qs_]], sig=True)
                t_free[pp] = tp_
                dst = (S.QT if m % 2 == 0 else S.KT)[m // 2][:, t0:t0 + T]
                q_free[qs_] = P.dma("gpsimd", dst, qst[qs_][:, 0:T], qsem[qs_], waits=[tp_])
                nq += 1
                cs_free[ti % 2] = td

            for m in range(32):
                s, buf, lt = ws.next()
                pp = m % 2
                b0, b1 = qkb[pp]
                tq = None
                for c in range(DC):
                    tq = P.pe(_C("matmul", b0.t[:, 0:T], lhsT=buf[:, c * 128:(c + 1) * 128],
                                 rhs=B.uT[:, c, 0:T], start=(c == 0), stop=(c == DC - 1)),
                              waits=[lt, B.uT_ready, b0.free] if c == 0 else (), sig=(c == DC - 1))
                ws.release(s, tq)
                tcopy = P.act(_C("activation", out=qsb[pp][:, 0:T], in_=b0.t[:, 0:T], func=AF.Copy),
                              waits=[tq, qsb_free[pp]], sig=True)
                if PIPE_QK:
                    if pend is not None:
                        finish(*pend)
                    pend = (m, pp, tq, tcopy)
                else:
                    finish(m, pp, tq, tcopy)
            if PIPE_QK:
                finish(*pend)
            for s in range(S4):
                vs_ = nv % 2
                tev = None
                for n_ in range(4):
                    bank = vb[n_ % 2]
                    tm = None
                    for c in range(DC):
                        tm = P.pe(_C("matmul",
                            bank.t[:, :], lhsT=B.uT[:, c, s * 128:(s + 1) * 128],
                            rhs=wv[:, c, n_ * 512:(n_ + 1) * 512], start=(c == 0), stop=(c == DC - 1)),
                            waits=[twv, B.uT_ready, bank.free] if c == 0 else (), sig=(c == DC - 1))
                    tlast = tm
                    tev = P.act(_C("activation",
                        out=vst[vs_][:, n_ * 512:(n_ + 1) * 512], in_=bank.t[:, :], func=AF.Copy),
                        waits=[tm, v_free[vs_]] if n_ == 0 else [tm], sig=True)
                    bank.free = tev
                r0 = t0 + s * 128
                v_free[vs_] = P.dma("gpsimd", S.V[r0:r0 + 128, :], vst[vs_][:], vsem[vs_], waits=[tev])
                nv += 1
            B.uT_free = tlast
        P.barrier()
        P.emit()

    import os
    if os.environ.get("STOP_AFTER_QKV"):
        return
    with ExitStack() as es:
        sb = mk_sb(nc, es)
        for b in G.PS:
            b.free = None
        KTs = [sb(f"KTs{k}", [128, NT], BF16) for k in range(2)]
        QTs = [sb(f"QTs{k}", [128, NT], BF16) for k in range(2)]
        Vs = [sb(f"Vs{k}", [128, NKC, 132], BF16) for k in range(2)]
        NPB = 6
        Pb = [sb(f"Pb{k}", [128, 2, 512], BF16) for k in range(NPB)]
        aost = [sb(f"aost{k}", [128, 512], BF16) for k in range(2)]
        lamp = sb("lamp", [128, 4, 64], F32)
        lsm = sb("lsm", [128, 8], F32)
        gsub = sb("gsub", [128, 1], F32)
        mhalf = sb("mhalf", [128, 1], F32)
        ep = sb("ep", [128, 4, 8], F32)
        accs = [sb(f"accs{k}", [128, 3 * 396], F32) for k in range(2)]
        accs_free = [None, None]
        nacc = 0
        tmpo = [sb(f"tmpo{k}", [128, 128], F32) for k in range(2)]
        obuf = [sb(f"obuf{k}", [128, 128], F32) for k in range(2)]
        sqb = [sb(f"sqb{k}", [128, 128], F32) for k in range(2)]
        onb8 = [[sb(f"onb{k}_{q}", [128, 128], BF16) for q in range(4)] for k in range(2)]
        onb_free = [None, None]
        dq = []
        hsem = [G.sems[0], G.sems[1]]
        aosem = [G.sems[2], G.sems[3]]
        tl = P.dma("gpsimd", lamp[:].rearrange("p a b -> p (a b)"),
                   I.lam.rearrange("a b -> (a b)").partition_broadcast(128), G.sems[4])
        tg = P.dma("gpsimd", gsub[:], I.subg, G.sems[5])
        P.dve(_C("memset", mhalf[:], -0.5))
        tk = None
        for j in range(2):
            tk = P.dve(_C("tensor_tensor", out=lamp[:, 2 * j, :], in0=lamp[:, 2 * j, :],
                                                      in1=lamp[:, 2 * j + 1, :], op=ALU.mult), waits=[tl], sig=True)
            tk = P.dve(_C("tensor_reduce", out=lsm[:, j:j + 1], in_=lamp[:, 2 * j, :], axis=AX.X, op=ALU.add),
                       waits=[tk], sig=True)
        tk = P.act(_C("activation", out=lsm[:, 2:4], in_=lsm[:, 0:2], func=AF.Exp), waits=[tk], sig=True)
        tk = P.dve(_C("tensor_tensor", out=lsm[:, 4:5], in0=lsm[:, 3:4], in1=lsm[:, 2:3], op=ALU.subtract),
                   waits=[tk], sig=True)
        tk = P.dve(_C("tensor_scalar", out=lsm[:, 5:6], in0=lsm[:, 4:5], scalar1=-LAMBDA_INIT, scalar2=None,
                                             op0=ALU.add), waits=[tk], sig=True)
        nlam_ready = tk
        tgs = P.dve(_C("tensor_scalar", out=gsub[:], in0=gsub[:], scalar1=1.0 - LAMBDA_INIT, scalar2=None,
                                              op0=ALU.mult), waits=[tg], sig=True)
        for k in range(2):
            P.dve(_C("memset", Vs[k][:, :, 128:132], 1.0))
        P.barrier()
        sbank = [(G.PS[0], G.PS[1]), (G.PS[2], G.PS[3])]
        accb = [G.PS[4], G.PS[5], G.PS[6]]
        tpbank = G.PS[7]
        acc_loc = [(accb[a // 3], (a % 3) * 132) for a in range(8)]
        qblocks = [(0, CTX, 2)] + [(CTX + 512 * j, 512, NKC) for j in range(G.SLAT // 512)]
        h_free = [None, None]
        pb_free = [None] * NPB
        npb = 0
        ao_free = [None, None]
        nao = 0
        acc_free = None
        eo_free = [None, None]
        nep = 0

        def load_head(h):
            k = h % 2
            P.dma("gpsimd", KTs[k][:], S.KT[h], hsem[k], waits=[h_free[k]])
            P.dma("gpsimd", QTs[k][:], S.QT[h], hsem[k])
            return P.dma("gpsimd", Vs[k][:, :, 0:128],
                         S.V[:, h * 128:(h + 1) * 128].rearrange("(c p) v -> p c v", p=128), hsem[k])

        def part2(h_, q0_, QN_, QS_, slot, t_on):
            nonlocal nao
            ao = aost[nao % 2]
            tt = None
            for qs in range(QS_):
                tpv = tpbank.t[:, qs * 64:(qs + 1) * 64].bitcast(BF16)
                tt = P.pe(_C("transpose", out=tpv, in_=onb8[slot][qs][:], identity=G.identB[:]),
                          waits=[t_on[qs], tpbank.free] if qs == 0 else [t_on[qs]], sig=(qs == QS_ - 1))
            tcp = P.dve(_C("tensor_scalar", out=ao[:, 0:QN_], in0=tpbank.t[:, 0:QS_ * 64].bitcast(BF16),
                           scalar1=gsub[:, 0:1], scalar2=None, op0=ALU.mult),
                        waits=[tt, tgs, ao_free[nao % 2]], sig=True)
            onb_free[slot] = tt
            tpbank.free = tcp
            ao_free[nao % 2] = P.dma("gpsimd", S.AOT[h_][:, q0_:q0_ + QN_], ao[:, 0:QN_], aosem[nao % 2], waits=[tcp])
            nao += 1

        hl = load_head(0)
        for h in range(NH):
            k = h % 2
            hl_next = load_head(h + 1) if h + 1 < NH else None
            KT, QT, V = KTs[k], QTs[k], Vs[k]
            last_pe = None
            for (q0, QN, nkc) in qblocks:
                QS = QN // 128
                pend = None

                def pv(kc, pi, te):
                    nonlocal last_pe
                    tm = None
                    started = set()
                    for e_ in range(2):
                        for qs in range(QS):
                            bank, col = acc_loc[e_ * 4 + qs]
                            first = (kc == 0 and id(bank) not in started)
                            started.add(id(bank))
                            tm = P.pe(_C("matmul",
                                bank.t[:, col:col + 129], lhsT=Pb[pi][:, e_, qs * 128:(qs + 1) * 128],
                                rhs=V[:, kc, 0:129], start=first, stop=(kc == nkc - 1), skip_group_check=True),
                                waits=[te, acc_free] if (kc == 0 and e_ == 0 and qs == 0) else [te],
                                sig=(e_ == 1 and qs == QS - 1))
                    pb_free[pi] = tm
                    last_pe = tm

                for kc in range(nkc):
                    pp = kc % 2
                    tsc = None
                    for e_ in range(2):
                        bank = sbank[pp][e_]
                        tsc = P.pe(_C("matmul",
                            bank.t[:, 0:QN], lhsT=KT[e_ * 64:(e_ + 1) * 64, kc * 128:(kc + 1) * 128],
                            rhs=QT[e_ * 64:(e_ + 1) * 64, q0:q0 + QN], start=True, stop=True),
                            waits=[hl, bank.free], sig=(e_ == 1))
                    pi = npb % NPB
                    npb += 1
                    te = None
                    for e_ in range(2):
                        bank = sbank[pp][e_]
                        te = P.act(_C("activation",
                            out=Pb[pi][:, e_, 0:QN], in_=bank.t[:, 0:QN], func=AF.Exp, scale=0.125),
                            waits=[tsc, pb_free[pi]] if e_ == 0 else (), sig=True)
                        bank.free = te
                    if pend is not None:
                        pv(*pend)
                    pend = (kc, pi, te)
                    if kc == min(3, nkc - 1) and dq:
                        part2(*dq.pop(0))
                pv(*pend)
                reads = []
                slot = nacc % 2
                asb = accs[slot]
                tev = None
                for bi_ in range(3):
                    tev = P.dve(_C("tensor_copy", out=asb[:, bi_ * 396:(bi_ + 1) * 396], in_=accb[bi_].t[:, 0:396]),
                                waits=[last_pe, accs_free[slot]] if bi_ == 0 else (), sig=(bi_ == 2))
                reads.append(tev)
                last_acc = tev
                t_on = []
                last_to = None
                for qs in range(QS):
                    b0, c0 = asb, qs * 132
                    b1, c1 = asb, (4 + qs) * 132
                    j = nep % 2
                    nep += 1
                    sc = ep[:, qs, :]
                    t_ = P.dve(_C("reciprocal", out=sc[:, 0:1], in_=b0[:, c0 + 128:c0 + 129]),
                               waits=[last_acc], sig=True)
                    t_ = P.dve(_C("reciprocal", out=sc[:, 1:2], in_=b1[:, c1 + 128:c1 + 129]), waits=[t_], sig=True)
                    t_ = P.dve(_C("tensor_tensor", out=sc[:, 2:3], in0=sc[:, 1:2], in1=lsm[:, 5:6], op=ALU.mult),
                               waits=[t_, nlam_ready], sig=True)
                    t_ = P.dve(_C("tensor_scalar", out=tmpo[j][:], in0=b1[:, c1:c1 + 128], scalar1=sc[:, 2:3],
                                  scalar2=None, op0=ALU.mult), waits=[t_], sig=True)
                    to = P.dve(_C("scalar_tensor_tensor", out=obuf[j][:], in0=b0[:, c0:c0 + 128], scalar=sc[:, 0:1],
                                  in1=tmpo[j][:], op0=ALU.mult, op1=ALU.add), waits=[t_], sig=True)
                    last_to = to
                    t_ = P.dve(_C("tensor_tensor", out=sqb[j][:], in0=obuf[j][:], in1=obuf[j][:], op=ALU.mult),
                               waits=[to], sig=True)
                    t_ = P.dve(_C("tensor_reduce", out=sc[:, 3:4], in_=sqb[j][:], axis=AX.X, op=ALU.add),
                               waits=[t_], sig=True)
                    t_ = P.dve(_C("tensor_scalar", out=sc[:, 4:5], in0=sc[:, 3:4], scalar1=1.0 / 128.0,
                                  scalar2=LN_EPS, op0=ALU.mult, op1=ALU.add), waits=[t_], sig=True)
                    t_ = P.pool(_C("tensor_tensor", out=sc[:, 5:6], in0=sc[:, 4:5], in1=mhalf[:], op=ALU.pow),
                                waits=[t_], sig=True)
                    t_ = P.dve(_C("tensor_scalar", out=onb8[slot][qs][:], in0=obuf[j][:], scalar1=sc[:, 5:6],
                                  scalar2=None, op0=ALU.mult), waits=[t_, onb_free[slot]], sig=True)
                    t_on.append(t_)
                acc_free = reads
                accs_free[slot] = last_to
                nacc += 1
                dq.append((h, q0, QN, QS, slot, t_on))
            h_free[k] = last_pe
            hl = hl_next
        while dq:
            part2(*dq.pop(0))
        P.barrier()
        P.emit()

    with ExitStack() as es:
        sb = mk_sb(nc, es)
        B = alloc_common(G, es)
        aT = [sb(f"aT{k}", [128, NH, 512], BF16) for k in range(2)]
        asem = [G.sems[4], G.sems[5]]
        a_free = [None, None]
        load_ln(G, B, i * 3 + sl)
        for (t0, T, w) in tiles:
            ws.push([S.wo[m] for m in range(16)])
        prev = None
        for ti, (t0, T, w) in enumerate(tiles):
            k = ti % 2
            ta = P.dma("gpsimd", aT[k][:, :, 0:T], S.AOT[:, :, t0:t0 + T].rearrange("h p t -> p h t"), asem[k],
                       waits=[a_free[k]])
            st = OutProj(G, B, NH, X_in, X_out, t0, T, modcol(i, sl, 2, 0), w, t0)
            if prev is not None:
                prev.end()
            st.begin()
            tm = None
            for m in range(16):
                s, buf, lt = ws.next()
                tm = st.slab(m, buf, lt, aT[k], ta)
                ws.release(s, tm)
            a_free[k] = tm
            prev = st
        prev.end()
        P.barrier()
        P.emit()


def phase_lru(G, X_in, X_out):
    nc, P, I, S = G.nc, G.P, G.I, G.S
    NT, tiles, lat_tiles = G.NT, G.tiles, G.lat_tiles
    ws = G.ws
    i, sl = 1, 1
    SL = G.SLAT
    with ExitStack() as es:
        sb = mk_sb(nc, es)
        B = Ctx()
        B.xin = sb("xin", [128, 4, D], F32)
        B.uT = sb("uT", [128, DC, 512], BF16)
        B.xin_sem = G.sems[0]
        B.xin_free = None
        B.uT_free = None
        B.tpb = [G.PS[6], G.PS[7]]
        for b in G.PS:
            b.free = None
        NST = 4
        ga = [sb(f"ga{k}", [128, 512], F32) for k in range(2)]
        gb_ = [sb(f"gb{k}", [128, 512], F32) for k in range(2)]
        gst = [sb(f"gst{k}", [128, 512], BF16) for k in range(NST)]
        xst = [sb(f"xst{k}", [128, 512], F32) for k in range(NST)]
        gsem = [G.sems[1 + k] for k in range(NST)]
        xsem = [G.sems[5 + k] for k in range(NST)]
        g_free = [None] * NST
        x_free = [None] * NST
        gab_free = [None, None]
        pb = [G.PS[0], G.PS[1], G.PS[2], G.PS[3]]
        for (t0, T, w) in tiles:
            ws.push([S.win[m] for m in range(40)])
        ng = 0
        nx = 0
        for (t0, T, w) in tiles:
            prologue(G, B, X_in, t0, T, i, sl, w)
            tm = None
            for m in range(40):
                s, buf, lt = ws.next()
                bank = pb[m % 4]
                for c in range(DC):
                    tm = P.pe(_C("matmul",
                        bank.t[:, 0:T], lhsT=buf[:, c * 128:(c + 1) * 128], rhs=B.uT[:, c, 0:T],
                        start=(c == 0), stop=(c == DC - 1)),
                        waits=[lt, B.uT_ready, bank.free] if c == 0 else (), sig=(c == DC - 1))
                ws.release(s, tm)
                if m < RC:
                    j = ng % 2
                    k_ = ng % NST
                    ng += 1
                    t_ = P.act(_C("activation", out=ga[j][:, 0:T], in_=bank.t[:, 0:T], func=AF.Square),
                               waits=[tm, gab_free[j]], sig=True)
                    P.dve(_C("tensor_scalar", out=ga[j][:, 0:T], in0=ga[j][:, 0:T], scalar1=0.044715,
                                                         scalar2=1.0, op0=ALU.mult, op1=ALU.add), waits=[t_])
                    t_ = P.dve(_C("tensor_tensor", out=ga[j][:, 0:T], in0=ga[j][:, 0:T],
                                                                         in1=bank.t[:, 0:T], op=ALU.mult), sig=True)
                    t_ = P.act(_C("activation", out=gb_[j][:, 0:T], in_=ga[j][:, 0:T], func=AF.Sigmoid,
                                                           scale=1.5957691216057308), waits=[t_], sig=True)
                    t_ = P.dve(_C("tensor_tensor",
                        out=gst[k_][:, 0:T], in0=gb_[j][:, 0:T], in1=bank.t[:, 0:T], op=ALU.mult),
                        waits=[t_, g_free[k_]], sig=True)
                    bank.free = t_
                    gab_free[j] = t_
                    g_free[k_] = P.dma("gpsimd", S.GT[m][:, t0:t0 + T], gst[k_][:, 0:T], gsem[k_], waits=[t_])
                else:
                    k_ = nx % NST
                    nx += 1
                    t_ = P.act(_C("activation", out=xst[k_][:, 0:T], in_=bank.t[:, 0:T], func=AF.Copy),
                               waits=[tm, x_free[k_]], sig=True)
                    bank.free = t_
                    x_free[k_] = P.dma("gpsimd", S.XRT[m - RC][:, t0:t0 + T], xst[k_][:, 0:T], xsem[k_], waits=[t_])
            B.uT_free = tm
        P.barrier()
        P.emit()

    with ExitStack() as es:
        sb = mk_sb(nc, es)
        for b in G.PS:
            b.free = None
        HX = sb("HX", [128, NT], F32)
        xr = sb("xr", [128, 2, NT], F32)
        xrb = sb("xrb", [128, 2, NT], BF16)
        R = sb("R", [128, NT], F32)
        IG = sb("IG", [128, NT], F32)
        MM = sb("MM", [128, NT], F32)
        hsum = sb("hsum", [128, SL], F32)
        Gs = sb("Gs", [128, SL], BF16)
        hgo = sb("hgo", [128, SL], BF16)
        convw = sb("convw", [128, RC, 4], F32)
        lvec = sb("lvec", [128, 6, RC], F32)
        apar = sb("apar", [128, 2, RC], F32)
        cvec = sb("cvec", [128, 2, 2, RC], F32)
        tl1 = P.dma("gpsimd", convw[:], I.convw, G.sems[0])
        tl2 = P.dma("gpsimd", lvec[:], I.lvec, G.sems[0])
        tl3 = P.dma("gpsimd", apar[:], I.apar, G.sems[0])
        t_ = P.act(_C("activation", out=cvec[:, :, 0, :], in_=apar[:], func=AF.Exp, scale=-1.0), waits=[tl3], sig=True)
        t_ = P.act(_C("activation", out=cvec[:, :, 0, :], in_=cvec[:, :, 0, :], func=AF.Ln, bias=1.0),
                   waits=[t_], sig=True)
        t_ = P.dve(_C("tensor_scalar", out=cvec[:, :, 0, :], in0=cvec[:, :, 0, :], scalar1=-8.0, scalar2=None,
                                             op0=ALU.mult), waits=[t_], sig=True)
        c_ready = P.dve(_C("tensor_scalar", out=cvec[:, :, 1, :], in0=cvec[:, :, 0, :], scalar1=2.0, scalar2=None,
                                                  op0=ALU.mult), waits=[t_], sig=True)
        xsem, gsem_, osem = G.sems[1], G.sems[2], G.sems[3]
        ws.push([S.wgate[hb] for hb in range(10)])
        tblocks = [(t, min(512, NT - t)) for t in range(0, NT, 512)]
        segs = [(0, CTX), (CTX, NT)]
        gbanks = [G.PS[0], G.PS[1], G.PS[2], G.PS[3]]
        ngb = 0
        hx_free = None
        xr_free = None
        xrb_free = None
        out_free = None
        g_free = None
        for hb in range(10):
            s, buf, lt = ws.next()
            tcv = None
            tcb = None
            for ic in range(2):
                ch = hb * 2 + ic
                tx = P.dma("gpsimd", HX[:], S.XRT[ch], xsem, waits=[hx_free])
                for (a_, b_) in segs:
                    P.dve(_C("tensor_scalar",
                        out=xr[:, ic, a_:b_], in0=HX[:, a_:b_], scalar1=convw[:, ch, 2:3], scalar2=lvec[:, 0, ch:ch + 1],
                        op0=ALU.mult, op1=ALU.add), waits=[tx, tl1, tl2, xr_free])
                    for (j_, d_) in ((1, 1), (0, 2)):
                        P.dve(_C("scalar_tensor_tensor",
                            out=xr[:, ic, a_ + d_:b_], in0=HX[:, a_:b_ - d_], scalar=convw[:, ch, j_:j_ + 1],
                            in1=xr[:, ic, a_ + d_:b_], op0=ALU.mult, op1=ALU.add))
                    tcv = P.dve(_C("scalar_tensor_tensor",
                        out=xr[:, ic, a_:b_ - 1], in0=HX[:, a_ + 1:b_], scalar=convw[:, ch, 3:4],
                        in1=xr[:, ic, a_:b_ - 1], op0=ALU.mult, op1=ALU.add), sig=True)
                hx_free = tcv
                tcb = P.act(_C("activation", out=xrb[:, ic, :], in_=xr[:, ic, :], func=AF.Copy),
                            waits=[tcv, xrb_free], sig=True)
            last_use = None
            for oc in range(2):
                ch = hb * 2 + oc
                tg_ = P.dma("gpsimd", Gs[:], S.GT[ch][:, CTX:NT], gsem_, waits=[g_free])
                for d in range(2):
                    tsr = None
                    tsi = None
                    for (tb0, tbn) in tblocks:
                        for gt in range(2):
                            bank = gbanks[ngb % 4]
                            ngb += 1
                            tm = None
                            for ic in range(2):
                                col = (((d * 2 + gt) * 2 + ic) * 2 + oc) * 128
                                tm = P.pe(_C("matmul",
                                    bank.t[:, 0:tbn], lhsT=buf[:, col:col + 128], rhs=xrb[:, ic, tb0:tb0 + tbn],
                                    start=(ic == 0), stop=(ic == 1)),
                                    waits=[lt, tcb, bank.free] if ic == 0 else (), sig=(ic == 1))
                            dst = R if gt == 0 else IG
                            bcol = (1 + d) if gt == 0 else (3 + d)
                            ta = P.act(_C("activation",
                                out=dst[:, tb0:tb0 + tbn], in_=bank.t[:, 0:tbn], func=AF.Sigmoid,
                                bias=lvec[:, bcol, ch:ch + 1]), waits=[tm, tl2, last_use], sig=True)
                            bank.free = ta
                            if gt == 0:
                                tsr = ta
                            else:
                                tsi = ta
                    last_gate_pe = tm
                    P.act(_C("activation", out=MM[:], in_=R[:], func=AF.Exp,
                                                            scale=cvec[:, d, 1, ch:ch + 1]), waits=[c_ready, last_use])
                    P.act(_C("activation", out=MM[:], in_=MM[:], func=AF.Sqrt, scale=-1.0, bias=1.0))
                    ta_ = P.act(_C("activation", out=R[:], in_=R[:], func=AF.Exp,
                                                                  scale=cvec[:, d, 0, ch:ch + 1]), sig=True)
                    P.dve(_C("tensor_tensor", out=IG[:], in0=IG[:], in1=MM[:], op=ALU.mult), waits=[ta_, tsi])
                    tb_ = P.dve(_C("tensor_tensor", out=IG[:], in0=IG[:], in1=xr[:, oc, :], op=ALU.mult), sig=True)
                    if d == 0:
                        P.dve(_C("tensor_tensor_scan", out=HX[:], data0=R[:], data1=IG[:], initial=0.0,
                                                             op0=ALU.mult, op1=ALU.add), waits=[tb_, hx_free])
                        tsc = P.dve(_C("tensor_copy", out=hsum[:], in_=HX[:, CTX:NT]), waits=[out_free], sig=True)
                    else:
                        t1_ = P.dve(_C("tensor_tensor_scan",
                            out=HX[:, 0:CTX][:, ::-1], data0=R[:, 0:CTX][:, ::-1], data1=IG[:, 0:CTX][:, ::-1],
                            initial=0.0, op0=ALU.mult, op1=ALU.add), waits=[tb_], sig=True)
                        t2_ = P.dve(_C("scalar_tensor_tensor",
                            out=IG[:, NT - 1:NT], in0=R[:, NT - 1:NT], scalar=HX[:, 0:1], in1=IG[:, NT - 1:NT],
                            op0=ALU.mult, op1=ALU.add), waits=[t1_], sig=True)
                        t3_ = P.dve(_C("tensor_tensor_scan",
                            out=HX[:, CTX:NT][:, ::-1], data0=R[:, CTX:NT][:, ::-1], data1=IG[:, CTX:NT][:, ::-1],
                            initial=0.0, op0=ALU.mult, op1=ALU.add), waits=[t2_], sig=True)
                        tsc = P.dve(_C("tensor_tensor", out=hsum[:], in0=hsum[:], in1=HX[:, CTX:NT], op=ALU.add),
                                    waits=[t3_], sig=True)
                    last_use = tsc
                    hx_free = tsc
                to_ = P.pool(_C("tensor_tensor", out=hgo[:], in0=hsum[:], in1=Gs[:], op=ALU.mult),
                             waits=[tsc, tg_, out_free], sig=True)
                g_free = to_
                out_free = P.dma("gpsimd", S.HGT[ch][:, CTX:NT], hgo[:], osem, waits=[to_])
            ws.release(s, last_gate_pe)
            xr_free = last_use
            xrb_free = last_gate_pe
        P.barrier()
        P.emit()

    with ExitStack() as es:
        sb = mk_sb(nc, es)
        B = alloc_common(G, es)
        aT = [sb(f"hT{k}", [128, RC, 512], BF16) for k in range(2)]
        asem = [G.sems[4], G.sems[5]]
        a_free = [None, None]
        load_ln(G, B, i * 3 + sl)
        for (t0, T, w) in lat_tiles:
            ws.push([S.wout[m] for m in range(16)])
        prev = None
        for ti, (t0, T, w) in enumerate(lat_tiles):
            k = ti % 2
            ta = P.dma("gpsimd", aT[k][:, :, 0:T], S.HGT[:, :, t0:t0 + T].rearrange("h p t -> p h t"), asem[k],
                       waits=[a_free[k]])
            st = OutProj(G, B, RC, X_in, X_out, t0, T, modcol(i, sl, 2, 0), w, t0)
            if prev is not None:
                prev.end()
            st.begin()
            tm = None
            for m in range(16):
                s, buf, lt = ws.next()
                tm = st.slab(m, buf, lt, aT[k], ta)
                ws.release(s, tm)
            a_free[k] = tm
            prev = st
        prev.end()
        P.barrier()
        P.emit()


def _tile_w(W, KC, MC):
    return np.ascontiguousarray(W.reshape(KC, 128, MC, 128).transpose(2, 1, 0, 3)).reshape(MC, 128, KC * 128)


def _fm(v, n):
    return np.ascontiguousarray(np.asarray(v).reshape(n, 128).T)


def rope_tables(S_LAT):
    GRID_W = 64
    t = np.arange(S_LAT)
    row = (t // GRID_W).astype(np.float32)
    col = (t % GRID_W).astype(np.float32)
    inv = (np.float32(10000.0) ** (-np.arange(0, 32, 2, dtype=np.float32) / np.float32(32))).astype(np.float32)
    cos = np.ones((128, CTX + S_LAT), np.float32)
    sin = np.zeros((128, CTX + S_LAT), np.float32)
    for e in range(2):
        for d in range(64):
            pos = row if d < 32 else col
            dd = d % 32
            j = dd % 16
            ang = (pos * inv[j]).astype(np.float32)
            cos[e * 64 + d, CTX:] = np.cos(ang)
            sgn = -1.0 if dd < 16 else 1.0
            sin[e * 64 + d, CTX:] = sgn * np.sin(ang)
    return cos, sin


def _perm_cols():
    p = np.zeros(128, np.int64)
    for e in range(2):
        for d in range(64):
            dd = d % 32
            partner = d + 16 if dd < 16 else d - 16
            p[e * 64 + d] = e * 64 + partner
    return p


def prep_shared(inp, S_LAT):
    f = lambda a: np.asarray(a, dtype=np.float32)
    sh = {}
    wada = f(inp["w_ada"])
    sh["wada"] = np.ascontiguousarray(wada.reshape(2, 16, 128, 36, 512).transpose(0, 3, 2, 1, 4)).reshape(2, 36, 128, 16 * 512)
    bada = f(inp["b_ada"])
    sh["bada"] = np.ascontiguousarray(bada.reshape(2, 144, 128).transpose(2, 0, 1))
    sh["lng"] = f(inp["ln_g"]).reshape(6, D)
    sh["lnb"] = f(inp["ln_b"]).reshape(6, D)
    wg, wu, wd = f(inp["ffn_w_gate"]), f(inp["ffn_w_up"]), f(inp["ffn_w_down"])
    wgu = np.empty((4, FC, 128, 4096), np.float32)
    wdn = np.empty((4, DC, 128, FC * 128), np.float32)
    for i in range(2):
        for j in range(2):
            g = _tile_w(wg[i, j], DC, FC)
            u = _tile_w(wu[i, j], DC, FC)
            wgu[i * 2 + j, :, :, 0:2048] = g
            wgu[i * 2 + j, :, :, 2048:4096] = u
            wdn[i * 2 + j] = _tile_w(wd[i, j], FC, DC)
    sh["wgu"], sh["wdn"] = wgu, wdn
    wqkv = f(inp["attn_w_qkv"])[0]
    perm = _perm_cols()
    wqk = np.empty((32, 128, 4096), np.float32)
    for h in range(NH):
        for t_, off in ((0, 0), (1, D)):
            Wh = wqkv[:, off + h * 128: off + (h + 1) * 128]
            wqk[h * 2 + t_, :, 0:2048] = _tile_w(Wh, DC, 1)[0]
            wqk[h * 2 + t_, :, 2048:4096] = _tile_w(Wh[:, perm], DC, 1)[0]
    sh["wqk"] = wqk
    Wv = wqkv[:, 2 * D:3 * D]
    sh["wv"] = np.ascontiguousarray(Wv.reshape(DC, 128, D).transpose(1, 0, 2)).reshape(128, 8, 4096).transpose(1, 0, 2).copy()
    sh["wo"] = _tile_w(f(inp["attn_w_o"])[0], DC, DC)
    sh["lam"] = np.stack([f(inp["attn_lambda_q1"])[0], f(inp["attn_lambda_k1"])[0],
                          f(inp["attn_lambda_q2"])[0], f(inp["attn_lambda_k2"])[0]])
    sh["subg"] = f(inp["attn_subln_g"])[0].reshape(128, 1).copy()
    cos, sin = rope_tables(S_LAT)
    sh["cos"], sh["sin"] = cos, sin
    sh["win"] = _tile_w(f(inp["lru_w_in"])[0], DC, 40)
    cw = f(inp["lru_conv_w"])[0]
    sh["convw"] = np.ascontiguousarray(cw.reshape(4, RC, 128).transpose(2, 1, 0))
    lv = np.empty((128, 6, RC), np.float32)
    lv[:, 0] = _fm(f(inp["lru_conv_b"])[0], RC)
    ba, bi = f(inp["lru_b_a"])[0], f(inp["lru_b_i"])[0]
    lv[:, 1] = _fm(ba[0], RC)
    lv[:, 2] = _fm(ba[1], RC)
    lv[:, 3] = _fm(bi[0], RC)
    lv[:, 4] = _fm(bi[1], RC)
    lv[:, 5] = 0.0
    sh["lvec"] = lv
    ap_ = f(inp["lru_a_param"])[0]
    sh["apar"] = np.stack([_fm(ap_[0], RC), _fm(ap_[1], RC)], axis=1).copy()
    wa, wi = f(inp["lru_w_a"])[0], f(inp["lru_w_i"])[0]
    wgate = np.empty((10, 128, 2048), np.float32)
    for hb in range(10):
        for d in range(2):
            for gt, Wt in ((0, wa), (1, wi)):
                for ic in range(2):
                    for oc in range(2):
                        col = (((d * 2 + gt) * 2 + ic) * 2 + oc) * 128
                        wgate[hb, :, col:col + 128] = Wt[d, hb, ic * 128:(ic + 1) * 128, oc * 128:(oc + 1) * 128]
    sh["wgate"] = wgate
    sh["wout"] = _tile_w(f(inp["lru_w_out"])[0], RC, DC)
    sh["eye"] = np.eye(128, dtype=np.float32)
    pm = np.zeros((128, 128), np.float32)
    pm[perm, np.arange(128)] = 1.0
    sh["permm"] = pm
    return sh


def prep_core(inp, b):
    f = lambda a: np.asarray(a, dtype=np.float32)
    d = {}
    d["x0"] = np.concatenate([f(inp["ctx"])[b], f(inp["x"])[b]], axis=0)
    cf = np.empty((128, 2, 16), np.float32)
    cf[:, 0] = _fm(f(inp["c"])[b], 16)
    cf[:, 1] = _fm(f(inp["c_ctx"]), 16)
    d["cfm"] = cf
    return d


_CACHE = {}


def kernel(**inputs):
    x = np.asarray(inputs["x"])
    Bn, S_LAT, _ = x.shape
    key = S_LAT
    if key not in _CACHE:
        _CACHE[key] = build(S_LAT)
    nc = _CACHE[key]
    sh = prep_shared(inputs, S_LAT)
    in_maps = []
    for b in range(Bn):
        m = dict(sh)
        m.update(prep_core(inputs, b))
        in_maps.append(m)
    res = run_bass_kernel_spmd(nc, in_maps, core_ids=list(range(Bn)))
    return np.stack([np.asarray(r["out"]) for r in res.results], axis=0).astype(np.float32)
```

```python
import math
from contextlib import ExitStack
import numpy as np
import concourse.bass as bass
import concourse.mybir as mybir
from concourse.bass_utils import run_bass_kernel_spmd

F32 = mybir.dt.float32
BF16 = mybir.dt.bfloat16
AF = mybir.ActivationFunctionType
ALU = mybir.AluOpType
AX = mybir.AxisListType

D = 2048
DC = 16
DFF = 5632
FC = 44
CTX = 256
DRNN = 2560
RC = 20
NH = 16
ALPHA = 2.0 ** 0.5
LN_EPS = 1e-6
LAMBDA_INIT = 0.8 - 0.6 * math.exp(-0.3 * 0)
WSLOT = 5632
NWBUF = 4
PIPE_QK = True
ENGS = ("tensor", "vector", "scalar", "gpsimd", "sync")


def _C(name, *a, **k):
    return (name, a, k)


class DSem:
    def __init__(self, h):
        self.h = h
        self.n = 0

    def tok(self):
        return (self.h, self.n)


class Op:
    __slots__ = ("fn", "ws", "inc", "dma")

    def __init__(self, fn, ws, inc, dma):
        self.fn, self.ws, self.inc, self.dma = fn, ws, inc, dma


class Prog:
    def __init__(self, nc):
        self.nc = nc
        self.ops = {e: [] for e in ENGS}
        self.esem = {e: nc.alloc_semaphore("es_" + e) for e in ENGS if e != "sync"}
        self.ecnt = {e: 0 for e in ENGS}
        self.waited = {e: {} for e in ENGS}
        self.dsems = []
        self.ninst = 0

    def new_sem(self, name=None):
        s = DSem(self.nc.alloc_semaphore(name or f"ds{len(self.dsems)}"))
        self.dsems.append(s)
        return s

    def _flat(self, toks, out):
        for t in toks:
            if t is None:
                continue
            if isinstance(t, list) or (isinstance(t, tuple) and (len(t) != 2 or isinstance(t[0], (tuple, list)) or t[0] is None)):
                self._flat(t, out)
            else:
                out.append(t)

    def _waits(self, eng, toks):
        fl = []
        self._flat(toks, fl)
        ws = []
        for sem, v in fl:
            if v <= 0:
                continue
            key = id(sem)
            if self.waited[eng].get(key, 0) >= v:
                continue
            self.waited[eng][key] = v
            ws.append((sem, v))
        return ws

    def op(self, eng, fn, waits=(), sig=False):
        ws = self._waits(eng, waits)
        tok = None
        inc = None
        if sig:
            self.ecnt[eng] += 1
            inc = (self.esem[eng], 1)
            tok = (self.esem[eng], self.ecnt[eng])
        self.ops[eng].append(Op(fn, ws, inc, False))
        self.ninst += 1
        return tok

    def pe(self, fn, waits=(), sig=False):
        return self.op("tensor", fn, waits, sig)

    def dve(self, fn, waits=(), sig=False):
        return self.op("vector", fn, waits, sig)

    def act(self, fn, waits=(), sig=False):
        return self.op("scalar", fn, waits, sig)

    def pool(self, fn, waits=(), sig=False):
        return self.op("gpsimd", fn, waits, sig)

    def dma(self, eng, out, in_, dsem, waits=()):
        ws = self._waits(eng, waits)
        dsem.n += 16
        self.ops[eng].append(Op(lambda e, o=out, i=in_: e.dma_start(out=o, in_=i), ws, (dsem.h, 16), True))
        self.ninst += 1
        return (dsem.h, dsem.n)

    def wait_only(self, eng, waits):
        ws = self._waits(eng, waits)
        if ws:
            self.ops[eng].append(Op(None, ws, None, False))

    def barrier(self):
        toks = []
        for e in ("tensor", "vector", "scalar", "gpsimd"):
            last = None
            for o in reversed(self.ops[e]):
                if o.fn is not None and not o.dma:
                    last = o
                    break
            if last is not None and last.inc is None:
                self.ecnt[e] += 1
                last.inc = (self.esem[e], 1)
            if self.ecnt[e] > 0:
                toks.append((self.esem[e], self.ecnt[e]))
        for s in self.dsems:
            if s.n > 0:
                toks.append((s.h, s.n))
        for e in ENGS:
            self.wait_only(e, toks)

    def emit(self):
        nc = self.nc
        ops = self.ops
        with nc.Block() as block:
            def body(ename):
                def f(e):
                    for o in ops[ename]:
                        for sem, v in o.ws:
                            e.wait_ge(sem, v)
                        if o.fn is None:
                            continue
                        ins = o.fn(e) if callable(o.fn) else getattr(e, o.fn[0])(*o.fn[1], **o.fn[2])
                        if o.inc is not None:
                            ins.then_inc(o.inc[0], o.inc[1])
                return f
            block.sync(body("sync"))
            block.tensor(body("tensor"))
            block.vector(body("vector"))
            block.scalar(body("scalar"))
            block.gpsimd(body("gpsimd"))
        self.ops = {e: [] for e in ENGS}


class Bank:
    def __init__(self, t):
        self.t = t
        self.free = None


class WStream:
    def __init__(self, P, nc):
        self.P = P
        self.slots = [nc.alloc_sbuf_tensor(f"wslot{i}", [128, WSLOT], BF16) for i in range(NWBUF)]
        self.sems = [P.new_sem(f"wsem{i}") for i in range(NWBUF)]
        self.free = [None] * NWBUF
        self.queue = []
        self.inflight = []
        self.gi = 0
        self.nrel = 0
        self.ready = None

    def push(self, srcs):
        self.queue.extend(srcs)

    def _issue(self):
        while self.queue and self.gi - self.nrel < NWBUF:
            src = self.queue.pop(0)
            s = self.gi % NWBUF
            self.gi += 1
            E = src.shape[1]
            tok = self.P.dma("sync", self.slots[s][:, 0:E], src, self.sems[s], waits=[self.free[s], self.ready])
            self.inflight.append((s, tok))

    def next(self):
        self._issue()
        s, tok = self.inflight.pop(0)
        return s, self.slots[s], tok

    def release(self, s, tok):
        self.free[s] = tok
        self.nrel += 1
        self._issue()


class Ctx:
    pass


_UID = [0]


def mk_sb(nc, es):
    _UID[0] += 1
    u = _UID[0]
    return lambda n, s, d: es.enter_context(nc.sbuf_tensor(f"{n}_u{u}", s, d))


def build(S_LAT=4096, dbg=False, phases=None):
    NT = CTX + S_LAT
    NKC = NT // 128
    nc = bass.Bass("TRN2", target_bir_lowering=False)
    P = Prog(nc)
    G = Ctx()
    G.nc, G.P, G.SLAT, G.NT, G.NKC = nc, P, S_LAT, NT, NKC
    tiles = [(0, CTX, 1)] + [(CTX + 512 * i, 512, 0) for i in range(S_LAT // 512)]
    lat_tiles = tiles[1:]
    G.tiles, G.lat_tiles = tiles, lat_tiles

    def din(name, shape, dt=F32):
        return nc.dram_tensor(name, list(shape), dt, kind="ExternalInput").ap()

    skind = "ExternalOutput" if dbg else "Internal"

    def dscr(name, shape, dt):
        return nc.dram_tensor(name, list(shape), dt, kind=skind).ap()

    I = Ctx()
    I.x0 = din("x0", [NT, D])
    I.cfm = din("cfm", [128, 2, 16])
    I.wada = din("wada", [2, 36, 128, 16 * 512])
    I.bada = din("bada", [128, 2, 144])
    I.lng = din("lng", [6, D])
    I.lnb = din("lnb", [6, D])
    I.wgu = din("wgu", [4, FC, 128, 4096])
    I.wdn = din("wdn", [4, DC, 128, FC * 128])
    I.wqk = din("wqk", [32, 128, 4096])
    I.wv = din("wv", [8, 128, 4096])
    I.wo = din("wo", [16, 128, 2048])
    I.lam = din("lam", [4, 64])
    I.subg = din("subg", [128, 1])
    I.cos = din("cos", [128, NT])
    I.sin = din("sin", [128, NT])
    I.win = din("win", [40, 128, 2048])
    I.convw = din("convw", [128, RC, 4])
    I.lvec = din("lvec", [128, 6, RC])
    I.apar = din("apar", [128, 2, RC])
    I.wgate = din("wgate", [10, 128, 2048])
    I.wout = din("wout", [16, 128, RC * 128])
    I.eye = din("eye", [128, 128])
    I.permm = din("permm", [128, 128])
    out = nc.dram_tensor("out", [S_LAT, D], F32, kind="ExternalOutput").ap()

    S = Ctx()
    S.XA = dscr("XA", [NT, D], F32)
    S.XB = dscr("XB", [NT, D], F32)
    S.wgu = dscr("s_wgu", [4, FC, 128, 4096], BF16)
    S.wdn = dscr("s_wdn", [4, DC, 128, FC * 128], BF16)
    S.wqk = dscr("s_wqk", [32, 128, 4096], BF16)
    S.wv = dscr("s_wv", [8, 128, 4096], BF16)
    S.wo = dscr("s_wo", [16, 128, 2048], BF16)
    S.win = dscr("s_win", [40, 128, 2048], BF16)
    S.wgate = dscr("s_wgate", [10, 128, 2048], BF16)
    S.wout = dscr("s_wout", [16, 128, RC * 128], BF16)
    S.QT = dscr("QT", [NH, 128, NT], BF16)
    S.KT = dscr("KT", [NH, 128, NT], BF16)
    S.V = dscr("V", [NT, D], BF16)
    S.AOT = dscr("AOT", [NH, 128, NT], BF16)
    S.GT = dscr("GT", [RC, 128, NT], BF16)
    S.XRT = dscr("XRT", [RC, 128, NT], F32)
    S.HGT = dscr("HGT", [RC, 128, NT], BF16)
    G.I, G.S, G.out = I, S, out

    G.ws = WStream(P, nc)
    G.identF = nc.alloc_sbuf_tensor("identF", [128, 128], F32)
    G.identB = nc.alloc_sbuf_tensor("identB", [128, 128], BF16)
    G.MOD = nc.alloc_sbuf_tensor("MOD", [128, 288, 2], F32)
    G.PS = [Bank(nc.alloc_psum_tensor(f"bank{i}", [128, 512], F32)) for i in range(8)]
    G.sems = [P.new_sem(f"gs{i}") for i in range(24)]

    phases = phases or ["ada", "conv", "l0f0", "attn", "l0f2", "l1f0", "lru", "l1f2"]

    t = P.dma("sync", G.identF[:], I.eye, G.sems[0])
    P.dve(_C("tensor_copy", out=G.identB[:], in_=G.identF[:]), waits=[t])
    P.barrier()
    P.emit()

    G.conv = "conv" in phases
    if "ada" in phases:
        phase_ada(G, conv=G.conv)
    X0, XA, XB = I.x0, S.XA, S.XB
    if "l0f0" in phases:
        phase_ffn(G, 0, 0, X0, XA, tiles)
    if "attn" in phases:
        phase_attn(G, XA, XB)
    if "l0f2" in phases:
        phase_ffn(G, 0, 2, XB, XA, tiles)
    if "l1f0" in phases:
        phase_ffn(G, 1, 0, XA, XB, tiles)
    if "lru" in phases:
        phase_lru(G, XB, XA)
    if "l1f2" in phases:
        phase_ffn(G, 1, 2, XA, out, lat_tiles, out_off=CTX)
    P.barrier()
    P.emit()
    return nc


def modcol(i, sl, comp, c):
    return i * 144 + (sl * 3 + comp) * 16 + c


def phase_ada(G, conv=True):
    nc, P, I = G.nc, G.P, G.I
    with ExitStack() as es:
        sb = mk_sb(nc, es)
        cv = Converter(G, es, "early", 3, ("vector", "scalar")) if conv else None
        craw = sb("craw", [128, 2, 16], F32)
        cs = sb("cs", [128, 16, 2], F32)
        bada = sb("badas", [128, 2, 144], F32)
        slabs = [sb(f"adaslab{i}", [128, 16 * 512], F32) for i in range(2)]
        ssem = [G.sems[0], G.sems[1]]
        t0 = P.dma("sync", craw[:], I.cfm, G.sems[2])
        t1 = P.dma("sync", bada[:], I.bada, G.sems[3])
        ta = None
        for w in range(2):
            ta = P.act(_C("activation", out=cs[:, :, w], in_=craw[:, w, :], func=AF.Silu), waits=[t0], sig=True)
        sfree = [None, None]
        banks = [G.PS[0], G.PS[1]]
        n = 0
        for i in range(2):
            for cb in range(36):
                s = n % 2
                lt = P.dma("sync", slabs[s][:], I.wada[i, cb], ssem[s], waits=[sfree[s]])
                for q in range(4):
                    bank = banks[(n * 4 + q) % 2]
                    tm = None
                    for k in range(16):
                        tm = P.pe(_C("matmul",
                            bank.t[:, 0:2], lhsT=slabs[s][:, k * 512 + q * 128: k * 512 + (q + 1) * 128],
                            rhs=cs[:, k, :], start=(k == 0), stop=(k == 15)),
                            waits=[lt, ta, bank.free] if k == 0 else (), sig=(k == 15))
                    col = i * 144 + cb * 4 + q
                    tv = P.dve(_C("tensor_scalar",
                        out=G.MOD[:, col, :], in0=bank.t[:, 0:2], scalar1=bada[:, i, cb * 4 + q:cb * 4 + q + 1], scalar2=None,
                        op0=ALU.add), waits=[tm, t1], sig=True)
                    bank.free = tv
                sfree[s] = tm
                n += 1
                if cv is not None:
                    cv.step(2)
        if cv is not None:
            cv.step(10 ** 6)
        for i in range(2):
            for sl in range(3):
                c0 = modcol(i, sl, 1, 0)
                P.dve(_C("tensor_scalar", out=G.MOD[:, c0:c0 + 16, :], in0=G.MOD[:, c0:c0 + 16, :],
                                                       scalar1=1.0, scalar2=None, op0=ALU.add), waits=[tv])
                if sl != 1:
                    g0 = modcol(i, sl, 2, 0)
                    P.dve(_C("tensor_scalar", out=G.MOD[:, g0:g0 + 16, :], in0=G.MOD[:, g0:g0 + 16, :],
                                                           scalar1=0.5, scalar2=None, op0=ALU.mult))
        P.barrier()
        P.emit()


class Converter:
    def __init__(self, G, es, which, NB, cast_engs):
        self.G = G
        nc, P, I, S = G.nc, G.P, G.I, G.S
        jobs = []

        def add(src, dst, ecols=None):
            sh = src.shape
            lead = sh[:-2]
            E = ecols or sh[-1]
            idxs = [()]
            for n in lead:
                idxs = [ix + (j,) for ix in idxs for j in range(n)]
            for ix in idxs:
                s_ = src
                d_ = dst
                for j in ix:
                    s_ = s_[j]
                    d_ = d_[j]
                for e0 in range(0, E, 4096):
                    n_ = min(4096, E - e0)
                    jobs.append((s_[:, e0:e0 + n_], d_[:, e0:e0 + n_], n_))

        if which == "early":
            add(I.wgu[0], S.wgu[0])
            add(I.wdn[0], S.wdn[0])
            add(I.wqk, S.wqk, ecols=2048)
            add(I.wv, S.wv)
            add(I.wo, S.wo)
        else:
            for fi in range(1, 4):
                add(I.wgu[fi], S.wgu[fi])
                add(I.wdn[fi], S.wdn[fi])
            add(I.win, S.win)
            add(I.wgate, S.wgate)
            add(I.wout, S.wout)
        self.jobs = jobs
        self.j = 0
        self.NB = NB
        self.cast_engs = cast_engs
        sb = mk_sb(nc, es)
        self.fin = [sb(f"cvin{i}", [128, 4096], F32) for i in range(NB)]
        self.fout = [sb(f"cvout{i}", [128, 4096], BF16) for i in range(NB)]
        self.lsem = [G.sems[8 + i] for i in range(NB)]
        self.ssem = [G.sems[8 + NB + i] for i in range(NB)]
        self.infree = [None] * NB
        self.outfree = [None] * NB

    def step(self, n):
        P = self.G.P
        for _ in range(n):
            if self.j >= len(self.jobs):
                return
            j = self.j
            self.j += 1
            src, dst, n_ = self.jobs[j]
            s = j % self.NB
            fin, fout = self.fin[s], self.fout[s]
            lt = P.dma("sync", fin[:, 0:n_], src, self.lsem[s], waits=[self.infree[s]])
            eng = self.cast_engs[j % len(self.cast_engs)]
            if eng == "scalar":
                ct = P.act(_C("activation", out=fout[:, 0:n_], in_=fin[:, 0:n_], func=AF.Copy),
                           waits=[lt, self.outfree[s]], sig=True)
            else:
                ct = P.op(eng, _C("tensor_copy", out=fout[:, 0:n_], in_=fin[:, 0:n_]),
                          waits=[lt, self.outfree[s]], sig=True)
            self.infree[s] = ct
            self.outfree[s] = P.dma("gpsimd", dst, fout[:, 0:n_], self.ssem[s], waits=[ct])


def prologue(G, B, X_in, t0, T, i, sl, w):
    P = G.P
    S4 = T // 128
    tokx = P.dma("gpsimd", B.xin[:, 0:S4, :], X_in[t0:t0 + T, :].rearrange("(s p) d -> p s d", p=128),
                 B.xin_sem, waits=[B.xin_free])
    ev = None
    tp = None
    for c in range(DC):
        bank = B.tpb[c % 2]
        for s in range(S4):
            tp = P.pe(_C("transpose",
                out=bank.t[:, s * 128:(s + 1) * 128], in_=B.xin[:, s, c * 128:(c + 1) * 128], identity=G.identF[:]),
                waits=[tokx, bank.free] if s == 0 else (), sig=(s == S4 - 1))
        sc = modcol(i, sl, 1, c)
        sh = modcol(i, sl, 0, c)
        ev = P.dve(_C("tensor_scalar",
            out=B.uT[:, c, 0:T], in0=bank.t[:, 0:T], scalar1=G.MOD[:, sc, w:w + 1], scalar2=G.MOD[:, sh, w:w + 1],
            op0=ALU.mult, op1=ALU.add), waits=[tp, B.uT_free] if c == 0 else [tp], sig=True)
        bank.free = ev
    B.xin_free = tp
    B.uT_ready = ev


def layer_norm_rows(G, B, xs, s, lnw, after):
    P = G.P
    tk = None
    for q in range(4):
        tk = P.dve(_C("bn_stats", out=B.stats[:, s, q, :], in_=xs[:, q * 512:(q + 1) * 512]),
                   waits=[after] if q == 0 else (), sig=(q == 3))
    t1 = P.dve(_C("bn_aggr", out=B.mv[:, s, :], in_=B.stats[:, s, :, :].rearrange("p a b -> p (a b)")),
               waits=[tk], sig=True)
    t2 = P.dve(_C("tensor_scalar", out=B.sm[:, s, 0:1], in0=B.mv[:, s, 1:2], scalar1=LN_EPS, scalar2=None,
                                         op0=ALU.add), waits=[t1], sig=True)
    t3 = P.act(_C("activation", out=B.sm[:, s, 1:2], in_=B.sm[:, s, 0:1], func=AF.Sqrt), waits=[t2], sig=True)
    t4 = P.dve(_C("reciprocal", out=B.sm[:, s, 2:3], in_=B.sm[:, s, 1:2]), waits=[t3], sig=True)
    t5 = P.dve(_C("scalar_tensor_tensor", out=B.sm[:, s, 3:4], in0=B.mv[:, s, 0:1], scalar=-1.0,
                                                in1=B.sm[:, s, 2:3], op0=ALU.mult, op1=ALU.mult), waits=[t4], sig=True)
    t6 = P.act(_C("activation", out=xs, in_=xs, func=AF.Identity, scale=B.sm[:, s, 2:3], bias=B.sm[:, s, 3:4]),
               waits=[t5, after], sig=True)
    P.pool(_C("tensor_tensor", out=xs, in0=xs, in1=B.gbc[:], op=ALU.mult), waits=[t6, lnw])
    t8 = P.pool(_C("tensor_tensor", out=xs, in0=xs, in1=B.bbc[:], op=ALU.add), sig=True)
    return t8


class OutProj:
    def __init__(self, G, B, KC, X_in, X_out, t0, T, gcol0, w, out_row0):
        self.G, self.B, self.KC, self.X_in, self.X_out = G, B, KC, X_in, X_out
        self.t0, self.T, self.gcol0, self.w, self.out_row0 = t0, T, gcol0, w, out_row0
        self.pending = None
        self.last_stt = None

    def begin(self):
        G, B, P = self.G, self.B, self.G.P
        S4 = self.T // 128
        self.xl = P.dma("gpsimd", B.x_all[:, 0:S4, :],
                        self.X_in[self.t0:self.t0 + self.T, :].rearrange("(s p) d -> p s d", p=128),
                        B.xall_sem, waits=[B.xall_free])

    def slab(self, m, buf, ltok, actT, act_ready):
        G, B, P, T, KC = self.G, self.B, self.G.P, self.T, self.KC
        bank = B.dnb[m % 2]
        tm = None
        for k in range(KC):
            tm = P.pe(_C("matmul", bank.t[:, 0:T], lhsT=buf[:, k * 128:(k + 1) * 128],
                                                         rhs=actT[:, k, 0:T], start=(k == 0), stop=(k == KC - 1)),
                      waits=[ltok, act_ready, bank.free] if k == 0 else (), sig=(k == KC - 1))
        gyb = B.gy[m % 3]
        gc = self.gcol0 + m
        ta = P.act(_C("activation",
            out=gyb[:, 0:T], in_=bank.t[:, 0:T], func=AF.Copy, scale=G.MOD[:, gc, self.w:self.w + 1]),
            waits=[tm, B.gy_free[m % 3]], sig=True)
        bank.free = ta
        if self.pending is not None:
            self.transposes(*self.pending)
        self.pending = (m, gyb, ta)
        return tm

    def transposes(self, m, gyb, ta):
        G, B, P, T = self.G, self.B, self.G.P, self.T
        S4 = T // 128
        bank = B.tpb[m % 2]
        tp = None
        for s in range(S4):
            tp = P.pe(_C("transpose", out=bank.t[:, s * 128:(s + 1) * 128],
                                                            in_=gyb[:, s * 128:(s + 1) * 128], identity=G.identF[:]),
                      waits=[ta, bank.free] if s == 0 else (), sig=(s == S4 - 1))
        B.gy_free[m % 3] = tp
        xv = B.x_all[:, 0:S4, m * 128:(m + 1) * 128]
        tv = P.dve(_C("scalar_tensor_tensor",
            out=xv, in0=xv, scalar=ALPHA, in1=bank.t[:, 0:T].rearrange("p (s d) -> p s d", d=128),
            op0=ALU.mult, op1=ALU.add), waits=[tp, self.xl], sig=True)
        bank.free = tv
        self.last_stt = tv

    def end(self):
        G, B, P, T = self.G, self.B, self.G.P, self.T
        S4 = T // 128
        self.transposes(*self.pending)
        self.pending = None
        sts = []
        for s in range(S4):
            xs = B.x_all[:, s, :]
            t8 = layer_norm_rows(G, B, xs, s, B.ln_ready, self.last_stt)
            r0 = self.out_row0 + s * 128
            sts.append(P.dma("gpsimd", self.X_out[r0:r0 + 128, :], xs, B.st_sem, waits=[t8]))
        B.xall_free = sts


def alloc_common(G, es, with_h=True):
    nc = G.nc
    sb = mk_sb(nc, es)
    B = Ctx()
    B.xin = sb("xin", [128, 4, D], F32)
    B.uT = sb("uT", [128, DC, 512], BF16)
    B.x_all = sb("x_all", [128, 4, D], F32)
    B.gbc = sb("gbc", [128, D], F32)
    B.bbc = sb("bbc", [128, D], F32)
    B.gy = [sb(f"gy{i}", [128, 512], F32) for i in range(3)]
    B.stats = sb("stats", [128, 4, 4, 6], F32)
    B.mv = sb("mv", [128, 4, 2], F32)
    B.sm = sb("sm", [128, 4, 4], F32)
    B.gy_free = [None] * 3
    B.xin_sem, B.xall_sem, B.st_sem, B.ln_sem = G.sems[0], G.sems[1], G.sems[2], G.sems[3]
    B.xin_free = None
    B.xall_free = None
    B.uT_free = None
    B.uT_ready = None
    B.tpb = [G.PS[6], G.PS[7]]
    B.dnb = [G.PS[4], G.PS[5]]
    for b in G.PS:
        b.free = None
    return B


def load_ln(G, B, idx):
    P, I = G.P, G.I
    P.dma("gpsimd", B.gbc[:], I.lng[idx].partition_broadcast(128), B.ln_sem)
    B.ln_ready = P.dma("gpsimd", B.bbc[:], I.lnb[idx].partition_broadcast(128), B.ln_sem)


def phase_ffn(G, i, sl, X_in, X_out, tiles, out_off=0):
    nc, P, S = G.nc, G.P, G.S
    fi = i * 2 + (0 if sl == 0 else 1)
    ws = G.ws
    with ExitStack() as es:
        sb = mk_sb(nc, es)
        B = alloc_common(G, es)
        hT = sb("hT", [128, FC, 512], BF16)
        sg = [sb(f"sg{k}", [128, 512], F32) for k in range(2)]
        sg_free = [None, None]
        gub = [(G.PS[0], G.PS[1]), (G.PS[2], G.PS[3])]
        load_ln(G, B, i * 3 + sl)
        for (t0, T, w) in tiles:
            ws.push([S.wgu[fi, f] for f in range(FC)])
            ws.push([S.wdn[fi, m] for m in range(DC)])
        prev = None
        for ti, (t0, T, w) in enumerate(tiles):
            prologue(G, B, X_in, t0, T, i, sl, w)
            if prev is not None:
                prev.end()
                prev = None
            tlast = None
            for f in range(FC):
                s, buf, lt = ws.next()
                pp = f % 2
                toks = [None, None]
                for gu in range(2):
                    bank = gub[pp][gu]
                    for c in range(DC):
                        toks[gu] = P.pe(_C("matmul",
                            bank.t[:, 0:T], lhsT=buf[:, (gu * 16 + c) * 128:(gu * 16 + c + 1) * 128],
                            rhs=B.uT[:, c, 0:T], start=(c == 0), stop=(c == DC - 1)),
                            waits=[lt, B.uT_ready, bank.free] if c == 0 else (), sig=(c == DC - 1))
                ws.release(s, toks[1])
                gb, ub = gub[pp]
                ta = P.act(_C("activation", out=sg[pp][:, 0:T], in_=gb.t[:, 0:T], func=AF.Silu),
                           waits=[toks[0], sg_free[pp]], sig=True)
                gb.free = ta
                tv = P.dve(_C("tensor_tensor", out=hT[:, f, 0:T], in0=sg[pp][:, 0:T],
                                                                        in1=ub.t[:, 0:T], op=ALU.mult),
                           waits=[ta, toks[1]], sig=True)
                ub.free = tv
                sg_free[pp] = tv
                tlast = toks[1]
            B.uT_free = tlast
            h_ready = tv
            st = OutProj(G, B, FC, X_in, X_out, t0, T, modcol(i, sl, 2, 0), w, t0 - out_off)
            st.begin()
            for m in range(DC):
                s, buf, lt = ws.next()
                tm = st.slab(m, buf, lt, hT, h_ready)
                ws.release(s, tm)
            prev = st
        prev.end()
        P.barrier()
        P.emit()


def phase_attn(G, X_in, X_out):
    nc, P, I, S = G.nc, G.P, G.I, G.S
    NT, NKC, tiles = G.NT, G.NKC, G.tiles
    ws = G.ws
    i, sl = 0, 1
    with ExitStack() as es:
        sb = mk_sb(nc, es)
        B = Ctx()
        B.xin = sb("xin", [128, 4, D], F32)
        B.uT = sb("uT", [128, DC, 512], BF16)
        B.xin_sem = G.sems[0]
        B.xin_free = None
        B.uT_free = None
        B.tpb = [G.PS[6], G.PS[7]]
        for b in G.PS:
            b.free = None
        wv = sb("wv", [128, DC, D], BF16)
        cosb = [sb(f"cos{k}", [128, 512], F32) for k in range(2)]
        sinb = [sb(f"sin{k}", [128, 512], F32) for k in range(2)]
        cs_free = [None, None]
        t1 = [sb(f"rt1_{k}", [128, 512], F32) for k in range(2)]
        t2 = [sb(f"rt2_{k}", [128, 512], F32) for k in range(2)]
        qst = [sb(f"qst{k}", [128, 512], BF16) for k in range(4)]
        qsb = [sb(f"qsb{k}", [128, 512], BF16) for k in range(2)]
        qsb_free = [None, None]
        permF = sb("permF", [128, 128], F32)
        permT = sb("permT", [128, 128], BF16)
        tpl = P.dma("gpsimd", permF[:], I.permm, G.sems[9])
        perm_ready = P.dve(_C("tensor_copy", out=permT[:], in_=permF[:]), waits=[tpl], sig=True)
        vst = [sb(f"vst{k}", [128, D], BF16) for k in range(2)]
        twv = None
        for k in range(8):
            twv = P.dma("gpsimd", wv[:, 2 * k:2 * k + 2, :].rearrange("p a b -> p (a b)"), S.wv[k], G.sems[2])
        qsem = [G.sems[3 + k] for k in range(4)]
        vsem = [G.sems[7 + k] for k in range(2)]
        q_free = [None] * 4
        v_free = [None] * 2
        t_free = [None, None]
        qkb = [(G.PS[0], G.PS[1]), (G.PS[2], G.PS[3])]
        vb = [G.PS[4], G.PS[5]]
        for (t0, T, w) in tiles:
            ws.push([S.wqk[m][:, 0:2048] for m in range(32)])
        nq = 0
        nv = 0
        for ti, (t0, T, w) in enumerate(tiles):
            S4 = T // 128
            cos, sin = cosb[ti % 2], sinb[ti % 2]
            P.dma("gpsimd", cos[:, 0:T], I.cos[:, t0:t0 + T], G.sems[1], waits=[cs_free[ti % 2]])
            tc_ = ts_ = P.dma("gpsimd", sin[:, 0:T], I.sin[:, t0:t0 + T], G.sems[1])
            prologue(G, B, X_in, t0, T, i, sl, w)
            tlast = None
            pend = None

            def finish(m, pp, tq, tcopy):
                nonlocal nq, tlast
                b0, b1 = qkb[pp]
                tpm = P.pe(_C("matmul", b1.t[:, 0:T], lhsT=permT[:], rhs=qsb[pp][:, 0:T], start=True, stop=True),
                           waits=[tcopy, b1.free, perm_ready], sig=True)
                qsb_free[pp] = tpm
                tlast = tpm
                P.dve(_C("tensor_tensor", out=t1[pp][:, 0:T], in0=b0.t[:, 0:T], in1=cos[:, 0:T], op=ALU.mult),
                      waits=[tq, tcopy, tc_, t_free[pp]])
                td = P.dve(_C("tensor_tensor", out=t2[pp][:, 0:T], in0=b1.t[:, 0:T], in1=sin[:, 0:T], op=ALU.mult),
                           waits=[tpm, ts_], sig=True)
                b0.free = td
                b1.free = td
                qs_ = nq % 4
                tp_ = P.pool(_C("tensor_tensor", out=qst[qs_][:, 0:T], in0=t1[pp][:, 0:T], in1=t2[pp][:, 0:T],
                                op=ALU.add), waits=[td, q_free[qs_]], sig=True)
                t_free[pp] = tp_
                dst = (S.QT if m % 2 == 0 else S.KT)[m // 2][:, t0:t0 + T]
                q_free[qs_] = P.dma("gpsimd", dst, qst[qs_][:, 0:T], qsem[qs_], waits=[tp_])
                nq += 1
                cs_free[ti % 2] = td

            for m in range(32):
                s, buf, lt = ws.next()
                pp = m % 2
                b0, b1 = qkb[pp]
                tq = None
                for c in range(DC):
                    tq = P.pe(_C("matmul", b0.t[:, 0:T], lhsT=buf[:, c * 128:(c + 1) * 128],
                                 rhs=B.uT[:, c, 0:T], start=(c == 0), stop=(c == DC - 1)),
                              waits=[lt, B.uT_ready, b0.free] if c == 0 else (), sig=(c == DC - 1))
                ws.release(s, tq)
                tcopy = P.act(_C("activation", out=qsb[pp][:, 0:T], in_=b0.t[:, 0:T], func=AF.Copy),
                              waits=[tq, qsb_free[pp]], sig=True)
                if PIPE_QK:
                    if pend is not None:
                        finish(*pend)
                    pend = (m, pp, tq, tcopy)
                else:
                    finish(m, pp, tq, tcopy)
            if PIPE_QK:
                finish(*pend)
            for s in range(S4):
                vs_ = nv % 2
                tev = None
                for n_ in range(4):
                    bank = vb[n_ % 2]
                    tm = None
                    for c in range(DC):
                        tm = P.pe(_C("matmul",
                            bank.t[:, :], lhsT=B.uT[:, c, s * 128:(s + 1) * 128],
                            rhs=wv[:, c, n_ * 512:(n_ + 1) * 512], start=(c == 0), stop=(c == DC - 1)),
                            waits=[twv, B.uT_ready, bank.free] if c == 0 else (), sig=(c == DC - 1))
                    tlast = tm
                    tev = P.act(_C("activation",
                        out=vst[vs_][:, n_ * 512:(n_ + 1) * 512], in_=bank.t[:, :], func=AF.Copy),
                        waits=[tm, v_free[vs_]] if n_ == 0 else [tm], sig=True)
                    bank.free = tev
                r0 = t0 + s * 128
                v_free[vs_] = P.dma("gpsimd", S.V[r0:r0 + 128, :], vst[vs_][:], vsem[vs_], waits=[tev])
                nv += 1
            B.uT_free = tlast
        P.barrier()
        P.emit()

    import os
    if os.environ.get("STOP_AFTER_QKV"):
        return
    with ExitStack() as es:
        sb = mk_sb(nc, es)
        for b in G.PS:
            b.free = None
        KTs = [sb(f"KTs{k}", [128, NT], BF16) for k in range(2)]
        QTs = [sb(f"QTs{k}", [128, NT], BF16) for k in range(2)]
        Vs = [sb(f"Vs{k}", [128, NKC, 132], BF16) for k in range(2)]
        NPB = 6
        Pb = [sb(f"Pb{k}", [128, 2, 512], BF16) for k in range(NPB)]
        aost = [sb(f"aost{k}", [128, 512], BF16) for k in range(2)]
        lamp = sb("lamp", [128, 4, 64], F32)
        lsm = sb("lsm", [128, 8], F32)
        gsub = sb("gsub", [128, 1], F32)
        mhalf = sb("mhalf", [128, 1], F32)
        ep = sb("ep", [128, 4, 8], F32)
        accs = [sb(f"accs{k}", [128, 3 * 396], F32) for k in range(2)]
        cvl = Converter(G, es, "late", 2, ("vector", "gpsimd")) if G.conv else None
        nkit = 0
        accs_free = [None, None]
        nacc = 0
        tmpo = [sb(f"tmpo{k}", [128, 128], F32) for k in range(2)]
        obuf = [sb(f"obuf{k}", [128, 128], F32) for k in range(2)]
        sqb = [sb(f"sqb{k}", [128, 128], F32) for k in range(2)]
        onb8 = [[sb(f"onb{k}_{q}", [128, 128], BF16) for q in range(4)] for k in range(2)]
        onb_free = [None, None]
        dq = []
        hsem = [G.sems[0], G.sems[1]]
        aosem = [G.sems[2], G.sems[3]]
        tl = P.dma("gpsimd", lamp[:].rearrange("p a b -> p (a b)"),
                   I.lam.rearrange("a b -> (a b)").partition_broadcast(128), G.sems[4])
        tg = P.dma("gpsimd", gsub[:], I.subg, G.sems[5])
        P.dve(_C("memset", mhalf[:], -0.5))
        tk = None
        for j in range(2):
            tk = P.dve(_C("tensor_tensor", out=lamp[:, 2 * j, :], in0=lamp[:, 2 * j, :],
                                                      in1=lamp[:, 2 * j + 1, :], op=ALU.mult), waits=[tl], sig=True)
            tk = P.dve(_C("tensor_reduce", out=lsm[:, j:j + 1], in_=lamp[:, 2 * j, :], axis=AX.X, op=ALU.add),
                       waits=[tk], sig=True)
        tk = P.act(_C("activation", out=lsm[:, 2:4], in_=lsm[:, 0:2], func=AF.Exp), waits=[tk], sig=True)
        tk = P.dve(_C("tensor_tensor", out=lsm[:, 4:5], in0=lsm[:, 3:4], in1=lsm[:, 2:3], op=ALU.subtract),
                   waits=[tk], sig=True)
        tk = P.dve(_C("tensor_scalar", out=lsm[:, 5:6], in0=lsm[:, 4:5], scalar1=-LAMBDA_INIT, scalar2=None,
                                             op0=ALU.add), waits=[tk], sig=True)
        nlam_ready = tk
        tgs = P.dve(_C("tensor_scalar", out=gsub[:], in0=gsub[:], scalar1=1.0 - LAMBDA_INIT, scalar2=None,
                                              op0=ALU.mult), waits=[tg], sig=True)
        for k in range(2):
            P.dve(_C("memset", Vs[k][:, :, 128:132], 1.0))
        P.barrier()
        sbank = [(G.PS[0], G.PS[1]), (G.PS[2], G.PS[3])]
        accb = [G.PS[4], G.PS[5], G.PS[6]]
        tpbank = G.PS[7]
        acc_loc = [(accb[a // 3], (a % 3) * 132) for a in range(8)]
        qblocks = [(0, CTX, 2)] + [(CTX + 512 * j, 512, NKC) for j in range(G.SLAT // 512)]
        h_free = [None, None]
        pb_free = [None] * NPB
        npb = 0
        ao_free = [None, None]
        nao = 0
        acc_free = None
        eo_free = [None, None]
        nep = 0

        def load_head(h):
            k = h % 2
            P.dma("gpsimd", KTs[k][:], S.KT[h], hsem[k], waits=[h_free[k]])
            P.dma("gpsimd", QTs[k][:], S.QT[h], hsem[k])
            return P.dma("gpsimd", Vs[k][:, :, 0:128],
                         S.V[:, h * 128:(h + 1) * 128].rearrange("(c p) v -> p c v", p=128), hsem[k])

        def part2(h_, q0_, QN_, QS_, slot, t_on):
            nonlocal nao
            ao = aost[nao % 2]
            tt = None
            for qs in range(QS_):
                tpv = tpbank.t[:, qs * 64:(qs + 1) * 64].bitcast(BF16)
                tt = P.pe(_C("transpose", out=tpv, in_=onb8[slot][qs][:], identity=G.identB[:]),
                          waits=[t_on[qs], tpbank.free] if qs == 0 else [t_on[qs]], sig=(qs == QS_ - 1))
            tcp = P.dve(_C("tensor_scalar", out=ao[:, 0:QN_], in0=tpbank.t[:, 0:QS_ * 64].bitcast(BF16),
                           scalar1=gsub[:, 0:1], scalar2=None, op0=ALU.mult),
                        waits=[tt, tgs, ao_free[nao % 2]], sig=True)
            onb_free[slot] = tt
            tpbank.free = tcp
            ao_free[nao % 2] = P.dma("gpsimd", S.AOT[h_][:, q0_:q0_ + QN_], ao[:, 0:QN_], aosem[nao % 2], waits=[tcp])
            nao += 1

        hl = load_head(0)
        for h in range(NH):
            k = h % 2
            hl_next = load_head(h + 1) if h + 1 < NH else None
            KT, QT, V = KTs[k], QTs[k], Vs[k]
            last_pe = None
            for (q0, QN, nkc) in qblocks:
                QS = QN // 128
                pend = None

                def pv(kc, pi, te):
                    nonlocal last_pe
                    tm = None
                    started = set()
                    for e_ in range(2):
                        for qs in range(QS):
                            bank, col = acc_loc[e_ * 4 + qs]
                            first = (kc == 0 and id(bank) not in started)
                            started.add(id(bank))
                            tm = P.pe(_C("matmul",
                                bank.t[:, col:col + 129], lhsT=Pb[pi][:, e_, qs * 128:(qs + 1) * 128],
                                rhs=V[:, kc, 0:129], start=first, stop=(kc == nkc - 1), skip_group_check=True),
                                waits=[te, acc_free] if (kc == 0 and e_ == 0 and qs == 0) else [te],
                                sig=(e_ == 1 and qs == QS - 1))
                    pb_free[pi] = tm
                    last_pe = tm

                for kc in range(nkc):
                    pp = kc % 2
                    tsc = None
                    for e_ in range(2):
                        bank = sbank[pp][e_]
                        tsc = P.pe(_C("matmul",
                            bank.t[:, 0:QN], lhsT=KT[e_ * 64:(e_ + 1) * 64, kc * 128:(kc + 1) * 128],
                            rhs=QT[e_ * 64:(e_ + 1) * 64, q0:q0 + QN], start=True, stop=True),
                            waits=[hl, bank.free], sig=(e_ == 1))
                    pi = npb % NPB
                    npb += 1
                    te = None
                    for e_ in range(2):
                        bank = sbank[pp][e_]
                        te = P.act(_C("activation",
                            out=Pb[pi][:, e_, 0:QN], in_=bank.t[:, 0:QN], func=AF.Exp, scale=0.125),
                            waits=[tsc, pb_free[pi]] if e_ == 0 else (), sig=True)
                        bank.free = te
                    if pend is not None:
                        pv(*pend)
                    pend = (kc, pi, te)
                    if kc == min(20, nkc - 1) and dq:
                        part2(*dq.pop(0))
                    nkit += 1
                    if cvl is not None and nkit % 16 == 8:
                        cvl.step(1)
                pv(*pend)
                reads = []
                slot = nacc % 2
                asb = accs[slot]
                tev = None
                for bi_ in range(3):
                    tev = P.dve(_C("tensor_copy", out=asb[:, bi_ * 396:(bi_ + 1) * 396], in_=accb[bi_].t[:, 0:396]),
                                waits=[last_pe, accs_free[slot]] if bi_ == 0 else (), sig=(bi_ == 2))
                reads.append(tev)
                last_acc = tev
                t_on = []
                last_to = None
                for qs in range(QS):
                    b0, c0 = asb, qs * 132
                    b1, c1 = asb, (4 + qs) * 132
                    j = nep % 2
                    nep += 1
                    sc = ep[:, qs, :]
                    t_ = P.dve(_C("reciprocal", out=sc[:, 0:1], in_=b0[:, c0 + 128:c0 + 129]),
                               waits=[last_acc], sig=True)
                    t_ = P.dve(_C("reciprocal", out=sc[:, 1:2], in_=b1[:, c1 + 128:c1 + 129]), waits=[t_], sig=True)
                    t_ = P.dve(_C("tensor_tensor", out=sc[:, 2:3], in0=sc[:, 1:2], in1=lsm[:, 5:6], op=ALU.mult),
                               waits=[t_, nlam_ready], sig=True)
                    t_ = P.dve(_C("tensor_scalar", out=tmpo[j][:], in0=b1[:, c1:c1 + 128], scalar1=sc[:, 2:3],
                                  scalar2=None, op0=ALU.mult), waits=[t_], sig=True)
                    to = P.dve(_C("scalar_tensor_tensor", out=obuf[j][:], in0=b0[:, c0:c0 + 128], scalar=sc[:, 0:1],
                                  in1=tmpo[j][:], op0=ALU.mult, op1=ALU.add), waits=[t_], sig=True)
                    last_to = to
                    t_ = P.dve(_C("tensor_tensor", out=sqb[j][:], in0=obuf[j][:], in1=obuf[j][:], op=ALU.mult),
                               waits=[to], sig=True)
                    t_ = P.dve(_C("tensor_reduce", out=sc[:, 3:4], in_=sqb[j][:], axis=AX.X, op=ALU.add),
                               waits=[t_], sig=True)
                    t_ = P.dve(_C("tensor_scalar", out=sc[:, 4:5], in0=sc[:, 3:4], scalar1=1.0 / 128.0,
                                  scalar2=LN_EPS, op0=ALU.mult, op1=ALU.add), waits=[t_], sig=True)
                    t_ = P.pool(_C("tensor_tensor", out=sc[:, 5:6], in0=sc[:, 4:5], in1=mhalf[:], op=ALU.pow),
                                waits=[t_], sig=True)
                    t_ = P.dve(_C("tensor_scalar", out=onb8[slot][qs][:], in0=obuf[j][:], scalar1=sc[:, 5:6],
                                  scalar2=None, op0=ALU.mult), waits=[t_, onb_free[slot]], sig=True)
                    t_on.append(t_)
                acc_free = reads
                accs_free[slot] = last_to
                nacc += 1
                dq.append((h, q0, QN, QS, slot, t_on))
            h_free[k] = last_pe
            hl = hl_next
        while dq:
            part2(*dq.pop(0))
        if cvl is not None:
            cvl.step(10 ** 6)
        P.barrier()
        P.emit()

    with ExitStack() as es:
        sb = mk_sb(nc, es)
        B = alloc_common(G, es)
        aT = [sb(f"aT{k}", [128, NH, 512], BF16) for k in range(2)]
        asem = [G.sems[4], G.sems[5]]
        a_free = [None, None]
        load_ln(G, B, i * 3 + sl)
        for (t0, T, w) in tiles:
            ws.push([S.wo[m] for m in range(16)])
        prev = None
        for ti, (t0, T, w) in enumerate(tiles):
            k = ti % 2
            ta = P.dma("gpsimd", aT[k][:, :, 0:T], S.AOT[:, :, t0:t0 + T].rearrange("h p t -> p h t"), asem[k],
                       waits=[a_free[k]])
            st = OutProj(G, B, NH, X_in, X_out, t0, T, modcol(i, sl, 2, 0), w, t0)
            if prev is not None:
                prev.end()
            st.begin()
            tm = None
            for m in range(16):
                s, buf, lt = ws.next()
                tm = st.slab(m, buf, lt, aT[k], ta)
                ws.release(s, tm)
            a_free[k] = tm
            prev = st
        prev.end()
        P.barrier()
        P.emit()


def phase_lru(G, X_in, X_out):
    nc, P, I, S = G.nc, G.P, G.I, G.S
    NT, tiles, lat_tiles = G.NT, G.tiles, G.lat_tiles
    ws = G.ws
    i, sl = 1, 1
    SL = G.SLAT
    with ExitStack() as es:
        sb = mk_sb(nc, es)
        B = Ctx()
        B.xin = sb("xin", [128, 4, D], F32)
        B.uT = sb("uT", [128, DC, 512], BF16)
        B.xin_sem = G.sems[0]
        B.xin_free = None
        B.uT_free = None
        B.tpb = [G.PS[6], G.PS[7]]
        for b in G.PS:
            b.free = None
        NST = 4
        ga = [sb(f"ga{k}", [128, 512], F32) for k in range(2)]
        gb_ = [sb(f"gb{k}", [128, 512], F32) for k in range(2)]
        gst = [sb(f"gst{k}", [128, 512], BF16) for k in range(NST)]
        xst = [sb(f"xst{k}", [128, 512], F32) for k in range(NST)]
        gsem = [G.sems[1 + k] for k in range(NST)]
        xsem = [G.sems[5 + k] for k in range(NST)]
        g_free = [None] * NST
        x_free = [None] * NST
        gab_free = [None, None]
        pb = [G.PS[0], G.PS[1], G.PS[2], G.PS[3]]
        for (t0, T, w) in tiles:
            ws.push([S.win[m] for m in range(40)])
        ng = 0
        nx = 0
        for (t0, T, w) in tiles:
            prologue(G, B, X_in, t0, T, i, sl, w)
            tm = None
            for m in range(40):
                s, buf, lt = ws.next()
                bank = pb[m % 4]
                for c in range(DC):
                    tm = P.pe(_C("matmul",
                        bank.t[:, 0:T], lhsT=buf[:, c * 128:(c + 1) * 128], rhs=B.uT[:, c, 0:T],
                        start=(c == 0), stop=(c == DC - 1)),
                        waits=[lt, B.uT_ready, bank.free] if c == 0 else (), sig=(c == DC - 1))
                ws.release(s, tm)
                if m < RC:
                    j = ng % 2
                    k_ = ng % NST
                    ng += 1
                    t_ = P.act(_C("activation", out=ga[j][:, 0:T], in_=bank.t[:, 0:T], func=AF.Square),
                               waits=[tm, gab_free[j]], sig=True)
                    P.dve(_C("tensor_scalar", out=ga[j][:, 0:T], in0=ga[j][:, 0:T], scalar1=0.044715,
                                                         scalar2=1.0, op0=ALU.mult, op1=ALU.add), waits=[t_])
                    t_ = P.dve(_C("tensor_tensor", out=ga[j][:, 0:T], in0=ga[j][:, 0:T],
                                                                         in1=bank.t[:, 0:T], op=ALU.mult), sig=True)
                    t_ = P.act(_C("activation", out=gb_[j][:, 0:T], in_=ga[j][:, 0:T], func=AF.Sigmoid,
                                                           scale=1.5957691216057308), waits=[t_], sig=True)
                    t_ = P.dve(_C("tensor_tensor",
                        out=gst[k_][:, 0:T], in0=gb_[j][:, 0:T], in1=bank.t[:, 0:T], op=ALU.mult),
                        waits=[t_, g_free[k_]], sig=True)
                    bank.free = t_
                    gab_free[j] = t_
                    g_free[k_] = P.dma("gpsimd", S.GT[m][:, t0:t0 + T], gst[k_][:, 0:T], gsem[k_], waits=[t_])
                else:
                    k_ = nx % NST
                    nx += 1
                    t_ = P.act(_C("activation", out=xst[k_][:, 0:T], in_=bank.t[:, 0:T], func=AF.Copy),
                               waits=[tm, x_free[k_]], sig=True)
                    bank.free = t_
                    x_free[k_] = P.dma("gpsimd", S.XRT[m - RC][:, t0:t0 + T], xst[k_][:, 0:T], xsem[k_], waits=[t_])
            B.uT_free = tm
        P.barrier()
        P.emit()

    with ExitStack() as es:
        sb = mk_sb(nc, es)
        for b in G.PS:
            b.free = None
        HX = sb("HX", [128, NT], F32)
        xr = sb("xr", [128, 2, NT], F32)
        xrb = sb("xrb", [128, 2, NT], BF16)
        R = sb("R", [128, NT], F32)
        IG = sb("IG", [128, NT], F32)
        MM = sb("MM", [128, NT], F32)
        hsum = sb("hsum", [128, SL], F32)
        Gs = sb("Gs", [128, SL], BF16)
        hgo = sb("hgo", [128, SL], BF16)
        convw = sb("convw", [128, RC, 4], F32)
        lvec = sb("lvec", [128, 6, RC], F32)
        apar = sb("apar", [128, 2, RC], F32)
        cvec = sb("cvec", [128, 2, 2, RC], F32)
        tl1 = P.dma("gpsimd", convw[:], I.convw, G.sems[0])
        tl2 = P.dma("gpsimd", lvec[:], I.lvec, G.sems[0])
        tl3 = P.dma("gpsimd", apar[:], I.apar, G.sems[0])
        t_ = P.act(_C("activation", out=cvec[:, :, 0, :], in_=apar[:], func=AF.Exp, scale=-1.0), waits=[tl3], sig=True)
        t_ = P.act(_C("activation", out=cvec[:, :, 0, :], in_=cvec[:, :, 0, :], func=AF.Ln, bias=1.0),
                   waits=[t_], sig=True)
        t_ = P.dve(_C("tensor_scalar", out=cvec[:, :, 0, :], in0=cvec[:, :, 0, :], scalar1=-8.0, scalar2=None,
                                             op0=ALU.mult), waits=[t_], sig=True)
        c_ready = P.dve(_C("tensor_scalar", out=cvec[:, :, 1, :], in0=cvec[:, :, 0, :], scalar1=2.0, scalar2=None,
                                                  op0=ALU.mult), waits=[t_], sig=True)
        xsem, gsem_, osem = G.sems[1], G.sems[2], G.sems[3]
        ws.push([S.wgate[hb] for hb in range(10)])
        tblocks = [(t, min(512, NT - t)) for t in range(0, NT, 512)]
        segs = [(0, CTX), (CTX, NT)]
        gbanks = [G.PS[0], G.PS[1], G.PS[2], G.PS[3]]
        ngb = 0
        hx_free = None
        xr_free = None
        xrb_free = None
        out_free = None
        g_free = None
        for hb in range(10):
            s, buf, lt = ws.next()
            tcv = None
            tcb = None
            for ic in range(2):
                ch = hb * 2 + ic
                tx = P.dma("gpsimd", HX[:], S.XRT[ch], xsem, waits=[hx_free])
                for (a_, b_) in segs:
                    P.dve(_C("tensor_scalar",
                        out=xr[:, ic, a_:b_], in0=HX[:, a_:b_], scalar1=convw[:, ch, 2:3], scalar2=lvec[:, 0, ch:ch + 1],
                        op0=ALU.mult, op1=ALU.add), waits=[tx, tl1, tl2, xr_free])
                    for (j_, d_) in ((1, 1), (0, 2)):
                        P.dve(_C("scalar_tensor_tensor",
                            out=xr[:, ic, a_ + d_:b_], in0=HX[:, a_:b_ - d_], scalar=convw[:, ch, j_:j_ + 1],
                            in1=xr[:, ic, a_ + d_:b_], op0=ALU.mult, op1=ALU.add))
                    tcv = P.dve(_C("scalar_tensor_tensor",
                        out=xr[:, ic, a_:b_ - 1], in0=HX[:, a_ + 1:b_], scalar=convw[:, ch, 3:4],
                        in1=xr[:, ic, a_:b_ - 1], op0=ALU.mult, op1=ALU.add), sig=True)
                hx_free = tcv
                tcb = P.act(_C("activation", out=xrb[:, ic, :], in_=xr[:, ic, :], func=AF.Copy),
                            waits=[tcv, xrb_free], sig=True)
            last_use = None
            for oc in range(2):
                ch = hb * 2 + oc
                tg_ = P.dma("gpsimd", Gs[:], S.GT[ch][:, CTX:NT], gsem_, waits=[g_free])
                for d in range(2):
                    tsr = None
                    tsi = None
                    for (tb0, tbn) in tblocks:
                        for gt in range(2):
                            bank = gbanks[ngb % 4]
                            ngb += 1
                            tm = None
                            for ic in range(2):
                                col = (((d * 2 + gt) * 2 + ic) * 2 + oc) * 128
                                tm = P.pe(_C("matmul",
                                    bank.t[:, 0:tbn], lhsT=buf[:, col:col + 128], rhs=xrb[:, ic, tb0:tb0 + tbn],
                                    start=(ic == 0), stop=(ic == 1)),
                                    waits=[lt, tcb, bank.free] if ic == 0 else (), sig=(ic == 1))
                            dst = R if gt == 0 else IG
                            bcol = (1 + d) if gt == 0 else (3 + d)
                            ta = P.act(_C("activation",
                                out=dst[:, tb0:tb0 + tbn], in_=bank.t[:, 0:tbn], func=AF.Sigmoid,
                                bias=lvec[:, bcol, ch:ch + 1]), waits=[tm, tl2, last_use], sig=True)
                            bank.free = ta
                            if gt == 0:
                                tsr = ta
                            else:
                                tsi = ta
                    last_gate_pe = tm
                    P.act(_C("activation", out=MM[:], in_=R[:], func=AF.Exp,
                                                            scale=cvec[:, d, 1, ch:ch + 1]), waits=[c_ready, last_use])
                    P.act(_C("activation", out=MM[:], in_=MM[:], func=AF.Sqrt, scale=-1.0, bias=1.0))
                    ta_ = P.act(_C("activation", out=R[:], in_=R[:], func=AF.Exp,
                                                                  scale=cvec[:, d, 0, ch:ch + 1]), sig=True)
                    P.dve(_C("tensor_tensor", out=IG[:], in0=IG[:], in1=MM[:], op=ALU.mult), waits=[ta_, tsi])
                    tb_ = P.dve(_C("tensor_tensor", out=IG[:], in0=IG[:], in1=xr[:, oc, :], op=ALU.mult), sig=True)
                    if d == 0:
                        P.dve(_C("tensor_tensor_scan", out=HX[:], data0=R[:], data1=IG[:], initial=0.0,
                                                             op0=ALU.mult, op1=ALU.add), waits=[tb_, hx_free])
                        tsc = P.dve(_C("tensor_copy", out=hsum[:], in_=HX[:, CTX:NT]), waits=[out_free], sig=True)
                    else:
                        t1_ = P.dve(_C("tensor_tensor_scan",
                            out=HX[:, 0:CTX][:, ::-1], data0=R[:, 0:CTX][:, ::-1], data1=IG[:, 0:CTX][:, ::-1],
                            initial=0.0, op0=ALU.mult, op1=ALU.add), waits=[tb_], sig=True)
                        t2_ = P.dve(_C("scalar_tensor_tensor",
                            out=IG[:, NT - 1:NT], in0=R[:, NT - 1:NT], scalar=HX[:, 0:1], in1=IG[:, NT - 1:NT],
                            op0=ALU.mult, op1=ALU.add), waits=[t1_], sig=True)
                        t3_ = P.dve(_C("tensor_tensor_scan",
                            out=HX[:, CTX:NT][:, ::-1], data0=R[:, CTX:NT][:, ::-1], data1=IG[:, CTX:NT][:, ::-1],
                            initial=0.0, op0=ALU.mult, op1=ALU.add), waits=[t2_], sig=True)
                        tsc = P.dve(_C("tensor_tensor", out=hsum[:], in0=hsum[:], in1=HX[:, CTX:NT], op=ALU.add),
                                    waits=[t3_], sig=True)
                    last_use = tsc
                    hx_free = tsc
                to_ = P.pool(_C("tensor_tensor", out=hgo[:], in0=hsum[:], in1=Gs[:], op=ALU.mult),
                             waits=[tsc, tg_, out_free], sig=True)
                g_free = to_
                out_free = P.dma("gpsimd", S.HGT[ch][:, CTX:NT], hgo[:], osem, waits=[to_])
            ws.release(s, last_gate_pe)
            xr_free = last_use
            xrb_free = last_gate_pe
        P.barrier()
        P.emit()

    with ExitStack() as es:
        sb = mk_sb(nc, es)
        B = alloc_common(G, es)
        aT = [sb(f"hT{k}", [128, RC, 512], BF16) for k in range(2)]
        asem = [G.sems[4], G.sems[5]]
        a_free = [None, None]
        load_ln(G, B, i * 3 + sl)
        for (t0, T, w) in lat_tiles:
            ws.push([S.wout[m] for m in range(16)])
        prev = None
        for ti, (t0, T, w) in enumerate(lat_tiles):
            k = ti % 2
            ta = P.dma("gpsimd", aT[k][:, :, 0:T], S.HGT[:, :, t0:t0 + T].rearrange("h p t -> p h t"), asem[k],
                       waits=[a_free[k]])
            st = OutProj(G, B, RC, X_in, X_out, t0, T, modcol(i, sl, 2, 0), w, t0)
            if prev is not None:
                prev.end()
            st.begin()
            tm = None
            for m in range(16):
                s, buf, lt = ws.next()
                tm = st.slab(m, buf, lt, aT[k], ta)
                ws.release(s, tm)
            a_free[k] = tm
            prev = st
        prev.end()
        P.barrier()
        P.emit()


def _tile_w(W, KC, MC):
    return np.ascontiguousarray(W.reshape(KC, 128, MC, 128).transpose(2, 1, 0, 3)).reshape(MC, 128, KC * 128)


def _fm(v, n):
    return np.ascontiguousarray(np.asarray(v).reshape(n, 128).T)


def rope_tables(S_LAT):
    GRID_W = 64
    t = np.arange(S_LAT)
    row = (t // GRID_W).astype(np.float32)
    col = (t % GRID_W).astype(np.float32)
    inv = (np.float32(10000.0) ** (-np.arange(0, 32, 2, dtype=np.float32) / np.float32(32))).astype(np.float32)
    cos = np.ones((128, CTX + S_LAT), np.float32)
    sin = np.zeros((128, CTX + S_LAT), np.float32)
    for e in range(2):
        for d in range(64):
            pos = row if d < 32 else col
            dd = d % 32
            j = dd % 16
            ang = (pos * inv[j]).astype(np.float32)
            cos[e * 64 + d, CTX:] = np.cos(ang)
            sgn = -1.0 if dd < 16 else 1.0
            sin[e * 64 + d, CTX:] = sgn * np.sin(ang)
    return cos, sin


def _perm_cols():
    p = np.zeros(128, np.int64)
    for e in range(2):
        for d in range(64):
            dd = d % 32
            partner = d + 16 if dd < 16 else d - 16
            p[e * 64 + d] = e * 64 + partner
    return p


def prep_shared(inp, S_LAT):
    f = lambda a: np.asarray(a, dtype=np.float32)
    sh = {}
    wada = f(inp["w_ada"])
    sh["wada"] = np.ascontiguousarray(wada.reshape(2, 16, 128, 36, 512).transpose(0, 3, 2, 1, 4)).reshape(2, 36, 128, 16 * 512)
    bada = f(inp["b_ada"])
    sh["bada"] = np.ascontiguousarray(bada.reshape(2, 144, 128).transpose(2, 0, 1))
    sh["lng"] = f(inp["ln_g"]).reshape(6, D)
    sh["lnb"] = f(inp["ln_b"]).reshape(6, D)
    wg, wu, wd = f(inp["ffn_w_gate"]), f(inp["ffn_w_up"]), f(inp["ffn_w_down"])
    wgu = np.empty((4, FC, 128, 4096), np.float32)
    wdn = np.empty((4, DC, 128, FC * 128), np.float32)
    for i in range(2):
        for j in range(2):
            g = _tile_w(wg[i, j], DC, FC)
            u = _tile_w(wu[i, j], DC, FC)
            wgu[i * 2 + j, :, :, 0:2048] = g
            wgu[i * 2 + j, :, :, 2048:4096] = u
            wdn[i * 2 + j] = _tile_w(wd[i, j], FC, DC)
    sh["wgu"], sh["wdn"] = wgu, wdn
    wqkv = f(inp["attn_w_qkv"])[0]
    perm = _perm_cols()
    wqk = np.empty((32, 128, 4096), np.float32)
    for h in range(NH):
        for t_, off in ((0, 0), (1, D)):
            Wh = wqkv[:, off + h * 128: off + (h + 1) * 128]
            wqk[h * 2 + t_, :, 0:2048] = _tile_w(Wh, DC, 1)[0]
            wqk[h * 2 + t_, :, 2048:4096] = _tile_w(Wh[:, perm], DC, 1)[0]
    sh["wqk"] = wqk
    Wv = wqkv[:, 2 * D:3 * D]
    sh["wv"] = np.ascontiguousarray(Wv.reshape(DC, 128, D).transpose(1, 0, 2)).reshape(128, 8, 4096).transpose(1, 0, 2).copy()
    sh["wo"] = _tile_w(f(inp["attn_w_o"])[0], DC, DC)
    sh["lam"] = np.stack([f(inp["attn_lambda_q1"])[0], f(inp["attn_lambda_k1"])[0],
                          f(inp["attn_lambda_q2"])[0], f(inp["attn_lambda_k2"])[0]])
    sh["subg"] = f(inp["attn_subln_g"])[0].reshape(128, 1).copy()
    cos, sin = rope_tables(S_LAT)
    sh["cos"], sh["sin"] = cos, sin
    sh["win"] = _tile_w(f(inp["lru_w_in"])[0], DC, 40)
    cw = f(inp["lru_conv_w"])[0]
    sh["convw"] = np.ascontiguousarray(cw.reshape(4, RC, 128).transpose(2, 1, 0))
    lv = np.empty((128, 6, RC), np.float32)
    lv[:, 0] = _fm(f(inp["lru_conv_b"])[0], RC)
    ba, bi = f(inp["lru_b_a"])[0], f(inp["lru_b_i"])[0]
    lv[:, 1] = _fm(ba[0], RC)
    lv[:, 2] = _fm(ba[1], RC)
    lv[:, 3] = _fm(bi[0], RC)
    lv[:, 4] = _fm(bi[1], RC)
    lv[:, 5] = 0.0
    sh["lvec"] = lv
    ap_ = f(inp["lru_a_param"])[0]
    sh["apar"] = np.stack([_fm(ap_[0], RC), _fm(ap_[1], RC)], axis=1).copy()
    wa, wi = f(inp["lru_w_a"])[0], f(inp["lru_w_i"])[0]
    wgate = np.empty((10, 128, 2048), np.float32)
    for hb in range(10):
        for d in range(2):
            for gt, Wt in ((0, wa), (1, wi)):
                for ic in range(2):
                    for oc in range(2):
                        col = (((d * 2 + gt) * 2 + ic) * 2 + oc) * 128
                        wgate[hb, :, col:col + 128] = Wt[d, hb, ic * 128:(ic + 1) * 128, oc * 128:(oc + 1) * 128]
    sh["wgate"] = wgate
    sh["wout"] = _tile_w(f(inp["lru_w_out"])[0], RC, DC)
    sh["eye"] = np.eye(128, dtype=np.float32)
    pm = np.zeros((128, 128), np.float32)
    pm[perm, np.arange(128)] = 1.0
    sh["permm"] = pm
    return sh


def prep_core(inp, b):
    f = lambda a: np.asarray(a, dtype=np.float32)
    d = {}
    d["x0"] = np.concatenate([f(inp["ctx"])[b], f(inp["x"])[b]], axis=0)
    cf = np.empty((128, 2, 16), np.float32)
    cf[:, 0] = _fm(f(inp["c"])[b], 16)
    cf[:, 1] = _fm(f(inp["c_ctx"]), 16)
    d["cfm"] = cf
    return d


_CACHE = {}


def kernel(**inputs):
    x = np.asarray(inputs["x"])
    Bn, S_LAT, _ = x.shape
    key = S_LAT
    if key not in _CACHE:
        _CACHE[key] = build(S_LAT)
    nc = _CACHE[key]
    sh = prep_shared(inputs, S_LAT)
    in_maps = []
    for b in range(Bn):
        m = dict(sh)
        m.update(prep_core(inputs, b))
        in_maps.append(m)
    res = run_bass_kernel_spmd(nc, in_maps, core_ids=list(range(Bn)))
    return np.stack([np.asarray(r["out"]) for r in res.results], axis=0).astype(np.float32)
```

```python
import math
from contextlib import ExitStack
import numpy as np
import concourse.bass as bass
import concourse.mybir as mybir
from concourse.bass_utils import run_bass_kernel_spmd

F32 = mybir.dt.float32
BF16 = mybir.dt.bfloat16
AF = mybir.ActivationFunctionType
ALU = mybir.AluOpType
AX = mybir.AxisListType

D = 2048
DC = 16
DFF = 5632
FC = 44
CTX = 256
DRNN = 2560
RC = 20
NH = 16
ALPHA = 2.0 ** 0.5
LN_EPS = 1e-6
LAMBDA_INIT = 0.8 - 0.6 * math.exp(-0.3 * 0)
WSLOT = 5632
NWBUF = 4
PIPE_QK = True
ENGS = ("tensor", "vector", "scalar", "gpsimd", "sync")


def _C(name, *a, **k):
    return (name, a, k)


class DSem:
    def __init__(self, h):
        self.h = h
        self.n = 0

    def tok(self):
        return (self.h, self.n)


class Op:
    __slots__ = ("fn", "ws", "inc", "dma")

    def __init__(self, fn, ws, inc, dma):
        self.fn, self.ws, self.inc, self.dma = fn, ws, inc, dma


class Prog:
    def __init__(self, nc):
        self.nc = nc
        self.ops = {e: [] for e in ENGS}
        self.esem = {e: nc.alloc_semaphore("es_" + e) for e in ENGS if e != "sync"}
        self.ecnt = {e: 0 for e in ENGS}
        self.waited = {e: {} for e in ENGS}
        self.dsems = []
        self.ninst = 0

    def new_sem(self, name=None):
        s = DSem(self.nc.alloc_semaphore(name or f"ds{len(self.dsems)}"))
        self.dsems.append(s)
        return s

    def _flat(self, toks, out):
        for t in toks:
            if t is None:
                continue
            if isinstance(t, list) or (isinstance(t, tuple) and (len(t) != 2 or isinstance(t[0], (tuple, list)) or t[0] is None)):
                self._flat(t, out)
            else:
                out.append(t)

    def _waits(self, eng, toks):
        fl = []
        self._flat(toks, fl)
        ws = []
        for sem, v in fl:
            if v <= 0:
                continue
            key = id(sem)
            if self.waited[eng].get(key, 0) >= v:
                continue
            self.waited[eng][key] = v
            ws.append((sem, v))
        return ws

    def op(self, eng, fn, waits=(), sig=False):
        ws = self._waits(eng, waits)
        tok = None
        inc = None
        if sig:
            self.ecnt[eng] += 1
            inc = (self.esem[eng], 1)
            tok = (self.esem[eng], self.ecnt[eng])
        self.ops[eng].append(Op(fn, ws, inc, False))
        self.ninst += 1
        return tok

    def pe(self, fn, waits=(), sig=False):
        return self.op("tensor", fn, waits, sig)

    def dve(self, fn, waits=(), sig=False):
        return self.op("vector", fn, waits, sig)

    def act(self, fn, waits=(), sig=False):
        return self.op("scalar", fn, waits, sig)

    def pool(self, fn, waits=(), sig=False):
        return self.op("gpsimd", fn, waits, sig)

    def dma(self, eng, out, in_, dsem, waits=()):
        ws = self._waits(eng, waits)
        dsem.n += 16
        self.ops[eng].append(Op(lambda e, o=out, i=in_: e.dma_start(out=o, in_=i), ws, (dsem.h, 16), True))
        self.ninst += 1
        return (dsem.h, dsem.n)

    def wait_only(self, eng, waits):
        ws = self._waits(eng, waits)
        if ws:
            self.ops[eng].append(Op(None, ws, None, False))

    def barrier(self):
        toks = []
        for e in ("tensor", "vector", "scalar", "gpsimd"):
            last = None
            for o in reversed(self.ops[e]):
                if o.fn is not None and not o.dma:
                    last = o
                    break
            if last is not None and last.inc is None:
                self.ecnt[e] += 1
                last.inc = (self.esem[e], 1)
            if self.ecnt[e] > 0:
                toks.append((self.esem[e], self.ecnt[e]))
        for s in self.dsems:
            if s.n > 0:
                toks.append((s.h, s.n))
        for e in ENGS:
            self.wait_only(e, toks)

    def emit(self):
        nc = self.nc
        ops = self.ops
        with nc.Block() as block:
            def body(ename):
                def f(e):
                    for o in ops[ename]:
                        for sem, v in o.ws:
                            e.wait_ge(sem, v)
                        if o.fn is None:
                            continue
                        ins = o.fn(e) if callable(o.fn) else getattr(e, o.fn[0])(*o.fn[1], **o.fn[2])
                        if o.inc is not None:
                            ins.then_inc(o.inc[0], o.inc[1])
                return f
            block.sync(body("sync"))
            block.tensor(body("tensor"))
            block.vector(body("vector"))
            block.scalar(body("scalar"))
            block.gpsimd(body("gpsimd"))
        self.ops = {e: [] for e in ENGS}


class Bank:
    def __init__(self, t):
        self.t = t
        self.free = None


class WStream:
    def __init__(self, P, nc):
        self.P = P
        self.slots = [nc.alloc_sbuf_tensor(f"wslot{i}", [128, WSLOT], BF16) for i in range(NWBUF)]
        self.sems = [P.new_sem(f"wsem{i}") for i in range(NWBUF)]
        self.free = [None] * NWBUF
        self.queue = []
        self.inflight = []
        self.gi = 0
        self.nrel = 0
        self.ready = None

    def push(self, srcs):
        self.queue.extend(srcs)

    def _issue(self):
        while self.queue and self.gi - self.nrel < NWBUF:
            src = self.queue.pop(0)
            s = self.gi % NWBUF
            self.gi += 1
            E = src.shape[1]
            tok = self.P.dma("sync", self.slots[s][:, 0:E], src, self.sems[s], waits=[self.free[s], self.ready])
            self.inflight.append((s, tok))

    def next(self):
        self._issue()
        s, tok = self.inflight.pop(0)
        return s, self.slots[s], tok

    def release(self, s, tok):
        self.free[s] = tok
        self.nrel += 1
        self._issue()


class Ctx:
    pass


_UID = [0]


def mk_sb(nc, es):
    _UID[0] += 1
    u = _UID[0]
    return lambda n, s, d: es.enter_context(nc.sbuf_tensor(f"{n}_u{u}", s, d))


def build(S_LAT=4096, dbg=False, phases=None):
    NT = CTX + S_LAT
    NKC = NT // 128
    nc = bass.Bass("TRN2", target_bir_lowering=False)
    P = Prog(nc)
    G = Ctx()
    G.nc, G.P, G.SLAT, G.NT, G.NKC = nc, P, S_LAT, NT, NKC
    tiles = [(0, CTX, 1)] + [(CTX + 512 * i, 512, 0) for i in range(S_LAT // 512)]
    lat_tiles = tiles[1:]
    G.tiles, G.lat_tiles = tiles, lat_tiles

    def din(name, shape, dt=F32):
        return nc.dram_tensor(name, list(shape), dt, kind="ExternalInput").ap()

    skind = "ExternalOutput" if dbg else "Internal"

    def dscr(name, shape, dt):
        return nc.dram_tensor(name, list(shape), dt, kind=skind).ap()

    I = Ctx()
    I.x0 = din("x0", [NT, D])
    I.cfm = din("cfm", [128, 2, 16])
    I.wada = din("wada", [2, 36, 128, 16 * 512])
    I.bada = din("bada", [128, 2, 144])
    I.lng = din("lng", [6, D])
    I.lnb = din("lnb", [6, D])
    I.wgu = din("wgu", [4, FC, 128, 4096])
    I.wdn = din("wdn", [4, DC, 128, FC * 128])
    I.wqk = din("wqk", [32, 128, 4096])
    I.wv = din("wv", [8, 128, 4096])
    I.wo = din("wo", [16, 128, 2048])
    I.lam = din("lam", [4, 64])
    I.subg = din("subg", [128, 1])
    I.cos = din("cos", [128, NT])
    I.sin = din("sin", [128, NT])
    I.win = din("win", [40, 128, 2048])
    I.convw = din("convw", [128, RC, 4])
    I.lvec = din("lvec", [128, 6, RC])
    I.apar = din("apar", [128, 2, RC])
    I.wgate = din("wgate", [10, 128, 2048])
    I.wout = din("wout", [16, 128, RC * 128])
    I.eye = din("eye", [128, 128])
    I.permm = din("permm", [128, 128])
    out = nc.dram_tensor("out", [S_LAT, D], F32, kind="ExternalOutput").ap()

    S = Ctx()
    S.XA = dscr("XA", [NT, D], F32)
    S.XB = dscr("XB", [NT, D], F32)
    S.wgu = dscr("s_wgu", [4, FC, 128, 4096], BF16)
    S.wdn = dscr("s_wdn", [4, DC, 128, FC * 128], BF16)
    S.wqk = dscr("s_wqk", [32, 128, 4096], BF16)
    S.wv = dscr("s_wv", [8, 128, 4096], BF16)
    S.wo = dscr("s_wo", [16, 128, 2048], BF16)
    S.win = dscr("s_win", [40, 128, 2048], BF16)
    S.wgate = dscr("s_wgate", [10, 128, 2048], BF16)
    S.wout = dscr("s_wout", [16, 128, RC * 128], BF16)
    S.QT = dscr("QT", [NH, 128, NT], BF16)
    S.KT = dscr("KT", [NH, 128, NT], BF16)
    S.V = dscr("V", [NT, D], BF16)
    S.AOT = dscr("AOT", [NH, 128, NT], BF16)
    S.GT = dscr("GT", [RC, 128, NT], BF16)
    S.XRT = dscr("XRT", [RC, 128, NT], F32)
    S.HGT = dscr("HGT", [RC, 128, NT], BF16)
    G.I, G.S, G.out = I, S, out

    G.ws = WStream(P, nc)
    G.identF = nc.alloc_sbuf_tensor("identF", [128, 128], F32)
    G.identB = nc.alloc_sbuf_tensor("identB", [128, 128], BF16)
    G.MOD = nc.alloc_sbuf_tensor("MOD", [128, 288, 2], F32)
    G.PS = [Bank(nc.alloc_psum_tensor(f"bank{i}", [128, 512], F32)) for i in range(8)]
    G.sems = [P.new_sem(f"gs{i}") for i in range(24)]

    phases = phases or ["ada", "conv", "l0f0", "attn", "l0f2", "l1f0", "lru", "l1f2"]

    t = P.dma("sync", G.identF[:], I.eye, G.sems[0])
    P.dve(_C("tensor_copy", out=G.identB[:], in_=G.identF[:]), waits=[t])
    P.barrier()
    P.emit()

    G.conv = "conv" in phases
    if "ada" in phases:
        phase_ada(G, conv=G.conv)
    X0, XA, XB = I.x0, S.XA, S.XB
    if "l0f0" in phases:
        phase_ffn(G, 0, 0, X0, XA, tiles)
    if "attn" in phases:
        phase_attn(G, XA, XB)
    if "l0f2" in phases:
        phase_ffn(G, 0, 2, XB, XA, tiles)
    if "l1f0" in phases:
        phase_ffn(G, 1, 0, XA, XB, tiles)
    if "lru" in phases:
        phase_lru(G, XB, XA)
    if "l1f2" in phases:
        phase_ffn(G, 1, 2, XA, out, lat_tiles, out_off=CTX)
    P.barrier()
    P.emit()
    return nc


def modcol(i, sl, comp, c):
    return i * 144 + (sl * 3 + comp) * 16 + c


def phase_ada(G, conv=True):
    nc, P, I = G.nc, G.P, G.I
    with ExitStack() as es:
        sb = mk_sb(nc, es)
        cv = Converter(G, es, "early", 3, ("vector", "scalar")) if conv else None
        craw = sb("craw", [128, 2, 16], F32)
        cs = sb("cs", [128, 16, 2], F32)
        bada = sb("badas", [128, 2, 144], F32)
        slabs = [sb(f"adaslab{i}", [128, 16 * 512], F32) for i in range(2)]
        ssem = [G.sems[0], G.sems[1]]
        t0 = P.dma("sync", craw[:], I.cfm, G.sems[2])
        t1 = P.dma("sync", bada[:], I.bada, G.sems[3])
        ta = None
        for w in range(2):
            ta = P.act(_C("activation", out=cs[:, :, w], in_=craw[:, w, :], func=AF.Silu), waits=[t0], sig=True)
        sfree = [None, None]
        banks = [G.PS[0], G.PS[1]]
        n = 0
        for i in range(2):
            for cb in range(36):
                s = n % 2
                lt = P.dma("sync", slabs[s][:], I.wada[i, cb], ssem[s], waits=[sfree[s]])
                for q in range(4):
                    bank = banks[(n * 4 + q) % 2]
                    tm = None
                    for k in range(16):
                        tm = P.pe(_C("matmul",
                            bank.t[:, 0:2], lhsT=slabs[s][:, k * 512 + q * 128: k * 512 + (q + 1) * 128],
                            rhs=cs[:, k, :], start=(k == 0), stop=(k == 15)),
                            waits=[lt, ta, bank.free] if k == 0 else (), sig=(k == 15))
                    col = i * 144 + cb * 4 + q
                    tv = P.dve(_C("tensor_scalar",
                        out=G.MOD[:, col, :], in0=bank.t[:, 0:2], scalar1=bada[:, i, cb * 4 + q:cb * 4 + q + 1], scalar2=None,
                        op0=ALU.add), waits=[tm, t1], sig=True)
                    bank.free = tv
                sfree[s] = tm
                n += 1
                if cv is not None:
                    cv.step(2)
        if cv is not None:
            cv.step(10 ** 6)
        for i in range(2):
            for sl in range(3):
                c0 = modcol(i, sl, 1, 0)
                P.dve(_C("tensor_scalar", out=G.MOD[:, c0:c0 + 16, :], in0=G.MOD[:, c0:c0 + 16, :],
                                                       scalar1=1.0, scalar2=None, op0=ALU.add), waits=[tv])
                if sl != 1:
                    g0 = modcol(i, sl, 2, 0)
                    P.dve(_C("tensor_scalar", out=G.MOD[:, g0:g0 + 16, :], in0=G.MOD[:, g0:g0 + 16, :],
                                                           scalar1=0.5, scalar2=None, op0=ALU.mult))
        P.barrier()
        P.emit()


class Converter:
    def __init__(self, G, es, which, NB, cast_engs):
        self.G = G
        nc, P, I, S = G.nc, G.P, G.I, G.S
        jobs = []

        def add(src, dst, ecols=None):
            sh = src.shape
            lead = sh[:-2]
            E = ecols or sh[-1]
            idxs = [()]
            for n in lead:
                idxs = [ix + (j,) for ix in idxs for j in range(n)]
            for ix in idxs:
                s_ = src
                d_ = dst
                for j in ix:
                    s_ = s_[j]
                    d_ = d_[j]
                for e0 in range(0, E, 4096):
                    n_ = min(4096, E - e0)
                    jobs.append((s_[:, e0:e0 + n_], d_[:, e0:e0 + n_], n_))

        if which == "early":
            add(I.wgu[0], S.wgu[0])
            add(I.wdn[0], S.wdn[0])
            add(I.wqk, S.wqk, ecols=2048)
            add(I.wv, S.wv)
            add(I.wo, S.wo)
        else:
            for fi in range(1, 4):
                add(I.wgu[fi], S.wgu[fi])
                add(I.wdn[fi], S.wdn[fi])
            add(I.win, S.win)
            add(I.wgate, S.wgate)
            add(I.wout, S.wout)
        self.jobs = jobs
        self.j = 0
        self.NB = NB
        self.cast_engs = cast_engs
        sb = mk_sb(nc, es)
        self.fin = [sb(f"cvin{i}", [128, 4096], F32) for i in range(NB)]
        self.fout = [sb(f"cvout{i}", [128, 4096], BF16) for i in range(NB)]
        self.lsem = [G.sems[8 + i] for i in range(NB)]
        self.ssem = [G.sems[8 + NB + i] for i in range(NB)]
        self.infree = [None] * NB
        self.outfree = [None] * NB

    def step(self, n):
        P = self.G.P
        for _ in range(n):
            if self.j >= len(self.jobs):
                return
            j = self.j
            self.j += 1
            src, dst, n_ = self.jobs[j]
            s = j % self.NB
            fin, fout = self.fin[s], self.fout[s]
            lt = P.dma("sync", fin[:, 0:n_], src, self.lsem[s], waits=[self.infree[s]])
            eng = self.cast_engs[j % len(self.cast_engs)]
            if eng == "scalar":
                ct = P.act(_C("activation", out=fout[:, 0:n_], in_=fin[:, 0:n_], func=AF.Copy),
                           waits=[lt, self.outfree[s]], sig=True)
            else:
                ct = P.op(eng, _C("tensor_copy", out=fout[:, 0:n_], in_=fin[:, 0:n_]),
                          waits=[lt, self.outfree[s]], sig=True)
            self.infree[s] = ct
            self.outfree[s] = P.dma("gpsimd", dst, fout[:, 0:n_], self.ssem[s], waits=[ct])


def prologue(G, B, X_in, t0, T, i, sl, w):
    P = G.P
    S4 = T // 128
    tokx = P.dma("gpsimd", B.xin[:, 0:S4, :], X_in[t0:t0 + T, :].rearrange("(s p) d -> p s d", p=128),
                 B.xin_sem, waits=[B.xin_free])
    ev = None
    tp = None
    for c in range(DC):
        bank = B.tpb[c % 2]
        for s in range(S4):
            tp = P.pe(_C("transpose",
                out=bank.t[:, s * 128:(s + 1) * 128], in_=B.xin[:, s, c * 128:(c + 1) * 128], identity=G.identF[:]),
                waits=[tokx, bank.free] if s == 0 else (), sig=(s == S4 - 1))
        sc = modcol(i, sl, 1, c)
        sh = modcol(i, sl, 0, c)
        ev = P.dve(_C("tensor_scalar",
            out=B.uT[:, c, 0:T], in0=bank.t[:, 0:T], scalar1=G.MOD[:, sc, w:w + 1], scalar2=G.MOD[:, sh, w:w + 1],
            op0=ALU.mult, op1=ALU.add), waits=[tp, B.uT_free] if c == 0 else [tp], sig=True)
        bank.free = ev
    B.xin_free = tp
    B.uT_ready = ev


def layer_norm_rows(G, B, xs, s, lnw, after):
    P = G.P
    tk = None
    for q in range(4):
        tk = P.dve(_C("bn_stats", out=B.stats[:, s, q, :], in_=xs[:, q * 512:(q + 1) * 512]),
                   waits=[after] if q == 0 else (), sig=(q == 3))
    t1 = P.dve(_C("bn_aggr", out=B.mv[:, s, :], in_=B.stats[:, s, :, :].rearrange("p a b -> p (a b)")),
               waits=[tk], sig=True)
    t2 = P.dve(_C("tensor_scalar", out=B.sm[:, s, 0:1], in0=B.mv[:, s, 1:2], scalar1=LN_EPS, scalar2=None,
                                         op0=ALU.add), waits=[t1], sig=True)
    t3 = P.act(_C("activation", out=B.sm[:, s, 1:2], in_=B.sm[:, s, 0:1], func=AF.Sqrt), waits=[t2], sig=True)
    t4 = P.dve(_C("reciprocal", out=B.sm[:, s, 2:3], in_=B.sm[:, s, 1:2]), waits=[t3], sig=True)
    t5 = P.dve(_C("scalar_tensor_tensor", out=B.sm[:, s, 3:4], in0=B.mv[:, s, 0:1], scalar=-1.0,
                                                in1=B.sm[:, s, 2:3], op0=ALU.mult, op1=ALU.mult), waits=[t4], sig=True)
    t6 = P.act(_C("activation", out=xs, in_=xs, func=AF.Identity, scale=B.sm[:, s, 2:3], bias=B.sm[:, s, 3:4]),
               waits=[t5, after], sig=True)
    P.pool(_C("tensor_tensor", out=xs, in0=xs, in1=B.gbc[:], op=ALU.mult), waits=[t6, lnw])
    t8 = P.pool(_C("tensor_tensor", out=xs, in0=xs, in1=B.bbc[:], op=ALU.add), sig=True)
    return t8


class OutProj:
    def __init__(self, G, B, KC, X_in, X_out, t0, T, gcol0, w, out_row0, k=0):
        self.k = k
        self.xa = B.x_all[k]
        self.G, self.B, self.KC, self.X_in, self.X_out = G, B, KC, X_in, X_out
        self.t0, self.T, self.gcol0, self.w, self.out_row0 = t0, T, gcol0, w, out_row0
        self.pending = None
        self.last_stt = None

    def begin(self):
        G, B, P = self.G, self.B, self.G.P
        S4 = self.T // 128
        self.xl = P.dma("gpsimd", self.xa[:, 0:S4, :],
                        self.X_in[self.t0:self.t0 + self.T, :].rearrange("(s p) d -> p s d", p=128),
                        B.xall_sem, waits=[B.xall_free[self.k]])

    def slab(self, m, buf, ltok, actT, act_ready):
        G, B, P, T, KC = self.G, self.B, self.G.P, self.T, self.KC
        bank = B.dnb[m % 2]
        tm = None
        for k in range(KC):
            tm = P.pe(_C("matmul", bank.t[:, 0:T], lhsT=buf[:, k * 128:(k + 1) * 128],
                                                         rhs=actT[:, k, 0:T], start=(k == 0), stop=(k == KC - 1)),
                      waits=[ltok, act_ready, bank.free] if k == 0 else (), sig=(k == KC - 1))
        gyb = B.gy[m % 3]
        gc = self.gcol0 + m
        ta = P.act(_C("activation",
            out=gyb[:, 0:T], in_=bank.t[:, 0:T], func=AF.Copy, scale=G.MOD[:, gc, self.w:self.w + 1]),
            waits=[tm, B.gy_free[m % 3]], sig=True)
        bank.free = ta
        if self.pending is not None:
            self.transposes(*self.pending)
        self.pending = (m, gyb, ta)
        return tm

    def transposes(self, m, gyb, ta):
        G, B, P, T = self.G, self.B, self.G.P, self.T
        S4 = T // 128
        bank = B.tpb[m % 2]
        tp = None
        for s in range(S4):
            tp = P.pe(_C("transpose", out=bank.t[:, s * 128:(s + 1) * 128],
                                                            in_=gyb[:, s * 128:(s + 1) * 128], identity=G.identF[:]),
                      waits=[ta, bank.free] if s == 0 else (), sig=(s == S4 - 1))
        B.gy_free[m % 3] = tp
        xv = self.xa[:, 0:S4, m * 128:(m + 1) * 128]
        tv = P.dve(_C("scalar_tensor_tensor",
            out=xv, in0=xv, scalar=ALPHA, in1=bank.t[:, 0:T].rearrange("p (s d) -> p s d", d=128),
            op0=ALU.mult, op1=ALU.add), waits=[tp, self.xl], sig=True)
        bank.free = tv
        self.last_stt = tv

    def end(self):
        G, B, P, T = self.G, self.B, self.G.P, self.T
        S4 = T // 128
        self.transposes(*self.pending)
        self.pending = None
        sts = []
        for s in range(S4):
            xs = self.xa[:, s, :]
            t8 = layer_norm_rows(G, B, xs, s + 4 * self.k, B.ln_ready, self.last_stt)
            r0 = self.out_row0 + s * 128
            sts.append(P.dma("gpsimd", self.X_out[r0:r0 + 128, :], xs, B.st_sem, waits=[t8]))
        B.xall_free[self.k] = sts


def alloc_common(G, es, with_xin=True, nx=1):
    nc = G.nc
    sb = mk_sb(nc, es)
    B = Ctx()
    if with_xin:
        B.xin = sb("xin", [128, 4, D], F32)
        B.uT = sb("uT", [128, DC, 512], BF16)
    B.x_all = [sb(f"x_all{k}", [128, 4, D], F32) for k in range(nx)]
    B.gbc = sb("gbc", [128, D], F32)
    B.bbc = sb("bbc", [128, D], F32)
    B.gy = [sb(f"gy{i}", [128, 512], F32) for i in range(3)]
    B.stats = sb("stats", [128, 8, 4, 6], F32)
    B.mv = sb("mv", [128, 8, 2], F32)
    B.sm = sb("sm", [128, 8, 4], F32)
    B.gy_free = [None] * 3
    B.xin_sem, B.xall_sem, B.st_sem, B.ln_sem = G.sems[0], G.sems[1], G.sems[2], G.sems[3]
    B.xin_free = None
    B.xall_free = [None] * nx
    B.uT_free = None
    B.uT_ready = None
    B.tpb = [G.PS[6], G.PS[7]]
    B.dnb = [G.PS[4], G.PS[5]]
    for b in G.PS:
        b.free = None
    return B


def load_ln(G, B, idx):
    P, I = G.P, G.I
    P.dma("gpsimd", B.gbc[:], I.lng[idx].partition_broadcast(128), B.ln_sem)
    B.ln_ready = P.dma("gpsimd", B.bbc[:], I.lnb[idx].partition_broadcast(128), B.ln_sem)


def phase_ffn(G, i, sl, X_in, X_out, tiles, out_off=0):
    nc, P, S = G.nc, G.P, G.S
    fi = i * 2 + (0 if sl == 0 else 1)
    ws = G.ws
    with ExitStack() as es:
        sb = mk_sb(nc, es)
        B = alloc_common(G, es)
        hT = sb("hT", [128, FC, 512], BF16)
        sg = [sb(f"sg{k}", [128, 512], F32) for k in range(2)]
        sg_free = [None, None]
        gub = [(G.PS[0], G.PS[1]), (G.PS[2], G.PS[3])]
        load_ln(G, B, i * 3 + sl)
        for (t0, T, w) in tiles:
            ws.push([S.wgu[fi, f] for f in range(FC)])
            ws.push([S.wdn[fi, m] for m in range(DC)])
        prev = None
        for ti, (t0, T, w) in enumerate(tiles):
            prologue(G, B, X_in, t0, T, i, sl, w)
            if prev is not None:
                prev.end()
                prev = None
            tlast = None
            for f in range(FC):
                s, buf, lt = ws.next()
                pp = f % 2
                toks = [None, None]
                for gu in range(2):
                    bank = gub[pp][gu]
                    for c in range(DC):
                        toks[gu] = P.pe(_C("matmul",
                            bank.t[:, 0:T], lhsT=buf[:, (gu * 16 + c) * 128:(gu * 16 + c + 1) * 128],
                            rhs=B.uT[:, c, 0:T], start=(c == 0), stop=(c == DC - 1)),
                            waits=[lt, B.uT_ready, bank.free] if c == 0 else (), sig=(c == DC - 1))
                ws.release(s, toks[1])
                gb, ub = gub[pp]
                ta = P.act(_C("activation", out=sg[pp][:, 0:T], in_=gb.t[:, 0:T], func=AF.Silu),
                           waits=[toks[0], sg_free[pp]], sig=True)
                gb.free = ta
                tv = P.dve(_C("tensor_tensor", out=hT[:, f, 0:T], in0=sg[pp][:, 0:T],
                                                                        in1=ub.t[:, 0:T], op=ALU.mult),
                           waits=[ta, toks[1]], sig=True)
                ub.free = tv
                sg_free[pp] = tv
                tlast = toks[1]
            B.uT_free = tlast
            h_ready = tv
            st = OutProj(G, B, FC, X_in, X_out, t0, T, modcol(i, sl, 2, 0), w, t0 - out_off)
            st.begin()
            for m in range(DC):
                s, buf, lt = ws.next()
                tm = st.slab(m, buf, lt, hT, h_ready)
                ws.release(s, tm)
            prev = st
        prev.end()
        P.barrier()
        P.emit()


def phase_attn(G, X_in, X_out):
    nc, P, I, S = G.nc, G.P, G.I, G.S
    NT, NKC, tiles = G.NT, G.NKC, G.tiles
    ws = G.ws
    i, sl = 0, 1
    with ExitStack() as es:
        sb = mk_sb(nc, es)
        B = Ctx()
        B.xin = sb("xin", [128, 4, D], F32)
        B.uT = sb("uT", [128, DC, 512], BF16)
        B.xin_sem = G.sems[0]
        B.xin_free = None
        B.uT_free = None
        B.tpb = [G.PS[6], G.PS[7]]
        for b in G.PS:
            b.free = None
        wv = sb("wv", [128, DC, D], BF16)
        cosb = [sb(f"cos{k}", [128, 512], F32) for k in range(2)]
        sinb = [sb(f"sin{k}", [128, 512], F32) for k in range(2)]
        cs_free = [None, None]
        t1 = [sb(f"rt1_{k}", [128, 512], F32) for k in range(2)]
        t2 = [sb(f"rt2_{k}", [128, 512], F32) for k in range(2)]
        qst = [sb(f"qst{k}", [128, 512], BF16) for k in range(4)]
        qsb = [sb(f"qsb{k}", [128, 512], BF16) for k in range(2)]
        qsb_free = [None, None]
        permF = sb("permF", [128, 128], F32)
        permT = sb("permT", [128, 128], BF16)
        tpl = P.dma("gpsimd", permF[:], I.permm, G.sems[9])
        perm_ready = P.dve(_C("tensor_copy", out=permT[:], in_=permF[:]), waits=[tpl], sig=True)
        vst = [sb(f"vst{k}", [128, D], BF16) for k in range(2)]
        twv = None
        for k in range(8):
            twv = P.dma("gpsimd", wv[:, 2 * k:2 * k + 2, :].rearrange("p a b -> p (a b)"), S.wv[k], G.sems[2])
        qsem = [G.sems[3 + k] for k in range(4)]
        vsem = [G.sems[7 + k] for k in range(2)]
        q_free = [None] * 4
        v_free = [None] * 2
        t_free = [None, None]
        qkb = [(G.PS[0], G.PS[1]), (G.PS[2], G.PS[3])]
        vb = [G.PS[4], G.PS[5]]
        for (t0, T, w) in tiles:
            ws.push([S.wqk[m][:, 0:2048] for m in range(32)])
        nq = 0
        nv = 0
        for ti, (t0, T, w) in enumerate(tiles):
            S4 = T // 128
            cos, sin = cosb[ti % 2], sinb[ti % 2]
            P.dma("gpsimd", cos[:, 0:T], I.cos[:, t0:t0 + T], G.sems[1], waits=[cs_free[ti % 2]])
            tc_ = ts_ = P.dma("gpsimd", sin[:, 0:T], I.sin[:, t0:t0 + T], G.sems[1])
            prologue(G, B, X_in, t0, T, i, sl, w)
            tlast = None
            pend = None

            def finish(m, pp, tq, tcopy):
                nonlocal nq, tlast
                b0, b1 = qkb[pp]
                tpm = P.pe(_C("matmul", b1.t[:, 0:T], lhsT=permT[:], rhs=qsb[pp][:, 0:T], start=True, stop=True),
                           waits=[tcopy, b1.free, perm_ready], sig=True)
                qsb_free[pp] = tpm
                tlast = tpm
                P.dve(_C("tensor_tensor", out=t1[pp][:, 0:T], in0=b0.t[:, 0:T], in1=cos[:, 0:T], op=ALU.mult),
                      waits=[tq, tcopy, tc_, t_free[pp]])
                td = P.dve(_C("tensor_tensor", out=t2[pp][:, 0:T], in0=b1.t[:, 0:T], in1=sin[:, 0:T], op=ALU.mult),
                           waits=[tpm, ts_], sig=True)
                b0.free = td
                b1.free = td
                qs_ = nq % 4
                tp_ = P.pool(_C("tensor_tensor", out=qst[qs_][:, 0:T], in0=t1[pp][:, 0:T], in1=t2[pp][:, 0:T],
                                op=ALU.add), waits=[td, q_free[qs_]], sig=True)
                t_free[pp] = tp_
                dst = (S.QT if m % 2 == 0 else S.KT)[m // 2][:, t0:t0 + T]
                q_free[qs_] = P.dma("gpsimd", dst, qst[qs_][:, 0:T], qsem[qs_], waits=[tp_])
                nq += 1
                cs_free[ti % 2] = td

            for m in range(32):
                s, buf, lt = ws.next()
                pp = m % 2
                b0, b1 = qkb[pp]
                tq = None
                for c in range(DC):
                    tq = P.pe(_C("matmul", b0.t[:, 0:T], lhsT=buf[:, c * 128:(c + 1) * 128],
                                 rhs=B.uT[:, c, 0:T], start=(c == 0), stop=(c == DC - 1)),
                              waits=[lt, B.uT_ready, b0.free] if c == 0 else (), sig=(c == DC - 1))
                ws.release(s, tq)
                tcopy = P.act(_C("activation", out=qsb[pp][:, 0:T], in_=b0.t[:, 0:T], func=AF.Copy),
                              waits=[tq, qsb_free[pp]], sig=True)
                if PIPE_QK:
                    if pend is not None:
                        finish(*pend)
                    pend = (m, pp, tq, tcopy)
                else:
                    finish(m, pp, tq, tcopy)
            if PIPE_QK:
                finish(*pend)
            for s in range(S4):
                vs_ = nv % 2
                tev = None
                for n_ in range(4):
                    bank = vb[n_ % 2]
                    tm = None
                    for c in range(DC):
                        tm = P.pe(_C("matmul",
                            bank.t[:, :], lhsT=B.uT[:, c, s * 128:(s + 1) * 128],
                            rhs=wv[:, c, n_ * 512:(n_ + 1) * 512], start=(c == 0), stop=(c == DC - 1)),
                            waits=[twv, B.uT_ready, bank.free] if c == 0 else (), sig=(c == DC - 1))
                    tlast = tm
                    tev = P.act(_C("activation",
                        out=vst[vs_][:, n_ * 512:(n_ + 1) * 512], in_=bank.t[:, :], func=AF.Copy),
                        waits=[tm, v_free[vs_]] if n_ == 0 else [tm], sig=True)
                    bank.free = tev
                r0 = t0 + s * 128
                v_free[vs_] = P.dma("gpsimd", S.V[r0:r0 + 128, :], vst[vs_][:], vsem[vs_], waits=[tev])
                nv += 1
            B.uT_free = tlast
        P.barrier()
        P.emit()

    import os
    if os.environ.get("STOP_AFTER_QKV"):
        return
    with ExitStack() as es:
        sb = mk_sb(nc, es)
        for b in G.PS:
            b.free = None
        KTs = [sb(f"KTs{k}", [128, NT], BF16) for k in range(2)]
        QTs = [sb(f"QTs{k}", [128, NT], BF16) for k in range(2)]
        Vs = [sb(f"Vs{k}", [128, NKC, 132], BF16) for k in range(2)]
        NPB = 6
        Pb = [sb(f"Pb{k}", [128, 2, 512], BF16) for k in range(NPB)]
        aost = [sb(f"aost{k}", [128, 512], BF16) for k in range(2)]
        lamp = sb("lamp", [128, 4, 64], F32)
        lsm = sb("lsm", [128, 8], F32)
        gsub = sb("gsub", [128, 1], F32)
        mhalf = sb("mhalf", [128, 1], F32)
        ep = sb("ep", [128, 4, 8], F32)
        accs = [sb(f"accs{k}", [128, 3 * 396], F32) for k in range(2)]
        cvl = Converter(G, es, "late", 2, ("vector", "gpsimd")) if G.conv else None
        nkit = 0
        accs_free = [None, None]
        nacc = 0
        tmpo = [sb(f"tmpo{k}", [128, 128], F32) for k in range(2)]
        obuf = [sb(f"obuf{k}", [128, 128], F32) for k in range(2)]
        sqb = [sb(f"sqb{k}", [128, 128], F32) for k in range(2)]
        onb8 = [[sb(f"onb{k}_{q}", [128, 128], BF16) for q in range(4)] for k in range(2)]
        onb_free = [None, None]
        dq = []
        hsem = [G.sems[0], G.sems[1]]
        aosem = [G.sems[2], G.sems[3]]
        tl = P.dma("gpsimd", lamp[:].rearrange("p a b -> p (a b)"),
                   I.lam.rearrange("a b -> (a b)").partition_broadcast(128), G.sems[4])
        tg = P.dma("gpsimd", gsub[:], I.subg, G.sems[5])
        P.dve(_C("memset", mhalf[:], -0.5))
        tk = None
        for j in range(2):
            tk = P.dve(_C("tensor_tensor", out=lamp[:, 2 * j, :], in0=lamp[:, 2 * j, :],
                                                      in1=lamp[:, 2 * j + 1, :], op=ALU.mult), waits=[tl], sig=True)
            tk = P.dve(_C("tensor_reduce", out=lsm[:, j:j + 1], in_=lamp[:, 2 * j, :], axis=AX.X, op=ALU.add),
                       waits=[tk], sig=True)
        tk = P.act(_C("activation", out=lsm[:, 2:4], in_=lsm[:, 0:2], func=AF.Exp), waits=[tk], sig=True)
        tk = P.dve(_C("tensor_tensor", out=lsm[:, 4:5], in0=lsm[:, 3:4], in1=lsm[:, 2:3], op=ALU.subtract),
                   waits=[tk], sig=True)
        tk = P.dve(_C("tensor_scalar", out=lsm[:, 5:6], in0=lsm[:, 4:5], scalar1=-LAMBDA_INIT, scalar2=None,
                                             op0=ALU.add), waits=[tk], sig=True)
        nlam_ready = tk
        tgs = P.dve(_C("tensor_scalar", out=gsub[:], in0=gsub[:], scalar1=1.0 - LAMBDA_INIT, scalar2=None,
                                              op0=ALU.mult), waits=[tg], sig=True)
        for k in range(2):
            P.dve(_C("memset", Vs[k][:, :, 128:132], 1.0))
        P.barrier()
        sbank = [(G.PS[0], G.PS[1]), (G.PS[2], G.PS[3])]
        accb = [G.PS[4], G.PS[5], G.PS[6]]
        tpbank = G.PS[7]
        acc_loc = [(accb[a // 3], (a % 3) * 132) for a in range(8)]
        qblocks = [(0, CTX, 2)] + [(CTX + 512 * j, 512, NKC) for j in range(G.SLAT // 512)]
        h_free = [None, None]
        pb_free = [None] * NPB
        npb = 0
        ao_free = [None, None]
        nao = 0
        acc_free = None
        eo_free = [None, None]
        nep = 0

        def load_head(h):
            k = h % 2
            P.dma("gpsimd", KTs[k][:], S.KT[h], hsem[k], waits=[h_free[k]])
            P.dma("gpsimd", QTs[k][:], S.QT[h], hsem[k])
            return P.dma("gpsimd", Vs[k][:, :, 0:128],
                         S.V[:, h * 128:(h + 1) * 128].rearrange("(c p) v -> p c v", p=128), hsem[k])

        def part2(h_, q0_, QN_, QS_, slot, t_on):
            nonlocal nao
            ao = aost[nao % 2]
            tt = None
            for qs in range(QS_):
                tpv = tpbank.t[:, qs * 64:(qs + 1) * 64].bitcast(BF16)
                tt = P.pe(_C("transpose", out=tpv, in_=onb8[slot][qs][:], identity=G.identB[:]),
                          waits=[t_on[qs], tpbank.free] if qs == 0 else [t_on[qs]], sig=(qs == QS_ - 1))
            tcp = P.dve(_C("tensor_scalar", out=ao[:, 0:QN_], in0=tpbank.t[:, 0:QS_ * 64].bitcast(BF16),
                           scalar1=gsub[:, 0:1], scalar2=None, op0=ALU.mult),
                        waits=[tt, tgs, ao_free[nao % 2]], sig=True)
            onb_free[slot] = tt
            tpbank.free = tcp
            ao_free[nao % 2] = P.dma("gpsimd", S.AOT[h_][:, q0_:q0_ + QN_], ao[:, 0:QN_], aosem[nao % 2], waits=[tcp])
            nao += 1

        hl = load_head(0)
        for h in range(NH):
            k = h % 2
            hl_next = load_head(h + 1) if h + 1 < NH else None
            KT, QT, V = KTs[k], QTs[k], Vs[k]
            last_pe = None
            for (q0, QN, nkc) in qblocks:
                QS = QN // 128
                pend = None

                def pv(kc, pi, te):
                    nonlocal last_pe
                    tm = None
                    started = set()
                    for e_ in range(2):
                        for qs in range(QS):
                            bank, col = acc_loc[e_ * 4 + qs]
                            first = (kc == 0 and id(bank) not in started)
                            started.add(id(bank))
                            tm = P.pe(_C("matmul",
                                bank.t[:, col:col + 129], lhsT=Pb[pi][:, e_, qs * 128:(qs + 1) * 128],
                                rhs=V[:, kc, 0:129], start=first, stop=(kc == nkc - 1), skip_group_check=True),
                                waits=[te, acc_free] if (kc == 0 and e_ == 0 and qs == 0) else [te],
                                sig=(e_ == 1 and qs == QS - 1))
                    pb_free[pi] = tm
                    last_pe = tm

                for kc in range(nkc):
                    pp = kc % 2
                    tsc = None
                    for e_ in range(2):
                        bank = sbank[pp][e_]
                        tsc = P.pe(_C("matmul",
                            bank.t[:, 0:QN], lhsT=KT[e_ * 64:(e_ + 1) * 64, kc * 128:(kc + 1) * 128],
                            rhs=QT[e_ * 64:(e_ + 1) * 64, q0:q0 + QN], start=True, stop=True),
                            waits=[hl, bank.free], sig=(e_ == 1))
                    pi = npb % NPB
                    npb += 1
                    te = None
                    for e_ in range(2):
                        bank = sbank[pp][e_]
                        te = P.act(_C("activation",
                            out=Pb[pi][:, e_, 0:QN], in_=bank.t[:, 0:QN], func=AF.Exp, scale=0.125),
                            waits=[tsc, pb_free[pi]] if e_ == 0 else (), sig=True)
                        bank.free = te
                    if pend is not None:
                        pv(*pend)
                    pend = (kc, pi, te)
                    if kc == min(20, nkc - 1) and dq:
                        part2(*dq.pop(0))
                    nkit += 1
                    if cvl is not None and nkit % 16 == 8:
                        cvl.step(1)
                pv(*pend)
                reads = []
                slot = nacc % 2
                asb = accs[slot]
                tev = None
                for bi_ in range(3):
                    tev = P.dve(_C("tensor_copy", out=asb[:, bi_ * 396:(bi_ + 1) * 396], in_=accb[bi_].t[:, 0:396]),
                                waits=[last_pe, accs_free[slot]] if bi_ == 0 else (), sig=(bi_ == 2))
                reads.append(tev)
                last_acc = tev
                t_on = []
                last_to = None
                for qs in range(QS):
                    b0, c0 = asb, qs * 132
                    b1, c1 = asb, (4 + qs) * 132
                    j = nep % 2
                    nep += 1
                    sc = ep[:, qs, :]
                    t_ = P.dve(_C("reciprocal", out=sc[:, 0:1], in_=b0[:, c0 + 128:c0 + 129]),
                               waits=[last_acc], sig=True)
                    t_ = P.dve(_C("reciprocal", out=sc[:, 1:2], in_=b1[:, c1 + 128:c1 + 129]), waits=[t_], sig=True)
                    t_ = P.dve(_C("tensor_tensor", out=sc[:, 2:3], in0=sc[:, 1:2], in1=lsm[:, 5:6], op=ALU.mult),
                               waits=[t_, nlam_ready], sig=True)
                    t_ = P.dve(_C("tensor_scalar", out=tmpo[j][:], in0=b1[:, c1:c1 + 128], scalar1=sc[:, 2:3],
                                  scalar2=None, op0=ALU.mult), waits=[t_], sig=True)
                    to = P.dve(_C("scalar_tensor_tensor", out=obuf[j][:], in0=b0[:, c0:c0 + 128], scalar=sc[:, 0:1],
                                  in1=tmpo[j][:], op0=ALU.mult, op1=ALU.add), waits=[t_], sig=True)
                    last_to = to
                    t_ = P.dve(_C("tensor_tensor", out=sqb[j][:], in0=obuf[j][:], in1=obuf[j][:], op=ALU.mult),
                               waits=[to], sig=True)
                    t_ = P.dve(_C("tensor_reduce", out=sc[:, 3:4], in_=sqb[j][:], axis=AX.X, op=ALU.add),
                               waits=[t_], sig=True)
                    t_ = P.dve(_C("tensor_scalar", out=sc[:, 4:5], in0=sc[:, 3:4], scalar1=1.0 / 128.0,
                                  scalar2=LN_EPS, op0=ALU.mult, op1=ALU.add), waits=[t_], sig=True)
                    t_ = P.pool(_C("tensor_tensor", out=sc[:, 5:6], in0=sc[:, 4:5], in1=mhalf[:], op=ALU.pow),
                                waits=[t_], sig=True)
                    t_ = P.dve(_C("tensor_scalar", out=onb8[slot][qs][:], in0=obuf[j][:], scalar1=sc[:, 5:6],
                                  scalar2=None, op0=ALU.mult), waits=[t_, onb_free[slot]], sig=True)
                    t_on.append(t_)
                acc_free = reads
                accs_free[slot] = last_to
                nacc += 1
                dq.append((h, q0, QN, QS, slot, t_on))
            h_free[k] = last_pe
            hl = hl_next
        while dq:
            part2(*dq.pop(0))
        if cvl is not None:
            cvl.step(10 ** 6)
        P.barrier()
        P.emit()

    with ExitStack() as es:
        sb = mk_sb(nc, es)
        B = alloc_common(G, es, with_xin=False, nx=2)
        aT = [sb(f"aT{k}", [128, NH, 512], BF16) for k in range(2)]
        asem = [G.sems[4], G.sems[5]]
        a_free = [None, None]
        load_ln(G, B, i * 3 + sl)
        for (t0, T, w) in tiles:
            ws.push([S.wo[m] for m in range(16)])
        prev = None
        for ti, (t0, T, w) in enumerate(tiles):
            k = ti % 2
            ta = P.dma("gpsimd", aT[k][:, :, 0:T], S.AOT[:, :, t0:t0 + T].rearrange("h p t -> p h t"), asem[k],
                       waits=[a_free[k]])
            st = OutProj(G, B, NH, X_in, X_out, t0, T, modcol(i, sl, 2, 0), w, t0, k=ti % 2)
            st.begin()
            if prev is not None:
                prev.end()
            tm = None
            for m in range(16):
                s, buf, lt = ws.next()
                tm = st.slab(m, buf, lt, aT[k], ta)
                ws.release(s, tm)
            a_free[k] = tm
            prev = st
        prev.end()
        P.barrier()
        P.emit()


def phase_lru(G, X_in, X_out):
    nc, P, I, S = G.nc, G.P, G.I, G.S
    NT, tiles, lat_tiles = G.NT, G.tiles, G.lat_tiles
    ws = G.ws
    i, sl = 1, 1
    SL = G.SLAT
    with ExitStack() as es:
        sb = mk_sb(nc, es)
        B = Ctx()
        B.xin = sb("xin", [128, 4, D], F32)
        B.uT = sb("uT", [128, DC, 512], BF16)
        B.xin_sem = G.sems[0]
        B.xin_free = None
        B.uT_free = None
        B.tpb = [G.PS[6], G.PS[7]]
        for b in G.PS:
            b.free = None
        NST = 4
        ga = [sb(f"ga{k}", [128, 512], F32) for k in range(2)]
        gb_ = [sb(f"gb{k}", [128, 512], F32) for k in range(2)]
        gst = [sb(f"gst{k}", [128, 512], BF16) for k in range(NST)]
        xst = [sb(f"xst{k}", [128, 512], F32) for k in range(NST)]
        gsem = [G.sems[1 + k] for k in range(NST)]
        xsem = [G.sems[5 + k] for k in range(NST)]
        g_free = [None] * NST
        x_free = [None] * NST
        gab_free = [None, None]
        pb = [G.PS[0], G.PS[1], G.PS[2], G.PS[3]]
        for (t0, T, w) in tiles:
            ws.push([S.win[m] for m in range(40)])
        ng = 0
        nx = 0
        for (t0, T, w) in tiles:
            prologue(G, B, X_in, t0, T, i, sl, w)
            tm = None
            for m in range(40):
                s, buf, lt = ws.next()
                bank = pb[m % 4]
                for c in range(DC):
                    tm = P.pe(_C("matmul",
                        bank.t[:, 0:T], lhsT=buf[:, c * 128:(c + 1) * 128], rhs=B.uT[:, c, 0:T],
                        start=(c == 0), stop=(c == DC - 1)),
                        waits=[lt, B.uT_ready, bank.free] if c == 0 else (), sig=(c == DC - 1))
                ws.release(s, tm)
                if m < RC:
                    j = ng % 2
                    k_ = ng % NST
                    ng += 1
                    t_ = P.act(_C("activation", out=ga[j][:, 0:T], in_=bank.t[:, 0:T], func=AF.Square),
                               waits=[tm, gab_free[j]], sig=True)
                    P.dve(_C("tensor_scalar", out=ga[j][:, 0:T], in0=ga[j][:, 0:T], scalar1=0.044715,
                                                         scalar2=1.0, op0=ALU.mult, op1=ALU.add), waits=[t_])
                    t_ = P.dve(_C("tensor_tensor", out=ga[j][:, 0:T], in0=ga[j][:, 0:T],
                                                                         in1=bank.t[:, 0:T], op=ALU.mult), sig=True)
                    t_ = P.act(_C("activation", out=gb_[j][:, 0:T], in_=ga[j][:, 0:T], func=AF.Sigmoid,
                                                           scale=1.5957691216057308), waits=[t_], sig=True)
                    t_ = P.dve(_C("tensor_tensor",
                        out=gst[k_][:, 0:T], in0=gb_[j][:, 0:T], in1=bank.t[:, 0:T], op=ALU.mult),
                        waits=[t_, g_free[k_]], sig=True)
                    bank.free = t_
                    gab_free[j] = t_
                    g_free[k_] = P.dma("gpsimd", S.GT[m][:, t0:t0 + T], gst[k_][:, 0:T], gsem[k_], waits=[t_])
                else:
                    k_ = nx % NST
                    nx += 1
                    t_ = P.act(_C("activation", out=xst[k_][:, 0:T], in_=bank.t[:, 0:T], func=AF.Copy),
                               waits=[tm, x_free[k_]], sig=True)
                    bank.free = t_
                    x_free[k_] = P.dma("gpsimd", S.XRT[m - RC][:, t0:t0 + T], xst[k_][:, 0:T], xsem[k_], waits=[t_])
            B.uT_free = tm
        P.barrier()
        P.emit()

    with ExitStack() as es:
        sb = mk_sb(nc, es)
        for b in G.PS:
            b.free = None
        HX = sb("HX", [128, NT], F32)
        xr = sb("xr", [128, 2, NT], F32)
        xrb = sb("xrb", [128, 2, NT], BF16)
        R = sb("R", [128, NT], F32)
        IG = sb("IG", [128, NT], F32)
        MM = sb("MM", [128, NT], F32)
        hsum = sb("hsum", [128, SL], F32)
        Gs = sb("Gs", [128, SL], BF16)
        hgo = sb("hgo", [128, SL], BF16)
        convw = sb("convw", [128, RC, 4], F32)
        lvec = sb("lvec", [128, 6, RC], F32)
        apar = sb("apar", [128, 2, RC], F32)
        cvec = sb("cvec", [128, 2, 2, RC], F32)
        tl1 = P.dma("gpsimd", convw[:], I.convw, G.sems[0])
        tl2 = P.dma("gpsimd", lvec[:], I.lvec, G.sems[0])
        tl3 = P.dma("gpsimd", apar[:], I.apar, G.sems[0])
        t_ = P.act(_C("activation", out=cvec[:, :, 0, :], in_=apar[:], func=AF.Exp, scale=-1.0), waits=[tl3], sig=True)
        t_ = P.act(_C("activation", out=cvec[:, :, 0, :], in_=cvec[:, :, 0, :], func=AF.Ln, bias=1.0),
                   waits=[t_], sig=True)
        t_ = P.dve(_C("tensor_scalar", out=cvec[:, :, 0, :], in0=cvec[:, :, 0, :], scalar1=-8.0, scalar2=None,
                                             op0=ALU.mult), waits=[t_], sig=True)
        c_ready = P.dve(_C("tensor_scalar", out=cvec[:, :, 1, :], in0=cvec[:, :, 0, :], scalar1=2.0, scalar2=None,
                                                  op0=ALU.mult), waits=[t_], sig=True)
        xsem, gsem_, osem = G.sems[1], G.sems[2], G.sems[3]
        ws.push([S.wgate[hb] for hb in range(10)])
        tblocks = [(t, min(512, NT - t)) for t in range(0, NT, 512)]
        segs = [(0, CTX), (CTX, NT)]
        gbanks = [G.PS[0], G.PS[1], G.PS[2], G.PS[3]]
        ngb = 0
        hx_free = None
        xr_free = None
        xrb_free = None
        out_free = None
        g_free = None
        for hb in range(10):
            s, buf, lt = ws.next()
            tcv = None
            tcb = None
            for ic in range(2):
                ch = hb * 2 + ic
                tx = P.dma("gpsimd", HX[:], S.XRT[ch], xsem, waits=[hx_free])
                for (a_, b_) in segs:
                    P.dve(_C("tensor_scalar",
                        out=xr[:, ic, a_:b_], in0=HX[:, a_:b_], scalar1=convw[:, ch, 2:3], scalar2=lvec[:, 0, ch:ch + 1],
                        op0=ALU.mult, op1=ALU.add), waits=[tx, tl1, tl2, xr_free])
                    for (j_, d_) in ((1, 1), (0, 2)):
                        P.dve(_C("scalar_tensor_tensor",
                            out=xr[:, ic, a_ + d_:b_], in0=HX[:, a_:b_ - d_], scalar=convw[:, ch, j_:j_ + 1],
                            in1=xr[:, ic, a_ + d_:b_], op0=ALU.mult, op1=ALU.add))
                    tcv = P.dve(_C("scalar_tensor_tensor",
                        out=xr[:, ic, a_:b_ - 1], in0=HX[:, a_ + 1:b_], scalar=convw[:, ch, 3:4],
                        in1=xr[:, ic, a_:b_ - 1], op0=ALU.mult, op1=ALU.add), sig=True)
                hx_free = tcv
                tcb = P.act(_C("activation", out=xrb[:, ic, :], in_=xr[:, ic, :], func=AF.Copy),
                            waits=[tcv, xrb_free], sig=True)
            last_use = None
            for oc in range(2):
                ch = hb * 2 + oc
                tg_ = P.dma("gpsimd", Gs[:], S.GT[ch][:, CTX:NT], gsem_, waits=[g_free])
                for d in range(2):
                    tsr = None
                    tsi = None
                    for (tb0, tbn) in tblocks:
                        for gt in range(2):
                            bank = gbanks[ngb % 4]
                            ngb += 1
                            tm = None
                            for ic in range(2):
                                col = (((d * 2 + gt) * 2 + ic) * 2 + oc) * 128
                                tm = P.pe(_C("matmul",
                                    bank.t[:, 0:tbn], lhsT=buf[:, col:col + 128], rhs=xrb[:, ic, tb0:tb0 + tbn],
                                    start=(ic == 0), stop=(ic == 1)),
                                    waits=[lt, tcb, bank.free] if ic == 0 else (), sig=(ic == 1))
                            dst = R if gt == 0 else IG
                            bcol = (1 + d) if gt == 0 else (3 + d)
                            ta = P.act(_C("activation",
                                out=dst[:, tb0:tb0 + tbn], in_=bank.t[:, 0:tbn], func=AF.Sigmoid,
                                bias=lvec[:, bcol, ch:ch + 1]), waits=[tm, tl2, last_use], sig=True)
                            bank.free = ta
                            if gt == 0:
                                tsr = ta
                            else:
                                tsi = ta
                    last_gate_pe = tm
                    P.act(_C("activation", out=MM[:], in_=R[:], func=AF.Exp,
                                                            scale=cvec[:, d, 1, ch:ch + 1]), waits=[c_ready, last_use])
                    P.act(_C("activation", out=MM[:], in_=MM[:], func=AF.Sqrt, scale=-1.0, bias=1.0))
                    ta_ = P.act(_C("activation", out=R[:], in_=R[:], func=AF.Exp,
                                                                  scale=cvec[:, d, 0, ch:ch + 1]), sig=True)
                    P.dve(_C("tensor_tensor", out=IG[:], in0=IG[:], in1=MM[:], op=ALU.mult), waits=[ta_, tsi])
                    tb_ = P.dve(_C("tensor_tensor", out=IG[:], in0=IG[:], in1=xr[:, oc, :], op=ALU.mult), sig=True)
                    if d == 0:
                        P.dve(_C("tensor_tensor_scan", out=HX[:], data0=R[:], data1=IG[:], initial=0.0,
                                                             op0=ALU.mult, op1=ALU.add), waits=[tb_, hx_free])
                        tsc = P.dve(_C("tensor_copy", out=hsum[:], in_=HX[:, CTX:NT]), waits=[out_free], sig=True)
                    else:
                        t1_ = P.dve(_C("tensor_tensor_scan",
                            out=HX[:, 0:CTX][:, ::-1], data0=R[:, 0:CTX][:, ::-1], data1=IG[:, 0:CTX][:, ::-1],
                            initial=0.0, op0=ALU.mult, op1=ALU.add), waits=[tb_], sig=True)
                        t2_ = P.dve(_C("scalar_tensor_tensor",
                            out=IG[:, NT - 1:NT], in0=R[:, NT - 1:NT], scalar=HX[:, 0:1], in1=IG[:, NT - 1:NT],
                            op0=ALU.mult, op1=ALU.add), waits=[t1_], sig=True)
                        t3_ = P.dve(_C("tensor_tensor_scan",
                            out=HX[:, CTX:NT][:, ::-1], data0=R[:, CTX:NT][:, ::-1], data1=IG[:, CTX:NT][:, ::-1],
                            initial=0.0, op0=ALU.mult, op1=ALU.add), waits=[t2_], sig=True)
                        tsc = P.dve(_C("tensor_tensor", out=hsum[:], in0=hsum[:], in1=HX[:, CTX:NT], op=ALU.add),
                                    waits=[t3_], sig=True)
                    last_use = tsc
                    hx_free = tsc
                to_ = P.pool(_C("tensor_tensor", out=hgo[:], in0=hsum[:], in1=Gs[:], op=ALU.mult),
                             waits=[tsc, tg_, out_free], sig=True)
                g_free = to_
                out_free = P.dma("gpsimd", S.HGT[ch][:, CTX:NT], hgo[:], osem, waits=[to_])
            ws.release(s, last_gate_pe)
            xr_free = last_use
            xrb_free = last_gate_pe
        P.barrier()
        P.emit()

    with ExitStack() as es:
        sb = mk_sb(nc, es)
        B = alloc_common(G, es, with_xin=False, nx=2)
        aT = [sb(f"hT{k}", [128, RC, 512], BF16) for k in range(2)]
        asem = [G.sems[4], G.sems[5]]
        a_free = [None, None]
        load_ln(G, B, i * 3 + sl)
        for (t0, T, w) in lat_tiles:
            ws.push([S.wout[m] for m in range(16)])
        prev = None
        for ti, (t0, T, w) in enumerate(lat_tiles):
            k = ti % 2
            ta = P.dma("gpsimd", aT[k][:, :, 0:T], S.HGT[:, :, t0:t0 + T].rearrange("h p t -> p h t"), asem[k],
                       waits=[a_free[k]])
            st = OutProj(G, B, RC, X_in, X_out, t0, T, modcol(i, sl, 2, 0), w, t0, k=ti % 2)
            st.begin()
            if prev is not None:
                prev.end()
            tm = None
            for m in range(16):
                s, buf, lt = ws.next()
                tm = st.slab(m, buf, lt, aT[k], ta)
                ws.release(s, tm)
            a_free[k] = tm
            prev = st
        prev.end()
        P.barrier()
        P.emit()


def _tile_w(W, KC, MC):
    return np.ascontiguousarray(W.reshape(KC, 128, MC, 128).transpose(2, 1, 0, 3)).reshape(MC, 128, KC * 128)


def _fm(v, n):
    return np.ascontiguousarray(np.asarray(v).reshape(n, 128).T)


def rope_tables(S_LAT):
    GRID_W = 64
    t = np.arange(S_LAT)
    row = (t // GRID_W).astype(np.float32)
    col = (t % GRID_W).astype(np.float32)
    inv = (np.float32(10000.0) ** (-np.arange(0, 32, 2, dtype=np.float32) / np.float32(32))).astype(np.float32)
    cos = np.ones((128, CTX + S_LAT), np.float32)
    sin = np.zeros((128, CTX + S_LAT), np.float32)
    for e in range(2):
        for d in range(64):
            pos = row if d < 32 else col
            dd = d % 32
            j = dd % 16
            ang = (pos * inv[j]).astype(np.float32)
            cos[e * 64 + d, CTX:] = np.cos(ang)
            sgn = -1.0 if dd < 16 else 1.0
            sin[e * 64 + d, CTX:] = sgn * np.sin(ang)
    return cos, sin


def _perm_cols():
    p = np.zeros(128, np.int64)
    for e in range(2):
        for d in range(64):
            dd = d % 32
            partner = d + 16 if dd < 16 else d - 16
            p[e * 64 + d] = e * 64 + partner
    return p


def prep_shared(inp, S_LAT):
    f = lambda a: np.asarray(a, dtype=np.float32)
    sh = {}
    wada = f(inp["w_ada"])
    sh["wada"] = np.ascontiguousarray(wada.reshape(2, 16, 128, 36, 512).transpose(0, 3, 2, 1, 4)).reshape(2, 36, 128, 16 * 512)
    bada = f(inp["b_ada"])
    sh["bada"] = np.ascontiguousarray(bada.reshape(2, 144, 128).transpose(2, 0, 1))
    sh["lng"] = f(inp["ln_g"]).reshape(6, D)
    sh["lnb"] = f(inp["ln_b"]).reshape(6, D)
    wg, wu, wd = f(inp["ffn_w_gate"]), f(inp["ffn_w_up"]), f(inp["ffn_w_down"])
    wgu = np.empty((4, FC, 128, 4096), np.float32)
    wdn = np.empty((4, DC, 128, FC * 128), np.float32)
    for i in range(2):
        for j in range(2):
            g = _tile_w(wg[i, j], DC, FC)
            u = _tile_w(wu[i, j], DC, FC)
            wgu[i * 2 + j, :, :, 0:2048] = g
            wgu[i * 2 + j, :, :, 2048:4096] = u
            wdn[i * 2 + j] = _tile_w(wd[i, j], FC, DC)
    sh["wgu"], sh["wdn"] = wgu, wdn
    wqkv = f(inp["attn_w_qkv"])[0]
    perm = _perm_cols()
    wqk = np.empty((32, 128, 4096), np.float32)
    for h in range(NH):
        for t_, off in ((0, 0), (1, D)):
            Wh = wqkv[:, off + h * 128: off + (h + 1) * 128]
            wqk[h * 2 + t_, :, 0:2048] = _tile_w(Wh, DC, 1)[0]
            wqk[h * 2 + t_, :, 2048:4096] = _tile_w(Wh[:, perm], DC, 1)[0]
    sh["wqk"] = wqk
    Wv = wqkv[:, 2 * D:3 * D]
    sh["wv"] = np.ascontiguousarray(Wv.reshape(DC, 128, D).transpose(1, 0, 2)).reshape(128, 8, 4096).transpose(1, 0, 2).copy()
    sh["wo"] = _tile_w(f(inp["attn_w_o"])[0], DC, DC)
    sh["lam"] = np.stack([f(inp["attn_lambda_q1"])[0], f(inp["attn_lambda_k1"])[0],
                          f(inp["attn_lambda_q2"])[0], f(inp["attn_lambda_k2"])[0]])
    sh["subg"] = f(inp["attn_subln_g"])[0].reshape(128, 1).copy()
    cos, sin = rope_tables(S_LAT)
    sh["cos"], sh["sin"] = cos, sin
    sh["win"] = _tile_w(f(inp["lru_w_in"])[0], DC, 40)
    cw = f(inp["lru_conv_w"])[0]
    sh["convw"] = np.ascontiguousarray(cw.reshape(4, RC, 128).transpose(2, 1, 0))
    lv = np.empty((128, 6, RC), np.float32)
    lv[:, 0] = _fm(f(inp["lru_conv_b"])[0], RC)
    ba, bi = f(inp["lru_b_a"])[0], f(inp["lru_b_i"])[0]
    lv[:, 1] = _fm(ba[0], RC)
    lv[:, 2] = _fm(ba[1], RC)
    lv[:, 3] = _fm(bi[0], RC)
    lv[:, 4] = _fm(bi[1], RC)
    lv[:, 5] = 0.0
    sh["lvec"] = lv
    ap_ = f(inp["lru_a_param"])[0]
    sh["apar"] = np.stack([_fm(ap_[0], RC), _fm(ap_[1], RC)], axis=1).copy()
    wa, wi = f(inp["lru_w_a"])[0], f(inp["lru_w_i"])[0]
    wgate = np.empty((10, 128, 2048), np.float32)
    for hb in range(10):
        for d in range(2):
            for gt, Wt in ((0, wa), (1, wi)):
                for ic in range(2):
                    for oc in range(2):
                        col = (((d * 2 + gt) * 2 + ic) * 2 + oc) * 128
                        wgate[hb, :, col:col + 128] = Wt[d, hb, ic * 128:(ic + 1) * 128, oc * 128:(oc + 1) * 128]
    sh["wgate"] = wgate
    sh["wout"] = _tile_w(f(inp["lru_w_out"])[0], RC, DC)
    sh["eye"] = np.eye(128, dtype=np.float32)
    pm = np.zeros((128, 128), np.float32)
    pm[perm, np.arange(128)] = 1.0
    sh["permm"] = pm
    return sh


def prep_core(inp, b):
    f = lambda a: np.asarray(a, dtype=np.float32)
    d = {}
    d["x0"] = np.concatenate([f(inp["ctx"])[b], f(inp["x"])[b]], axis=0)
    cf = np.empty((128, 2, 16), np.float32)
    cf[:, 0] = _fm(f(inp["c"])[b], 16)
    cf[:, 1] = _fm(f(inp["c_ctx"]), 16)
    d["cfm"] = cf
    return d


_CACHE = {}


def kernel(**inputs):
    x = np.asarray(inputs["x"])
    Bn, S_LAT, _ = x.shape
    key = S_LAT
    if key not in _CACHE:
        _CACHE[key] = build(S_LAT)
    nc = _CACHE[key]
    sh = prep_shared(inputs, S_LAT)
    in_maps = []
    for b in range(Bn):
        m = dict(sh)
        m.update(prep_core(inputs, b))
        in_maps.append(m)
    res = run_bass_kernel_spmd(nc, in_maps, core_ids=list(range(Bn)))
    return np.stack([np.asarray(r["out"]) for r in res.results], axis=0).astype(np.float32)
```

```python
import math
from contextlib import ExitStack
import numpy as np
import concourse.bass as bass
import concourse.mybir as mybir
from concourse.bass_utils import run_bass_kernel_spmd

F32 = mybir.dt.float32
BF16 = mybir.dt.bfloat16
AF = mybir.ActivationFunctionType
ALU = mybir.AluOpType
AX = mybir.AxisListType

D = 2048
DC = 16
DFF = 5632
FC = 44
CTX = 256
DRNN = 2560
RC = 20
NH = 16
ALPHA = 2.0 ** 0.5
LN_EPS = 1e-6
LAMBDA_INIT = 0.8 - 0.6 * math.exp(-0.3 * 0)
WSLOT = 5632
NWBUF = 4
PIPE_QK = True
ENGS = ("tensor", "vector", "scalar", "gpsimd", "sync")


def _C(name, *a, **k):
    return (name, a, k)


class DSem:
    def __init__(self, h):
        self.h = h
        self.n = 0

    def tok(self):
        return (self.h, self.n)


class Op:
    __slots__ = ("fn", "ws", "inc", "dma")

    def __init__(self, fn, ws, inc, dma):
        self.fn, self.ws, self.inc, self.dma = fn, ws, inc, dma


class Prog:
    def __init__(self, nc):
        self.nc = nc
        self.ops = {e: [] for e in ENGS}
        self.esem = {e: nc.alloc_semaphore("es_" + e) for e in ENGS if e != "sync"}
        self.ecnt = {e: 0 for e in ENGS}
        self.waited = {e: {} for e in ENGS}
        self.dsems = []
        self.ninst = 0

    def new_sem(self, name=None):
        s = DSem(self.nc.alloc_semaphore(name or f"ds{len(self.dsems)}"))
        self.dsems.append(s)
        return s

    def _flat(self, toks, out):
        for t in toks:
            if t is None:
                continue
            if isinstance(t, list) or (isinstance(t, tuple) and (len(t) != 2 or isinstance(t[0], (tuple, list)) or t[0] is None)):
                self._flat(t, out)
            else:
                out.append(t)

    def _waits(self, eng, toks):
        fl = []
        self._flat(toks, fl)
        ws = []
        for sem, v in fl:
            if v <= 0:
                continue
            key = id(sem)
            if self.waited[eng].get(key, 0) >= v:
                continue
            self.waited[eng][key] = v
            ws.append((sem, v))
        return ws

    def op(self, eng, fn, waits=(), sig=False):
        ws = self._waits(eng, waits)
        tok = None
        inc = None
        if sig:
            self.ecnt[eng] += 1
            inc = (self.esem[eng], 1)
            tok = (self.esem[eng], self.ecnt[eng])
        self.ops[eng].append(Op(fn, ws, inc, False))
        self.ninst += 1
        return tok

    def pe(self, fn, waits=(), sig=False):
        return self.op("tensor", fn, waits, sig)

    def dve(self, fn, waits=(), sig=False):
        return self.op("vector", fn, waits, sig)

    def act(self, fn, waits=(), sig=False):
        return self.op("scalar", fn, waits, sig)

    def pool(self, fn, waits=(), sig=False):
        return self.op("gpsimd", fn, waits, sig)

    def dma(self, eng, out, in_, dsem, waits=()):
        ws = self._waits(eng, waits)
        dsem.n += 16
        self.ops[eng].append(Op(lambda e, o=out, i=in_: e.dma_start(out=o, in_=i), ws, (dsem.h, 16), True))
        self.ninst += 1
        return (dsem.h, dsem.n)

    def wait_only(self, eng, waits):
        ws = self._waits(eng, waits)
        if ws:
            self.ops[eng].append(Op(None, ws, None, False))

    def barrier(self):
        toks = []
        for e in ("tensor", "vector", "scalar", "gpsimd"):
            last = None
            for o in reversed(self.ops[e]):
                if o.fn is not None and not o.dma:
                    last = o
                    break
            if last is not None and last.inc is None:
                self.ecnt[e] += 1
                last.inc = (self.esem[e], 1)
            if self.ecnt[e] > 0:
                toks.append((self.esem[e], self.ecnt[e]))
        for s in self.dsems:
            if s.n > 0:
                toks.append((s.h, s.n))
        for e in ENGS:
            self.wait_only(e, toks)

    def emit(self):
        nc = self.nc
        ops = self.ops
        with nc.Block() as block:
            def body(ename):
                def f(e):
                    for o in ops[ename]:
                        for sem, v in o.ws:
                            e.wait_ge(sem, v)
                        if o.fn is None:
                            continue
                        ins = o.fn(e) if callable(o.fn) else getattr(e, o.fn[0])(*o.fn[1], **o.fn[2])
                        if o.inc is not None:
                            ins.then_inc(o.inc[0], o.inc[1])
                return f
            block.sync(body("sync"))
            block.tensor(body("tensor"))
            block.vector(body("vector"))
            block.scalar(body("scalar"))
            block.gpsimd(body("gpsimd"))
        self.ops = {e: [] for e in ENGS}


class Bank:
    def __init__(self, t):
        self.t = t
        self.free = None


class WStream:
    def __init__(self, P, nc):
        self.P = P
        self.slots = [nc.alloc_sbuf_tensor(f"wslot{i}", [128, WSLOT], BF16) for i in range(NWBUF)]
        self.sems = [P.new_sem(f"wsem{i}") for i in range(NWBUF)]
        self.free = [None] * NWBUF
        self.queue = []
        self.inflight = []
        self.gi = 0
        self.nrel = 0
        self.ready = None

    def push(self, srcs):
        self.queue.extend(srcs)

    def _issue(self):
        while self.queue and self.gi - self.nrel < NWBUF:
            src = self.queue.pop(0)
            s = self.gi % NWBUF
            self.gi += 1
            E = src.shape[1]
            tok = self.P.dma("sync", self.slots[s][:, 0:E], src, self.sems[s], waits=[self.free[s], self.ready])
            self.inflight.append((s, tok))

    def next(self):
        self._issue()
        s, tok = self.inflight.pop(0)
        return s, self.slots[s], tok

    def release(self, s, tok):
        self.free[s] = tok
        self.nrel += 1
        self._issue()


class Ctx:
    pass


_UID = [0]


def mk_sb(nc, es):
    _UID[0] += 1
    u = _UID[0]
    return lambda n, s, d: es.enter_context(nc.sbuf_tensor(f"{n}_u{u}", s, d))


def build(S_LAT=4096, dbg=False, phases=None):
    NT = CTX + S_LAT
    NKC = NT // 128
    nc = bass.Bass("TRN2", target_bir_lowering=False)
    P = Prog(nc)
    G = Ctx()
    G.nc, G.P, G.SLAT, G.NT, G.NKC = nc, P, S_LAT, NT, NKC
    tiles = [(0, CTX, 1)] + [(CTX + 512 * i, 512, 0) for i in range(S_LAT // 512)]
    lat_tiles = tiles[1:]
    G.tiles, G.lat_tiles = tiles, lat_tiles

    def din(name, shape, dt=F32):
        return nc.dram_tensor(name, list(shape), dt, kind="ExternalInput").ap()

    skind = "ExternalOutput" if dbg else "Internal"

    def dscr(name, shape, dt):
        return nc.dram_tensor(name, list(shape), dt, kind=skind).ap()

    I = Ctx()
    I.x0 = din("x0", [NT, D])
    I.cfm = din("cfm", [128, 2, 16])
    I.wada = din("wada", [2, 36, 128, 16 * 512])
    I.bada = din("bada", [128, 2, 144])
    I.lng = din("lng", [6, D])
    I.lnb = din("lnb", [6, D])
    I.wgu = din("wgu", [4, FC, 128, 4096])
    I.wdn = din("wdn", [4, DC, 128, FC * 128])
    I.wqk = din("wqk", [32, 128, 4096])
    I.wv = din("wv", [8, 128, 4096])
    I.wo = din("wo", [16, 128, 2048])
    I.lam = din("lam", [4, 64])
    I.subg = din("subg", [128, 1])
    I.cos = din("cos", [128, NT])
    I.sin = din("sin", [128, NT])
    I.win = din("win", [40, 128, 2048])
    I.convw = din("convw", [128, RC, 4])
    I.lvec = din("lvec", [128, 6, RC])
    I.apar = din("apar", [128, 2, RC])
    I.wgate = din("wgate", [10, 128, 2048])
    I.wout = din("wout", [16, 128, RC * 128])
    I.eye = din("eye", [128, 128])
    I.permm = din("permm", [128, 128])
    out = nc.dram_tensor("out", [S_LAT, D], F32, kind="ExternalOutput").ap()

    S = Ctx()
    S.XA = dscr("XA", [NT, D], F32)
    S.XB = dscr("XB", [NT, D], F32)
    S.wgu = dscr("s_wgu", [4, FC, 128, 4096], BF16)
    S.wdn = dscr("s_wdn", [4, DC, 128, FC * 128], BF16)
    S.wqk = dscr("s_wqk", [32, 128, 4096], BF16)
    S.wv = dscr("s_wv", [8, 128, 4096], BF16)
    S.wo = dscr("s_wo", [16, 128, 2048], BF16)
    S.win = dscr("s_win", [40, 128, 2048], BF16)
    S.wgate = dscr("s_wgate", [10, 128, 2048], BF16)
    S.wout = dscr("s_wout", [16, 128, RC * 128], BF16)
    S.QT = dscr("QT", [NH, 128, NT], BF16)
    S.KT = dscr("KT", [NH, 128, NT], BF16)
    S.V = dscr("V", [NT, D], BF16)
    S.AOT = dscr("AOT", [NH, 128, NT], BF16)
    S.GT = dscr("GT", [RC, 128, NT], BF16)
    S.XRT = dscr("XRT", [RC, 128, NT], F32)
    S.HGT = dscr("HGT", [RC, 128, NT], BF16)
    G.I, G.S, G.out = I, S, out

    G.ws = WStream(P, nc)
    G.identF = nc.alloc_sbuf_tensor("identF", [128, 128], F32)
    G.identB = nc.alloc_sbuf_tensor("identB", [128, 128], BF16)
    G.MOD = nc.alloc_sbuf_tensor("MOD", [128, 288, 2], F32)
    G.PSpair = [nc.alloc_psum_tensor(f"bankpair{i}", [128, 1024], F32) for i in range(4)]
    G.PS = [Bank(G.PSpair[i // 2][:, (i % 2) * 512:(i % 2 + 1) * 512]) for i in range(8)]
    G.sems = [P.new_sem(f"gs{i}") for i in range(24)]

    phases = phases or ["ada", "conv", "l0f0", "attn", "l0f2", "l1f0", "lru", "l1f2"]

    t = P.dma("sync", G.identF[:], I.eye, G.sems[0])
    P.dve(_C("tensor_copy", out=G.identB[:], in_=G.identF[:]), waits=[t])
    P.barrier()
    P.emit()

    G.conv = "conv" in phases
    if "ada" in phases:
        phase_ada(G, conv=G.conv)
    X0, XA, XB = I.x0, S.XA, S.XB
    if "l0f0" in phases:
        phase_ffn(G, 0, 0, X0, XA, tiles)
    if "attn" in phases:
        phase_attn(G, XA, XB)
    if "l0f2" in phases:
        phase_ffn(G, 0, 2, XB, XA, tiles)
    if "l1f0" in phases:
        phase_ffn(G, 1, 0, XA, XB, tiles)
    if "lru" in phases:
        phase_lru(G, XB, XA)
    if "l1f2" in phases:
        phase_ffn(G, 1, 2, XA, out, lat_tiles, out_off=CTX)
    P.barrier()
    P.emit()
    return nc


def modcol(i, sl, comp, c):
    return i * 144 + (sl * 3 + comp) * 16 + c


def phase_ada(G, conv=True):
    nc, P, I = G.nc, G.P, G.I
    with ExitStack() as es:
        sb = mk_sb(nc, es)
        cv = Converter(G, es, "early", 3, ("vector", "scalar")) if conv else None
        craw = sb("craw", [128, 2, 16], F32)
        cs = sb("cs", [128, 16, 2], F32)
        bada = sb("badas", [128, 2, 144], F32)
        slabs = [sb(f"adaslab{i}", [128, 16 * 512], F32) for i in range(2)]
        ssem = [G.sems[0], G.sems[1]]
        t0 = P.dma("sync", craw[:], I.cfm, G.sems[2])
        t1 = P.dma("sync", bada[:], I.bada, G.sems[3])
        ta = None
        for w in range(2):
            ta = P.act(_C("activation", out=cs[:, :, w], in_=craw[:, w, :], func=AF.Silu), waits=[t0], sig=True)
        sfree = [None, None]
        banks = [G.PS[0], G.PS[1]]
        n = 0
        for i in range(2):
            for cb in range(36):
                s = n % 2
                lt = P.dma("sync", slabs[s][:], I.wada[i, cb], ssem[s], waits=[sfree[s]])
                for q in range(4):
                    bank = banks[(n * 4 + q) % 2]
                    tm = None
                    for k in range(16):
                        tm = P.pe(_C("matmul",
                            bank.t[:, 0:2], lhsT=slabs[s][:, k * 512 + q * 128: k * 512 + (q + 1) * 128],
                            rhs=cs[:, k, :], start=(k == 0), stop=(k == 15)),
                            waits=[lt, ta, bank.free] if k == 0 else (), sig=(k == 15))
                    col = i * 144 + cb * 4 + q
                    tv = P.dve(_C("tensor_scalar",
                        out=G.MOD[:, col, :], in0=bank.t[:, 0:2], scalar1=bada[:, i, cb * 4 + q:cb * 4 + q + 1], scalar2=None,
                        op0=ALU.add), waits=[tm, t1], sig=True)
                    bank.free = tv
                sfree[s] = tm
                n += 1
                if cv is not None:
                    cv.step(2)
        if cv is not None:
            cv.step(10 ** 6)
        for i in range(2):
            for sl in range(3):
                c0 = modcol(i, sl, 1, 0)
                P.dve(_C("tensor_scalar", out=G.MOD[:, c0:c0 + 16, :], in0=G.MOD[:, c0:c0 + 16, :],
                                                       scalar1=1.0, scalar2=None, op0=ALU.add), waits=[tv])
                if sl != 1:
                    g0 = modcol(i, sl, 2, 0)
                    P.dve(_C("tensor_scalar", out=G.MOD[:, g0:g0 + 16, :], in0=G.MOD[:, g0:g0 + 16, :],
                                                           scalar1=0.5, scalar2=None, op0=ALU.mult))
        P.barrier()
        P.emit()


class Converter:
    def __init__(self, G, es, which, NB, cast_engs):
        self.G = G
        nc, P, I, S = G.nc, G.P, G.I, G.S
        jobs = []

        def add(src, dst, ecols=None):
            sh = src.shape
            lead = sh[:-2]
            E = ecols or sh[-1]
            idxs = [()]
            for n in lead:
                idxs = [ix + (j,) for ix in idxs for j in range(n)]
            for ix in idxs:
                s_ = src
                d_ = dst
                for j in ix:
                    s_ = s_[j]
                    d_ = d_[j]
                for e0 in range(0, E, 4096):
                    n_ = min(4096, E - e0)
                    jobs.append((s_[:, e0:e0 + n_], d_[:, e0:e0 + n_], n_))

        if which == "early":
            add(I.wgu[0], S.wgu[0])
            add(I.wdn[0], S.wdn[0])
            add(I.wqk, S.wqk, ecols=2048)
            add(I.wv, S.wv)
            add(I.wo, S.wo)
        else:
            for fi in range(1, 4):
                add(I.wgu[fi], S.wgu[fi])
                add(I.wdn[fi], S.wdn[fi])
            add(I.win, S.win)
            add(I.wgate, S.wgate)
            add(I.wout, S.wout)
        self.jobs = jobs
        self.j = 0
        self.NB = NB
        self.cast_engs = cast_engs
        sb = mk_sb(nc, es)
        self.fin = [sb(f"cvin{i}", [128, 4096], F32) for i in range(NB)]
        self.fout = [sb(f"cvout{i}", [128, 4096], BF16) for i in range(NB)]
        self.lsem = [G.sems[8 + i] for i in range(NB)]
        self.ssem = [G.sems[8 + NB + i] for i in range(NB)]
        self.infree = [None] * NB
        self.outfree = [None] * NB

    def step(self, n):
        P = self.G.P
        for _ in range(n):
            if self.j >= len(self.jobs):
                return
            j = self.j
            self.j += 1
            src, dst, n_ = self.jobs[j]
            s = j % self.NB
            fin, fout = self.fin[s], self.fout[s]
            lt = P.dma("sync", fin[:, 0:n_], src, self.lsem[s], waits=[self.infree[s]])
            eng = self.cast_engs[j % len(self.cast_engs)]
            if eng == "scalar":
                ct = P.act(_C("activation", out=fout[:, 0:n_], in_=fin[:, 0:n_], func=AF.Copy),
                           waits=[lt, self.outfree[s]], sig=True)
            else:
                ct = P.op(eng, _C("tensor_copy", out=fout[:, 0:n_], in_=fin[:, 0:n_]),
                          waits=[lt, self.outfree[s]], sig=True)
            self.infree[s] = ct
            self.outfree[s] = P.dma("gpsimd", dst, fout[:, 0:n_], self.ssem[s], waits=[ct])


def prologue(G, B, X_in, t0, T, i, sl, w):
    P = G.P
    S4 = T // 128
    tokx = P.dma("gpsimd", B.xin[:, 0:S4, :], X_in[t0:t0 + T, :].rearrange("(s p) d -> p s d", p=128),
                 B.xin_sem, waits=[B.xin_free])
    ev = None
    tp = None
    for c in range(DC):
        bank = B.tpb[c % 2]
        for s in range(S4):
            tp = P.pe(_C("transpose",
                out=bank.t[:, s * 128:(s + 1) * 128], in_=B.xin[:, s, c * 128:(c + 1) * 128], identity=G.identF[:]),
                waits=[tokx, bank.free] if s == 0 else (), sig=(s == S4 - 1))
        sc = modcol(i, sl, 1, c)
        sh = modcol(i, sl, 0, c)
        ev = P.dve(_C("tensor_scalar",
            out=B.uT[:, c, 0:T], in0=bank.t[:, 0:T], scalar1=G.MOD[:, sc, w:w + 1], scalar2=G.MOD[:, sh, w:w + 1],
            op0=ALU.mult, op1=ALU.add), waits=[tp, B.uT_free] if c == 0 else [tp], sig=True)
        bank.free = ev
    B.xin_free = tp
    B.uT_ready = ev


def layer_norm_rows(G, B, xs, s, lnw, after):
    P = G.P
    tk = None
    for q in range(4):
        tk = P.dve(_C("bn_stats", out=B.stats[:, s, q, :], in_=xs[:, q * 512:(q + 1) * 512]),
                   waits=[after] if q == 0 else (), sig=(q == 3))
    t1 = P.dve(_C("bn_aggr", out=B.mv[:, s, :], in_=B.stats[:, s, :, :].rearrange("p a b -> p (a b)")),
               waits=[tk], sig=True)
    t2 = P.dve(_C("tensor_scalar", out=B.sm[:, s, 0:1], in0=B.mv[:, s, 1:2], scalar1=LN_EPS, scalar2=None,
                                         op0=ALU.add), waits=[t1], sig=True)
    t3 = P.act(_C("activation", out=B.sm[:, s, 1:2], in_=B.sm[:, s, 0:1], func=AF.Sqrt), waits=[t2], sig=True)
    t4 = P.dve(_C("reciprocal", out=B.sm[:, s, 2:3], in_=B.sm[:, s, 1:2]), waits=[t3], sig=True)
    t5 = P.dve(_C("scalar_tensor_tensor", out=B.sm[:, s, 3:4], in0=B.mv[:, s, 0:1], scalar=-1.0,
                                                in1=B.sm[:, s, 2:3], op0=ALU.mult, op1=ALU.mult), waits=[t4], sig=True)
    t6 = P.act(_C("activation", out=xs, in_=xs, func=AF.Identity, scale=B.sm[:, s, 2:3], bias=B.sm[:, s, 3:4]),
               waits=[t5, after], sig=True)
    P.pool(_C("tensor_tensor", out=xs, in0=xs, in1=B.gbc[:], op=ALU.mult), waits=[t6, lnw])
    t8 = P.pool(_C("tensor_tensor", out=xs, in0=xs, in1=B.bbc[:], op=ALU.add), sig=True)
    return t8


class OutProj:
    def __init__(self, G, B, KC, X_in, X_out, t0, T, gcol0, w, out_row0, k=0):
        self.k = k
        self.xa = B.x_all[k]
        self.G, self.B, self.KC, self.X_in, self.X_out = G, B, KC, X_in, X_out
        self.t0, self.T, self.gcol0, self.w, self.out_row0 = t0, T, gcol0, w, out_row0
        self.pending = None
        self.last_stt = None

    def begin(self):
        G, B, P = self.G, self.B, self.G.P
        S4 = self.T // 128
        self.xl = P.dma("gpsimd", self.xa[:, 0:S4, :],
                        self.X_in[self.t0:self.t0 + self.T, :].rearrange("(s p) d -> p s d", p=128),
                        B.xall_sem, waits=[B.xall_free[self.k]])

    def slab(self, m, buf, ltok, actT, act_ready):
        G, B, P, T, KC = self.G, self.B, self.G.P, self.T, self.KC
        bank = B.dnb[m % 2]
        tm = None
        for k in range(KC):
            tm = P.pe(_C("matmul", bank.t[:, 0:T], lhsT=buf[:, k * 128:(k + 1) * 128],
                                                         rhs=actT[:, k, 0:T], start=(k == 0), stop=(k == KC - 1)),
                      waits=[ltok, act_ready, bank.free] if k == 0 else (), sig=(k == KC - 1))
        gyb = B.gy[m % 3]
        gc = self.gcol0 + m
        ta = P.act(_C("activation",
            out=gyb[:, 0:T], in_=bank.t[:, 0:T], func=AF.Copy, scale=G.MOD[:, gc, self.w:self.w + 1]),
            waits=[tm, B.gy_free[m % 3]], sig=True)
        bank.free = ta
        if self.pending is not None:
            self.transposes(*self.pending)
        self.pending = (m, gyb, ta)
        return tm

    def transposes(self, m, gyb, ta):
        G, B, P, T = self.G, self.B, self.G.P, self.T
        S4 = T // 128
        bank = B.tpb[m % 2]
        tp = None
        for s in range(S4):
            tp = P.pe(_C("transpose", out=bank.t[:, s * 128:(s + 1) * 128],
                                                            in_=gyb[:, s * 128:(s + 1) * 128], identity=G.identF[:]),
                      waits=[ta, bank.free] if s == 0 else (), sig=(s == S4 - 1))
        B.gy_free[m % 3] = tp
        xv = self.xa[:, 0:S4, m * 128:(m + 1) * 128]
        tv = P.dve(_C("scalar_tensor_tensor",
            out=xv, in0=xv, scalar=ALPHA, in1=bank.t[:, 0:T].rearrange("p (s d) -> p s d", d=128),
            op0=ALU.mult, op1=ALU.add), waits=[tp, self.xl], sig=True)
        bank.free = tv
        self.last_stt = tv

    def end(self):
        G, B, P, T = self.G, self.B, self.G.P, self.T
        S4 = T // 128
        self.transposes(*self.pending)
        self.pending = None
        sts = []
        for s in range(S4):
            xs = self.xa[:, s, :]
            t8 = layer_norm_rows(G, B, xs, s + 4 * self.k, B.ln_ready, self.last_stt)
            r0 = self.out_row0 + s * 128
            sts.append(P.dma("gpsimd", self.X_out[r0:r0 + 128, :], xs, B.st_sem, waits=[t8]))
        B.xall_free[self.k] = sts


def alloc_common(G, es, with_xin=True, nx=1):
    nc = G.nc
    sb = mk_sb(nc, es)
    B = Ctx()
    if with_xin:
        B.xin = sb("xin", [128, 4, D], F32)
        B.uT = sb("uT", [128, DC, 512], BF16)
    B.x_all = [sb(f"x_all{k}", [128, 4, D], F32) for k in range(nx)]
    B.gbc = sb("gbc", [128, D], F32)
    B.bbc = sb("bbc", [128, D], F32)
    B.gy = [sb(f"gy{i}", [128, 512], F32) for i in range(3)]
    B.stats = sb("stats", [128, 8, 4, 6], F32)
    B.mv = sb("mv", [128, 8, 2], F32)
    B.sm = sb("sm", [128, 8, 4], F32)
    B.gy_free = [None] * 3
    B.xin_sem, B.xall_sem, B.st_sem, B.ln_sem = G.sems[0], G.sems[1], G.sems[2], G.sems[3]
    B.xin_free = None
    B.xall_free = [None] * nx
    B.uT_free = None
    B.uT_ready = None
    B.tpb = [G.PS[6], G.PS[7]]
    B.dnb = [G.PS[4], G.PS[5]]
    for b in G.PS:
        b.free = None
    return B


def load_ln(G, B, idx):
    P, I = G.P, G.I
    P.dma("gpsimd", B.gbc[:], I.lng[idx].partition_broadcast(128), B.ln_sem)
    B.ln_ready = P.dma("gpsimd", B.bbc[:], I.lnb[idx].partition_broadcast(128), B.ln_sem)


def phase_ffn(G, i, sl, X_in, X_out, tiles, out_off=0):
    nc, P, S = G.nc, G.P, G.S
    fi = i * 2 + (0 if sl == 0 else 1)
    ws = G.ws
    with ExitStack() as es:
        sb = mk_sb(nc, es)
        B = alloc_common(G, es)
        hT = sb("hT", [128, FC, 512], BF16)
        sg = [sb(f"sg{k}", [128, 512], F32) for k in range(2)]
        sg_free = [None, None]
        gub = [(G.PS[0], G.PS[1]), (G.PS[2], G.PS[3])]
        load_ln(G, B, i * 3 + sl)
        for (t0, T, w) in tiles:
            ws.push([S.wgu[fi, f] for f in range(FC)])
            ws.push([S.wdn[fi, m] for m in range(DC)])
        prev = None
        for ti, (t0, T, w) in enumerate(tiles):
            prologue(G, B, X_in, t0, T, i, sl, w)
            if prev is not None:
                prev.end()
                prev = None
            tlast = None
            for f in range(FC):
                s, buf, lt = ws.next()
                pp = f % 2
                toks = [None, None]
                for gu in range(2):
                    bank = gub[pp][gu]
                    for c in range(DC):
                        toks[gu] = P.pe(_C("matmul",
                            bank.t[:, 0:T], lhsT=buf[:, (gu * 16 + c) * 128:(gu * 16 + c + 1) * 128],
                            rhs=B.uT[:, c, 0:T], start=(c == 0), stop=(c == DC - 1)),
                            waits=[lt, B.uT_ready, bank.free] if c == 0 else (), sig=(c == DC - 1))
                ws.release(s, toks[1])
                gb, ub = gub[pp]
                ta = P.act(_C("activation", out=sg[pp][:, 0:T], in_=gb.t[:, 0:T], func=AF.Silu),
                           waits=[toks[0], sg_free[pp]], sig=True)
                gb.free = ta
                tv = P.dve(_C("tensor_tensor", out=hT[:, f, 0:T], in0=sg[pp][:, 0:T],
                                                                        in1=ub.t[:, 0:T], op=ALU.mult),
                           waits=[ta, toks[1]], sig=True)
                ub.free = tv
                sg_free[pp] = tv
                tlast = toks[1]
            B.uT_free = tlast
            h_ready = tv
            st = OutProj(G, B, FC, X_in, X_out, t0, T, modcol(i, sl, 2, 0), w, t0 - out_off)
            st.begin()
            for m in range(DC):
                s, buf, lt = ws.next()
                tm = st.slab(m, buf, lt, hT, h_ready)
                ws.release(s, tm)
            prev = st
        prev.end()
        P.barrier()
        P.emit()


def phase_attn(G, X_in, X_out):
    nc, P, I, S = G.nc, G.P, G.I, G.S
    NT, NKC, tiles = G.NT, G.NKC, G.tiles
    ws = G.ws
    i, sl = 0, 1
    with ExitStack() as es:
        sb = mk_sb(nc, es)
        B = Ctx()
        B.xin = sb("xin", [128, 4, D], F32)
        B.uT = sb("uT", [128, DC, 512], BF16)
        B.xin_sem = G.sems[0]
        B.xin_free = None
        B.uT_free = None
        B.tpb = [G.PS[6], G.PS[7]]
        for b in G.PS:
            b.free = None
        wv = sb("wv", [128, DC, D], BF16)
        cosb = [sb(f"cos{k}", [128, 512], F32) for k in range(2)]
        sinb = [sb(f"sin{k}", [128, 512], F32) for k in range(2)]
        cs_free = [None, None]
        t1 = [sb(f"rt1_{k}", [128, 512], F32) for k in range(2)]
        t2 = [sb(f"rt2_{k}", [128, 512], F32) for k in range(2)]
        qst = [sb(f"qst{k}", [128, 512], BF16) for k in range(4)]
        qsb = [sb(f"qsb{k}", [128, 512], BF16) for k in range(2)]
        qsb_free = [None, None]
        permF = sb("permF", [128, 128], F32)
        permT = sb("permT", [128, 128], BF16)
        tpl = P.dma("gpsimd", permF[:], I.permm, G.sems[9])
        perm_ready = P.dve(_C("tensor_copy", out=permT[:], in_=permF[:]), waits=[tpl], sig=True)
        vst = [sb(f"vst{k}", [128, D], BF16) for k in range(2)]
        twv = None
        for k in range(8):
            twv = P.dma("gpsimd", wv[:, 2 * k:2 * k + 2, :].rearrange("p a b -> p (a b)"), S.wv[k], G.sems[2])
        qsem = [G.sems[3 + k] for k in range(4)]
        vsem = [G.sems[7 + k] for k in range(2)]
        q_free = [None] * 4
        v_free = [None] * 2
        t_free = [None, None]
        qkb = [(G.PS[0], G.PS[1]), (G.PS[2], G.PS[3])]
        vb = [G.PS[4], G.PS[5]]
        for (t0, T, w) in tiles:
            ws.push([S.wqk[m][:, 0:2048] for m in range(32)])
        nq = 0
        nv = 0
        for ti, (t0, T, w) in enumerate(tiles):
            S4 = T // 128
            cos, sin = cosb[ti % 2], sinb[ti % 2]
            P.dma("gpsimd", cos[:, 0:T], I.cos[:, t0:t0 + T], G.sems[1], waits=[cs_free[ti % 2]])
            tc_ = ts_ = P.dma("gpsimd", sin[:, 0:T], I.sin[:, t0:t0 + T], G.sems[1])
            prologue(G, B, X_in, t0, T, i, sl, w)
            tlast = None
            pend = None

            def finish(m, pp, tq, tcopy):
                nonlocal nq, tlast
                b0, b1 = qkb[pp]
                tpm = P.pe(_C("matmul", b1.t[:, 0:T], lhsT=permT[:], rhs=qsb[pp][:, 0:T], start=True, stop=True),
                           waits=[tcopy, b1.free, perm_ready], sig=True)
                qsb_free[pp] = tpm
                tlast = tpm
                P.dve(_C("tensor_tensor", out=t1[pp][:, 0:T], in0=b0.t[:, 0:T], in1=cos[:, 0:T], op=ALU.mult),
                      waits=[tq, tcopy, tc_, t_free[pp]])
                td = P.dve(_C("tensor_tensor", out=t2[pp][:, 0:T], in0=b1.t[:, 0:T], in1=sin[:, 0:T], op=ALU.mult),
                           waits=[tpm, ts_], sig=True)
                b0.free = td
                b1.free = td
                qs_ = nq % 4
                tp_ = P.pool(_C("tensor_tensor", out=qst[qs_][:, 0:T], in0=t1[pp][:, 0:T], in1=t2[pp][:, 0:T],
                                op=ALU.add), waits=[td, q_free[qs_]], sig=True)
                t_free[pp] = tp_
                dst = (S.QT if m % 2 == 0 else S.KT)[m // 2][:, t0:t0 + T]
                q_free[qs_] = P.dma("gpsimd", dst, qst[qs_][:, 0:T], qsem[qs_], waits=[tp_])
                nq += 1
                cs_free[ti % 2] = td

            for m in range(32):
                s, buf, lt = ws.next()
                pp = m % 2
                b0, b1 = qkb[pp]
                tq = None
                for c in range(DC):
                    tq = P.pe(_C("matmul", b0.t[:, 0:T], lhsT=buf[:, c * 128:(c + 1) * 128],
                                 rhs=B.uT[:, c, 0:T], start=(c == 0), stop=(c == DC - 1)),
                              waits=[lt, B.uT_ready, b0.free] if c == 0 else (), sig=(c == DC - 1))
                ws.release(s, tq)
                tcopy = P.act(_C("activation", out=qsb[pp][:, 0:T], in_=b0.t[:, 0:T], func=AF.Copy),
                              waits=[tq, qsb_free[pp]], sig=True)
                if PIPE_QK:
                    if pend is not None:
                        finish(*pend)
                    pend = (m, pp, tq, tcopy)
                else:
                    finish(m, pp, tq, tcopy)
            if PIPE_QK:
                finish(*pend)
            for s in range(S4):
                vs_ = nv % 2
                tev = None
                for n_ in range(4):
                    bank = vb[n_ % 2]
                    tm = None
                    for c in range(DC):
                        tm = P.pe(_C("matmul",
                            bank.t[:, :], lhsT=B.uT[:, c, s * 128:(s + 1) * 128],
                            rhs=wv[:, c, n_ * 512:(n_ + 1) * 512], start=(c == 0), stop=(c == DC - 1)),
                            waits=[twv, B.uT_ready, bank.free] if c == 0 else (), sig=(c == DC - 1))
                    tlast = tm
                    tev = P.act(_C("activation",
                        out=vst[vs_][:, n_ * 512:(n_ + 1) * 512], in_=bank.t[:, :], func=AF.Copy),
                        waits=[tm, v_free[vs_]] if n_ == 0 else [tm], sig=True)
                    bank.free = tev
                r0 = t0 + s * 128
                v_free[vs_] = P.dma("gpsimd", S.V[r0:r0 + 128, :], vst[vs_][:], vsem[vs_], waits=[tev])
                nv += 1
            B.uT_free = tlast
        P.barrier()
        P.emit()

    import os
    if os.environ.get("STOP_AFTER_QKV"):
        return
    with ExitStack() as es:
        sb = mk_sb(nc, es)
        for b in G.PS:
            b.free = None
        KTs = [sb(f"KTs{k}", [128, NT], BF16) for k in range(2)]
        QTs = [sb(f"QTs{k}", [128, NT], BF16) for k in range(2)]
        Vs = [sb(f"Vs{k}", [128, NKC, 132], BF16) for k in range(2)]
        NPB = 6
        Pb = [sb(f"Pb{k}", [128, 2, 512], BF16) for k in range(NPB)]
        aost = [sb(f"aost{k}", [128, 512], BF16) for k in range(2)]
        lamp = sb("lamp", [128, 4, 64], F32)
        lsm = sb("lsm", [128, 8], F32)
        gsub = sb("gsub", [128, 1], F32)
        mhalf = sb("mhalf", [128, 1], F32)
        ep = sb("ep", [128, 4, 8], F32)
        accs = [sb(f"accs{k}", [128, 3 * 396], F32) for k in range(2)]
        cvl = Converter(G, es, "late", 2, ("vector", "gpsimd")) if G.conv else None
        nkit = 0
        accs_free = [None, None]
        nacc = 0
        tmpo = [sb(f"tmpo{k}", [128, 128], F32) for k in range(2)]
        obuf = [sb(f"obuf{k}", [128, 128], F32) for k in range(2)]
        sqb = [sb(f"sqb{k}", [128, 128], F32) for k in range(2)]
        onb8 = [[sb(f"onb{k}_{q}", [128, 128], BF16) for q in range(4)] for k in range(2)]
        onb_free = [None, None]
        dq = []
        hsem = [G.sems[0], G.sems[1]]
        aosem = [G.sems[2], G.sems[3]]
        tl = P.dma("gpsimd", lamp[:].rearrange("p a b -> p (a b)"),
                   I.lam.rearrange("a b -> (a b)").partition_broadcast(128), G.sems[4])
        tg = P.dma("gpsimd", gsub[:], I.subg, G.sems[5])
        P.dve(_C("memset", mhalf[:], -0.5))
        tk = None
        for j in range(2):
            tk = P.dve(_C("tensor_tensor", out=lamp[:, 2 * j, :], in0=lamp[:, 2 * j, :],
                                                      in1=lamp[:, 2 * j + 1, :], op=ALU.mult), waits=[tl], sig=True)
            tk = P.dve(_C("tensor_reduce", out=lsm[:, j:j + 1], in_=lamp[:, 2 * j, :], axis=AX.X, op=ALU.add),
                       waits=[tk], sig=True)
        tk = P.act(_C("activation", out=lsm[:, 2:4], in_=lsm[:, 0:2], func=AF.Exp), waits=[tk], sig=True)
        tk = P.dve(_C("tensor_tensor", out=lsm[:, 4:5], in0=lsm[:, 3:4], in1=lsm[:, 2:3], op=ALU.subtract),
                   waits=[tk], sig=True)
        tk = P.dve(_C("tensor_scalar", out=lsm[:, 5:6], in0=lsm[:, 4:5], scalar1=-LAMBDA_INIT, scalar2=None,
                                             op0=ALU.add), waits=[tk], sig=True)
        nlam_ready = tk
        tgs = P.dve(_C("tensor_scalar", out=gsub[:], in0=gsub[:], scalar1=1.0 - LAMBDA_INIT, scalar2=None,
                                              op0=ALU.mult), waits=[tg], sig=True)
        for k in range(2):
            P.dve(_C("memset", Vs[k][:, :, 128:132], 1.0))
        P.barrier()
        sbank = [(G.PS[0], G.PS[1]), (G.PS[2], G.PS[3])]
        accb = [G.PS[4], G.PS[5], G.PS[6]]
        tpbank = G.PS[7]
        acc_loc = [(accb[a // 3], (a % 3) * 132) for a in range(8)]
        qblocks = [(0, CTX, 2)] + [(CTX + 512 * j, 512, NKC) for j in range(G.SLAT // 512)]
        h_free = [None, None]
        pb_free = [None] * NPB
        npb = 0
        ao_free = [None, None]
        nao = 0
        acc_free = None
        eo_free = [None, None]
        nep = 0

        def load_head(h):
            k = h % 2
            P.dma("gpsimd", KTs[k][:], S.KT[h], hsem[k], waits=[h_free[k]])
            P.dma("gpsimd", QTs[k][:], S.QT[h], hsem[k])
            return P.dma("gpsimd", Vs[k][:, :, 0:128],
                         S.V[:, h * 128:(h + 1) * 128].rearrange("(c p) v -> p c v", p=128), hsem[k])

        def part2(h_, q0_, QN_, QS_, slot, t_on):
            nonlocal nao
            ao = aost[nao % 2]
            tt = None
            for qs in range(QS_):
                tpv = tpbank.t[:, qs * 64:(qs + 1) * 64].bitcast(BF16)
                tt = P.pe(_C("transpose", out=tpv, in_=onb8[slot][qs][:], identity=G.identB[:]),
                          waits=[t_on[qs], tpbank.free] if qs == 0 else [t_on[qs]], sig=(qs == QS_ - 1))
            tcp = P.dve(_C("tensor_scalar", out=ao[:, 0:QN_], in0=tpbank.t[:, 0:QS_ * 64].bitcast(BF16),
                           scalar1=gsub[:, 0:1], scalar2=None, op0=ALU.mult),
                        waits=[tt, tgs, ao_free[nao % 2]], sig=True)
            onb_free[slot] = tt
            tpbank.free = tcp
            ao_free[nao % 2] = P.dma("gpsimd", S.AOT[h_][:, q0_:q0_ + QN_], ao[:, 0:QN_], aosem[nao % 2], waits=[tcp])
            nao += 1

        hl = load_head(0)
        for h in range(NH):
            k = h % 2
            hl_next = load_head(h + 1) if h + 1 < NH else None
            KT, QT, V = KTs[k], QTs[k], Vs[k]
            last_pe = None
            for (q0, QN, nkc) in qblocks:
                QS = QN // 128
                pend = None

                def pv(kc, pi, te):
                    nonlocal last_pe
                    tm = None
                    started = set()
                    for e_ in range(2):
                        for qs in range(QS):
                            bank, col = acc_loc[e_ * 4 + qs]
                            first = (kc == 0 and id(bank) not in started)
                            started.add(id(bank))
                            tm = P.pe(_C("matmul",
                                bank.t[:, col:col + 129], lhsT=Pb[pi][:, e_, qs * 128:(qs + 1) * 128],
                                rhs=V[:, kc, 0:129], start=first, stop=(kc == nkc - 1), skip_group_check=True),
                                waits=[te, acc_free] if (kc == 0 and e_ == 0 and qs == 0) else [te],
                                sig=(e_ == 1 and qs == QS - 1))
                    pb_free[pi] = tm
                    last_pe = tm

                for kc in range(nkc):
                    pp = kc % 2
                    tsc = None
                    for e_ in range(2):
                        bank = sbank[pp][e_]
                        tsc = P.pe(_C("matmul",
                            bank.t[:, 0:QN], lhsT=KT[e_ * 64:(e_ + 1) * 64, kc * 128:(kc + 1) * 128],
                            rhs=QT[e_ * 64:(e_ + 1) * 64, q0:q0 + QN], start=True, stop=True),
                            waits=[hl, bank.free], sig=(e_ == 1))
                    pi = npb % NPB
                    npb += 1
                    te = None
                    if QN == 512:
                        te = P.act(_C("activation", out=Pb[pi][:, :, :].rearrange("p e q -> p (e q)"),
                                      in_=G.PSpair[pp][:, :], func=AF.Exp, scale=0.125),
                                   waits=[tsc, pb_free[pi]], sig=True)
                        sbank[pp][0].free = te
                        sbank[pp][1].free = te
                    else:
                        for e_ in range(2):
                            bank = sbank[pp][e_]
                            te = P.act(_C("activation",
                                out=Pb[pi][:, e_, 0:QN], in_=bank.t[:, 0:QN], func=AF.Exp, scale=0.125),
                                waits=[tsc, pb_free[pi]] if e_ == 0 else (), sig=True)
                            bank.free = te
                    if pend is not None:
                        pv(*pend)
                    pend = (kc, pi, te)
                    if kc == min(30, nkc - 1) and dq:
                        part2(*dq.pop(0))
                    nkit += 1
                    if cvl is not None and nkit % 16 == 8:
                        cvl.step(1)
                pv(*pend)
                reads = []
                slot = nacc % 2
                asb = accs[slot]
                tev = None
                for bi_ in range(3):
                    tev = P.dve(_C("tensor_copy", out=asb[:, bi_ * 396:(bi_ + 1) * 396], in_=accb[bi_].t[:, 0:396]),
                                waits=[last_pe, accs_free[slot]] if bi_ == 0 else (), sig=(bi_ == 2))
                reads.append(tev)
                last_acc = tev
                t_on = []
                last_to = None
                for qs in range(QS):
                    b0, c0 = asb, qs * 132
                    b1, c1 = asb, (4 + qs) * 132
                    j = nep % 2
                    nep += 1
                    sc = ep[:, qs, :]
                    t_ = P.dve(_C("reciprocal", out=sc[:, 0:1], in_=b0[:, c0 + 128:c0 + 129]),
                               waits=[last_acc], sig=True)
                    t_ = P.dve(_C("reciprocal", out=sc[:, 1:2], in_=b1[:, c1 + 128:c1 + 129]), waits=[t_], sig=True)
                    t_ = P.dve(_C("tensor_tensor", out=sc[:, 2:3], in0=sc[:, 1:2], in1=lsm[:, 5:6], op=ALU.mult),
                               waits=[t_, nlam_ready], sig=True)
                    t_ = P.dve(_C("tensor_scalar", out=tmpo[j][:], in0=b1[:, c1:c1 + 128], scalar1=sc[:, 2:3],
                                  scalar2=None, op0=ALU.mult), waits=[t_], sig=True)
                    to = P.dve(_C("scalar_tensor_tensor", out=obuf[j][:], in0=b0[:, c0:c0 + 128], scalar=sc[:, 0:1],
                                  in1=tmpo[j][:], op0=ALU.mult, op1=ALU.add), waits=[t_], sig=True)
                    last_to = to
                    t_ = P.dve(_C("tensor_tensor", out=sqb[j][:], in0=obuf[j][:], in1=obuf[j][:], op=ALU.mult),
                               waits=[to], sig=True)
                    t_ = P.dve(_C("tensor_reduce", out=sc[:, 3:4], in_=sqb[j][:], axis=AX.X, op=ALU.add),
                               waits=[t_], sig=True)
                    t_ = P.dve(_C("tensor_scalar", out=sc[:, 4:5], in0=sc[:, 3:4], scalar1=1.0 / 128.0,
                                  scalar2=LN_EPS, op0=ALU.mult, op1=ALU.add), waits=[t_], sig=True)
                    t_ = P.pool(_C("tensor_tensor", out=sc[:, 5:6], in0=sc[:, 4:5], in1=mhalf[:], op=ALU.pow),
                                waits=[t_], sig=True)
                    t_ = P.dve(_C("tensor_scalar", out=onb8[slot][qs][:], in0=obuf[j][:], scalar1=sc[:, 5:6],
                                  scalar2=None, op0=ALU.mult), waits=[t_, onb_free[slot]], sig=True)
                    t_on.append(t_)
                acc_free = reads
                accs_free[slot] = last_to
                nacc += 1
                dq.append((h, q0, QN, QS, slot, t_on))
            h_free[k] = last_pe
            hl = hl_next
        while dq:
            part2(*dq.pop(0))
        if cvl is not None:
            cvl.step(10 ** 6)
        P.barrier()
        P.emit()

    with ExitStack() as es:
        sb = mk_sb(nc, es)
        B = alloc_common(G, es, with_xin=False, nx=2)
        aT = [sb(f"aT{k}", [128, NH, 512], BF16) for k in range(2)]
        asem = [G.sems[4], G.sems[5]]
        a_free = [None, None]
        load_ln(G, B, i * 3 + sl)
        for (t0, T, w) in tiles:
            ws.push([S.wo[m] for m in range(16)])
        prev = None
        for ti, (t0, T, w) in enumerate(tiles):
            k = ti % 2
            ta = P.dma("gpsimd", aT[k][:, :, 0:T], S.AOT[:, :, t0:t0 + T].rearrange("h p t -> p h t"), asem[k],
                       waits=[a_free[k]])
            st = OutProj(G, B, NH, X_in, X_out, t0, T, modcol(i, sl, 2, 0), w, t0, k=ti % 2)
            st.begin()
            if prev is not None:
                prev.end()
            tm = None
            for m in range(16):
                s, buf, lt = ws.next()
                tm = st.slab(m, buf, lt, aT[k], ta)
                ws.release(s, tm)
            a_free[k] = tm
            prev = st
        prev.end()
        P.barrier()
        P.emit()


def phase_lru(G, X_in, X_out):
    nc, P, I, S = G.nc, G.P, G.I, G.S
    NT, tiles, lat_tiles = G.NT, G.tiles, G.lat_tiles
    ws = G.ws
    i, sl = 1, 1
    SL = G.SLAT
    with ExitStack() as es:
        sb = mk_sb(nc, es)
        B = Ctx()
        B.xin = sb("xin", [128, 4, D], F32)
        B.uT = sb("uT", [128, DC, 512], BF16)
        B.xin_sem = G.sems[0]
        B.xin_free = None
        B.uT_free = None
        B.tpb = [G.PS[6], G.PS[7]]
        for b in G.PS:
            b.free = None
        NST = 4
        ga = [sb(f"ga{k}", [128, 512], F32) for k in range(2)]
        gb_ = [sb(f"gb{k}", [128, 512], F32) for k in range(2)]
        gst = [sb(f"gst{k}", [128, 512], BF16) for k in range(NST)]
        xst = [sb(f"xst{k}", [128, 512], F32) for k in range(NST)]
        gsem = [G.sems[1 + k] for k in range(NST)]
        xsem = [G.sems[5 + k] for k in range(NST)]
        g_free = [None] * NST
        x_free = [None] * NST
        gab_free = [None, None]
        pb = [G.PS[0], G.PS[1], G.PS[2], G.PS[3]]
        for (t0, T, w) in tiles:
            ws.push([S.win[m] for m in range(40)])
        ng = 0
        nx = 0
        for (t0, T, w) in tiles:
            prologue(G, B, X_in, t0, T, i, sl, w)
            tm = None
            for m in range(40):
                s, buf, lt = ws.next()
                bank = pb[m % 4]
                for c in range(DC):
                    tm = P.pe(_C("matmul",
                        bank.t[:, 0:T], lhsT=buf[:, c * 128:(c + 1) * 128], rhs=B.uT[:, c, 0:T],
                        start=(c == 0), stop=(c == DC - 1)),
                        waits=[lt, B.uT_ready, bank.free] if c == 0 else (), sig=(c == DC - 1))
                ws.release(s, tm)
                if m < RC:
                    j = ng % 2
                    k_ = ng % NST
                    ng += 1
                    t_ = P.act(_C("activation", out=ga[j][:, 0:T], in_=bank.t[:, 0:T], func=AF.Square),
                               waits=[tm, gab_free[j]], sig=True)
                    P.dve(_C("tensor_scalar", out=ga[j][:, 0:T], in0=ga[j][:, 0:T], scalar1=0.044715,
                                                         scalar2=1.0, op0=ALU.mult, op1=ALU.add), waits=[t_])
                    t_ = P.dve(_C("tensor_tensor", out=ga[j][:, 0:T], in0=ga[j][:, 0:T],
                                                                         in1=bank.t[:, 0:T], op=ALU.mult), sig=True)
                    t_ = P.act(_C("activation", out=gb_[j][:, 0:T], in_=ga[j][:, 0:T], func=AF.Sigmoid,
                                                           scale=1.5957691216057308), waits=[t_], sig=True)
                    t_ = P.dve(_C("tensor_tensor",
                        out=gst[k_][:, 0:T], in0=gb_[j][:, 0:T], in1=bank.t[:, 0:T], op=ALU.mult),
                        waits=[t_, g_free[k_]], sig=True)
                    bank.free = t_
                    gab_free[j] = t_
                    g_free[k_] = P.dma("gpsimd", S.GT[m][:, t0:t0 + T], gst[k_][:, 0:T], gsem[k_], waits=[t_])
                else:
                    k_ = nx % NST
                    nx += 1
                    t_ = P.act(_C("activation", out=xst[k_][:, 0:T], in_=bank.t[:, 0:T], func=AF.Copy),
                               waits=[tm, x_free[k_]], sig=True)
                    bank.free = t_
                    x_free[k_] = P.dma("gpsimd", S.XRT[m - RC][:, t0:t0 + T], xst[k_][:, 0:T], xsem[k_], waits=[t_])
            B.uT_free = tm
        P.barrier()
        P.emit()

    with ExitStack() as es:
        sb = mk_sb(nc, es)
        for b in G.PS:
            b.free = None
        HX = sb("HX", [128, NT], F32)
        xr = sb("xr", [128, 2, NT], F32)
        xrb = sb("xrb", [128, 2, NT], BF16)
        R = sb("R", [128, NT], F32)
        IG = sb("IG", [128, NT], F32)
        MM = sb("MM", [128, NT], F32)
        hsum = sb("hsum", [128, SL], F32)
        Gs = sb("Gs", [128, SL], BF16)
        hgo = sb("hgo", [128, SL], BF16)
        convw = sb("convw", [128, RC, 4], F32)
        lvec = sb("lvec", [128, 6, RC], F32)
        apar = sb("apar", [128, 2, RC], F32)
        cvec = sb("cvec", [128, 2, 2, RC], F32)
        tl1 = P.dma("gpsimd", convw[:], I.convw, G.sems[0])
        tl2 = P.dma("gpsimd", lvec[:], I.lvec, G.sems[0])
        tl3 = P.dma("gpsimd", apar[:], I.apar, G.sems[0])
        t_ = P.act(_C("activation", out=cvec[:, :, 0, :], in_=apar[:], func=AF.Exp, scale=-1.0), waits=[tl3], sig=True)
        t_ = P.act(_C("activation", out=cvec[:, :, 0, :], in_=cvec[:, :, 0, :], func=AF.Ln, bias=1.0),
                   waits=[t_], sig=True)
        t_ = P.dve(_C("tensor_scalar", out=cvec[:, :, 0, :], in0=cvec[:, :, 0, :], scalar1=-8.0, scalar2=None,
                                             op0=ALU.mult), waits=[t_], sig=True)
        c_ready = P.dve(_C("tensor_scalar", out=cvec[:, :, 1, :], in0=cvec[:, :, 0, :], scalar1=2.0, scalar2=None,
                                                  op0=ALU.mult), waits=[t_], sig=True)
        xsem, gsem_, osem = G.sems[1], G.sems[2], G.sems[3]
        ws.push([S.wgate[hb] for hb in range(10)])
        tblocks = [(t, min(512, NT - t)) for t in range(0, NT, 512)]
        segs = [(0, CTX), (CTX, NT)]
        gbanks = [G.PS[0], G.PS[1], G.PS[2], G.PS[3]]
        ngb = 0
        hx_free = None
        xr_free = None
        xrb_free = None
        out_free = None
        g_free = None
        for hb in range(10):
            s, buf, lt = ws.next()
            tcv = None
            tcb = None
            for ic in range(2):
                ch = hb * 2 + ic
                tx = P.dma("gpsimd", HX[:], S.XRT[ch], xsem, waits=[hx_free])
                for (a_, b_) in segs:
                    P.dve(_C("tensor_scalar",
                        out=xr[:, ic, a_:b_], in0=HX[:, a_:b_], scalar1=convw[:, ch, 2:3], scalar2=lvec[:, 0, ch:ch + 1],
                        op0=ALU.mult, op1=ALU.add), waits=[tx, tl1, tl2, xr_free])
                    for (j_, d_) in ((1, 1), (0, 2)):
                        P.dve(_C("scalar_tensor_tensor",
                            out=xr[:, ic, a_ + d_:b_], in0=HX[:, a_:b_ - d_], scalar=convw[:, ch, j_:j_ + 1],
                            in1=xr[:, ic, a_ + d_:b_], op0=ALU.mult, op1=ALU.add))
                    tcv = P.dve(_C("scalar_tensor_tensor",
                        out=xr[:, ic, a_:b_ - 1], in0=HX[:, a_ + 1:b_], scalar=convw[:, ch, 3:4],
                        in1=xr[:, ic, a_:b_ - 1], op0=ALU.mult, op1=ALU.add), sig=True)
                hx_free = tcv
                tcb = P.act(_C("activation", out=xrb[:, ic, :], in_=xr[:, ic, :], func=AF.Copy),
                            waits=[tcv, xrb_free], sig=True)
            last_use = None
            for oc in range(2):
                ch = hb * 2 + oc
                tg_ = P.dma("gpsimd", Gs[:], S.GT[ch][:, CTX:NT], gsem_, waits=[g_free])
                for d in range(2):
                    tsr = None
                    tsi = None
                    for (tb0, tbn) in tblocks:
                        for gt in range(2):
                            bank = gbanks[ngb % 4]
                            ngb += 1
                            tm = None
                            for ic in range(2):
                                col = (((d * 2 + gt) * 2 + ic) * 2 + oc) * 128
                                tm = P.pe(_C("matmul",
                                    bank.t[:, 0:tbn], lhsT=buf[:, col:col + 128], rhs=xrb[:, ic, tb0:tb0 + tbn],
                                    start=(ic == 0), stop=(ic == 1)),
                                    waits=[lt, tcb, bank.free] if ic == 0 else (), sig=(ic == 1))
                            dst = R if gt == 0 else IG
                            bcol = (1 + d) if gt == 0 else (3 + d)
                            ta = P.act(_C("activation",
                                out=dst[:, tb0:tb0 + tbn], in_=bank.t[:, 0:tbn], func=AF.Sigmoid,
                                bias=lvec[:, bcol, ch:ch + 1]), waits=[tm, tl2, last_use], sig=True)
                            bank.free = ta
                            if gt == 0:
                                tsr = ta
                            else:
                                tsi = ta
                    last_gate_pe = tm
                    P.act(_C("activation", out=MM[:], in_=R[:], func=AF.Exp,
                                                            scale=cvec[:, d, 1, ch:ch + 1]), waits=[c_ready, last_use])
                    P.act(_C("activation", out=MM[:], in_=MM[:], func=AF.Sqrt, scale=-1.0, bias=1.0))
                    ta_ = P.act(_C("activation", out=R[:], in_=R[:], func=AF.Exp,
                                                                  scale=cvec[:, d, 0, ch:ch + 1]), sig=True)
                    P.dve(_C("tensor_tensor", out=IG[:], in0=IG[:], in1=MM[:], op=ALU.mult), waits=[ta_, tsi])
                    tb_ = P.dve(_C("tensor_tensor", out=IG[:], in0=IG[:], in1=xr[:, oc, :], op=ALU.mult), sig=True)
                    if d == 0:
                        P.dve(_C("tensor_tensor_scan", out=HX[:], data0=R[:], data1=IG[:], initial=0.0,
                                                             op0=ALU.mult, op1=ALU.add), waits=[tb_, hx_free])
                        tsc = P.dve(_C("tensor_copy", out=hsum[:], in_=HX[:, CTX:NT]), waits=[out_free], sig=True)
                    else:
                        t1_ = P.dve(_C("tensor_tensor_scan",
                            out=HX[:, 0:CTX][:, ::-1], data0=R[:, 0:CTX][:, ::-1], data1=IG[:, 0:CTX][:, ::-1],
                            initial=0.0, op0=ALU.mult, op1=ALU.add), waits=[tb_], sig=True)
                        t2_ = P.dve(_C("scalar_tensor_tensor",
                            out=IG[:, NT - 1:NT], in0=R[:, NT - 1:NT], scalar=HX[:, 0:1], in1=IG[:, NT - 1:NT],
                            op0=ALU.mult, op1=ALU.add), waits=[t1_], sig=True)
                        t3_ = P.dve(_C("tensor_tensor_scan",
                            out=HX[:, CTX:NT][:, ::-1], data0=R[:, CTX:NT][:, ::-1], data1=IG[:, CTX:NT][:, ::-1],
                            initial=0.0, op0=ALU.mult, op1=ALU.add), waits=[t2_], sig=True)
                        tsc = P.dve(_C("tensor_tensor", out=hsum[:], in0=hsum[:], in1=HX[:, CTX:NT], op=ALU.add),
                                    waits=[t3_], sig=True)
                    last_use = tsc
                    hx_free = tsc
                to_ = P.pool(_C("tensor_tensor", out=hgo[:], in0=hsum[:], in1=Gs[:], op=ALU.mult),
                             waits=[tsc, tg_, out_free], sig=True)
                g_free = to_
                out_free = P.dma("gpsimd", S.HGT[ch][:, CTX:NT], hgo[:], osem, waits=[to_])
            ws.release(s, last_gate_pe)
            xr_free = last_use
            xrb_free = last_gate_pe
        P.barrier()
        P.emit()

    with ExitStack() as es:
        sb = mk_sb(nc, es)
        B = alloc_common(G, es, with_xin=False, nx=2)
        aT = [sb(f"hT{k}", [128, RC, 512], BF16) for k in range(2)]
        asem = [G.sems[4], G.sems[5]]
        a_free = [None, None]
        load_ln(G, B, i * 3 + sl)
        for (t0, T, w) in lat_tiles:
            ws.push([S.wout[m] for m in range(16)])
        prev = None
        for ti, (t0, T, w) in enumerate(lat_tiles):
            k = ti % 2
            ta = P.dma("gpsimd", aT[k][:, :, 0:T], S.HGT[:, :, t0:t0 + T].rearrange("h p t -> p h t"), asem[k],
                       waits=[a_free[k]])
            st = OutProj(G, B, RC, X_in, X_out, t0, T, modcol(i, sl, 2, 0), w, t0, k=ti % 2)
            st.begin()
            if prev is not None:
                prev.end()
            tm = None
            for m in range(16):
                s, buf, lt = ws.next()
                tm = st.slab(m, buf, lt, aT[k], ta)
                ws.release(s, tm)
            a_free[k] = tm
            prev = st
        prev.end()
        P.barrier()
        P.emit()


def _tile_w(W, KC, MC):
    return np.ascontiguousarray(W.reshape(KC, 128, MC, 128).transpose(2, 1, 0, 3)).reshape(MC, 128, KC * 128)


def _fm(v, n):
    return np.ascontiguousarray(np.asarray(v).reshape(n, 128).T)


def rope_tables(S_LAT):
    GRID_W = 64
    t = np.arange(S_LAT)
    row = (t // GRID_W).astype(np.float32)
    col = (t % GRID_W).astype(np.float32)
    inv = (np.float32(10000.0) ** (-np.arange(0, 32, 2, dtype=np.float32) / np.float32(32))).astype(np.float32)
    cos = np.ones((128, CTX + S_LAT), np.float32)
    sin = np.zeros((128, CTX + S_LAT), np.float32)
    for e in range(2):
        for d in range(64):
            pos = row if d < 32 else col
            dd = d % 32
            j = dd % 16
            ang = (pos * inv[j]).astype(np.float32)
            cos[e * 64 + d, CTX:] = np.cos(ang)
            sgn = -1.0 if dd < 16 else 1.0
            sin[e * 64 + d, CTX:] = sgn * np.sin(ang)
    return cos, sin


def _perm_cols():
    p = np.zeros(128, np.int64)
    for e in range(2):
        for d in range(64):
            dd = d % 32
            partner = d + 16 if dd < 16 else d - 16
            p[e * 64 + d] = e * 64 + partner
    return p


def prep_shared(inp, S_LAT):
    f = lambda a: np.asarray(a, dtype=np.float32)
    sh = {}
    wada = f(inp["w_ada"])
    sh["wada"] = np.ascontiguousarray(wada.reshape(2, 16, 128, 36, 512).transpose(0, 3, 2, 1, 4)).reshape(2, 36, 128, 16 * 512)
    bada = f(inp["b_ada"])
    sh["bada"] = np.ascontiguousarray(bada.reshape(2, 144, 128).transpose(2, 0, 1))
    sh["lng"] = f(inp["ln_g"]).reshape(6, D)
    sh["lnb"] = f(inp["ln_b"]).reshape(6, D)
    wg, wu, wd = f(inp["ffn_w_gate"]), f(inp["ffn_w_up"]), f(inp["ffn_w_down"])
    wgu = np.empty((4, FC, 128, 4096), np.float32)
    wdn = np.empty((4, DC, 128, FC * 128), np.float32)
    for i in range(2):
        for j in range(2):
            g = _tile_w(wg[i, j], DC, FC)
            u = _tile_w(wu[i, j], DC, FC)
            wgu[i * 2 + j, :, :, 0:2048] = g
            wgu[i * 2 + j, :, :, 2048:4096] = u
            wdn[i * 2 + j] = _tile_w(wd[i, j], FC, DC)
    sh["wgu"], sh["wdn"] = wgu, wdn
    wqkv = f(inp["attn_w_qkv"])[0]
    perm = _perm_cols()
    wqk = np.empty((32, 128, 4096), np.float32)
    for h in range(NH):
        for t_, off in ((0, 0), (1, D)):
            Wh = wqkv[:, off + h * 128: off + (h + 1) * 128]
            wqk[h * 2 + t_, :, 0:2048] = _tile_w(Wh, DC, 1)[0]
            wqk[h * 2 + t_, :, 2048:4096] = _tile_w(Wh[:, perm], DC, 1)[0]
    sh["wqk"] = wqk
    Wv = wqkv[:, 2 * D:3 * D]
    sh["wv"] = np.ascontiguousarray(Wv.reshape(DC, 128, D).transpose(1, 0, 2)).reshape(128, 8, 4096).transpose(1, 0, 2).copy()
    sh["wo"] = _tile_w(f(inp["attn_w_o"])[0], DC, DC)
    sh["lam"] = np.stack([f(inp["attn_lambda_q1"])[0], f(inp["attn_lambda_k1"])[0],
                          f(inp["attn_lambda_q2"])[0], f(inp["attn_lambda_k2"])[0]])
    sh["subg"] = f(inp["attn_subln_g"])[0].reshape(128, 1).copy()
    cos, sin = rope_tables(S_LAT)
    sh["cos"], sh["sin"] = cos, sin
    sh["win"] = _tile_w(f(inp["lru_w_in"])[0], DC, 40)
    cw = f(inp["lru_conv_w"])[0]
    sh["convw"] = np.ascontiguousarray(cw.reshape(4, RC, 128).transpose(2, 1, 0))
    lv = np.empty((128, 6, RC), np.float32)
    lv[:, 0] = _fm(f(inp["lru_conv_b"])[0], RC)
    ba, bi = f(inp["lru_b_a"])[0], f(inp["lru_b_i"])[0]
    lv[:, 1] = _fm(ba[0], RC)
    lv[:, 2] = _fm(ba[1], RC)
    lv[:, 3] = _fm(bi[0], RC)
    lv[:, 4] = _fm(bi[1], RC)
    lv[:, 5] = 0.0
    sh["lvec"] = lv
    ap_ = f(inp["lru_a_param"])[0]
    sh["apar"] = np.stack([_fm(ap_[0], RC), _fm(ap_[1], RC)], axis=1).copy()
    wa, wi = f(inp["lru_w_a"])[0], f(inp["lru_w_i"])[0]
    wgate = np.empty((10, 128, 2048), np.float32)
    for hb in range(10):
        for d in range(2):
            for gt, Wt in ((0, wa), (1, wi)):
                for ic in range(2):
                    for oc in range(2):
                        col = (((d * 2 + gt) * 2 + ic) * 2 + oc) * 128
                        wgate[hb, :, col:col + 128] = Wt[d, hb, ic * 128:(ic + 1) * 128, oc * 128:(oc + 1) * 128]
    sh["wgate"] = wgate
    sh["wout"] = _tile_w(f(inp["lru_w_out"])[0], RC, DC)
    sh["eye"] = np.eye(128, dtype=np.float32)
    pm = np.zeros((128, 128), np.float32)
    pm[perm, np.arange(128)] = 1.0
    sh["permm"] = pm
    return sh


def prep_core(inp, b):
    f = lambda a: np.asarray(a, dtype=np.float32)
    d = {}
    d["x0"] = np.concatenate([f(inp["ctx"])[b], f(inp["x"])[b]], axis=0)
    cf = np.empty((128, 2, 16), np.float32)
    cf[:, 0] = _fm(f(inp["c"])[b], 16)
    cf[:, 1] = _fm(f(inp["c_ctx"]), 16)
    d["cfm"] = cf
    return d


_CACHE = {}


def kernel(**inputs):
    x = np.asarray(inputs["x"])
    Bn, S_LAT, _ = x.shape
    key = S_LAT
    if key not in _CACHE:
        _CACHE[key] = build(S_LAT)
    nc = _CACHE[key]
    sh = prep_shared(inputs, S_LAT)
    in_maps = []
    for b in range(Bn):
        m = dict(sh)
        m.update(prep_core(inputs, b))
        in_maps.append(m)
    res = run_bass_kernel_spmd(nc, in_maps, core_ids=list(range(Bn)))
    return np.stack([np.asarray(r["out"]) for r in res.results], axis=0).astype(np.float32)
```
